# Optimizing a Trainium2 kernel written in Bass

```python
import jax, jax.numpy as jnp
from jax import lax
import numpy as np

D_MODEL = 4096
BATCH = 32
SEQ = 256
DEPTH = 2
DEC_BATCH = 8
DEC_SEQ = 1024
PAST_LEN = 512

GRID_W = 64
HEAD_DIM = 128
A_Q_HEADS = D_MODEL // (2 * HEAD_DIM)
A_KV_HEADS = A_Q_HEADS // 4
A_GROUPS = A_Q_HEADS // A_KV_HEADS
A_HALF_WIN = 128
A_BLOCK = 128
B_HEADS = D_MODEL // (2 * HEAD_DIM)
B_WIN_ROWS = 8
B_WIN_COLS = 16
A_WIDTH = A_Q_HEADS * HEAD_DIM
A_KV_WIDTH = A_KV_HEADS * HEAD_DIM
B_WIDTH = B_HEADS * HEAD_DIM
MIX_WIDTH = A_WIDTH + B_WIDTH
ATTN_SPLITS = (A_WIDTH, A_KV_WIDTH, A_KV_WIDTH, B_WIDTH, B_WIDTH, B_WIDTH, MIX_WIDTH)
ATTN_IN_WIDTH = sum(ATTN_SPLITS)
POOL_WINDOWS = (2, 4, 8, 16)
N_POOL_GROUPS = 4
POOL_WIDTH = D_MODEL
POOL_GROUP = POOL_WIDTH // N_POOL_GROUPS
ROPE_BASE = 10000.0
NORM_EPS = 1e-6
NEG_INF = -1e30
N_ATTN_LAYERS = (DEPTH + 1) // 2
N_POOL_LAYERS = DEPTH // 2

kernel_name = 'hybrid_diffusion_prefix_ctx_step'

F32 = jnp.float32


def rmsnorm(x, g):
    xf = x.astype(F32)
    y = xf * lax.rsqrt(jnp.mean(xf * xf, axis=-1, keepdims=True) + NORM_EPS) * g.astype(F32)
    return y.astype(x.dtype)


def split_cols(a, widths):
    idx = [int(i) for i in np.cumsum(widths)[:-1]]
    return jnp.split(a, idx, axis=-1)


def axial_rope(n):
    t = jnp.arange(n)
    quarter = HEAD_DIM // 4
    inv_freq = ROPE_BASE ** (-jnp.arange(quarter, dtype=F32) / quarter)
    ang_r = (t // GRID_W).astype(F32)[:, None] * inv_freq
    ang_c = (t % GRID_W).astype(F32)[:, None] * inv_freq
    ang = jnp.concatenate([ang_r, ang_r, ang_c, ang_c], axis=-1)
    return jnp.cos(ang), jnp.sin(ang)


def apply_rope(x, cos, sin):
    quarter = HEAD_DIM // 4
    xr = x.reshape(x.shape[:-1] + (2, 2, quarter))
    rot = jnp.stack([-xr[..., 1, :], xr[..., 0, :]], axis=-2).reshape(x.shape)
    out = x.astype(F32) * cos[None, :, None, :] + rot.astype(F32) * sin[None, :, None, :]
    return out.astype(x.dtype)


def dense_attention(q, k, v, sink):
    b, n, hq, _ = q.shape
    kv = k.shape[2]
    g = hq // kv
    qg = q.reshape(b, n, kv, g, HEAD_DIM)
    s = jnp.einsum('bqkgd,bjkd->bkgqj', qg, k).astype(F32) * (HEAD_DIM ** -0.5)
    if sink is not None:
        s_sink = jnp.broadcast_to(sink.astype(F32).reshape(kv, g, 1, 1), s.shape[:-1] + (1,))
        s = jnp.concatenate([s, s_sink], axis=-1)
    p = jax.nn.softmax(s, axis=-1)[..., :n].astype(v.dtype)
    o = jnp.einsum('bkgqj,bjkd->bqkgd', p, v)
    return o.reshape(b, n, hq * HEAD_DIM)


def window_attention(q, k, v, ck, cv, sink):
    b, n = q.shape[:2]
    nb = n // A_BLOCK
    ctx_len = ck.shape[1]
    span = 3 * A_BLOCK
    scale = HEAD_DIM ** -0.5

    def band(t):
        tp = jnp.pad(t, ((0, 0), (A_BLOCK, A_BLOCK), (0, 0), (0, 0)))
        tp = tp.reshape(b, nb + 2, A_BLOCK, A_KV_HEADS, HEAD_DIM)
        return jnp.concatenate([tp[:, :nb], tp[:, 1:nb + 1], tp[:, 2:]], axis=2)

    kb, vb = band(k), band(v)
    qb = q.reshape(b, nb, A_BLOCK, A_KV_HEADS, A_GROUPS, HEAD_DIM)
    s_win = jnp.einsum('bnqkgd,bnjkd->bkgnqj', qb, kb).astype(F32) * scale
    qpos = jnp.arange(n).reshape(nb, A_BLOCK)
    kpos = jnp.arange(nb)[:, None] * A_BLOCK - A_BLOCK + jnp.arange(span)[None, :]
    valid = ((jnp.abs(qpos[:, :, None] - kpos[:, None, :]) <= A_HALF_WIN)
             & (kpos[:, None, :] >= 0) & (kpos[:, None, :] < n))
    s_win = jnp.where(valid, s_win, NEG_INF)
    s_ctx = jnp.einsum('bnqkgd,bjkd->bkgnqj', qb, ck).astype(F32) * scale
    s_sink = jnp.broadcast_to(sink.astype(F32).reshape(A_KV_HEADS, A_GROUPS, 1, 1, 1),
                              s_win.shape[:-1] + (1,))
    p = jax.nn.softmax(jnp.concatenate([s_win, s_ctx, s_sink], axis=-1), axis=-1)
    p_win = p[..., :span].astype(v.dtype)
    p_ctx = p[..., span:span + ctx_len].astype(v.dtype)
    o = (jnp.einsum('bkgnqj,bnjkd->bnqkgd', p_win, vb)
         + jnp.einsum('bkgnqj,bjkd->bnqkgd', p_ctx, cv))
    return o.reshape(b, n, A_Q_HEADS * HEAD_DIM)


def neighbourhood_attention(q, k, v, ck, cv, rpb):
    b, n = q.shape[:2]
    rows = n // GRID_W
    wr = min(B_WIN_ROWS, rows)
    kw = wr * GRID_W
    scale = HEAD_DIM ** -0.5
    r = jnp.arange(rows)
    row_start = jnp.clip(r - wr // 2, 0, rows - wr)
    row_idx = row_start[:, None] + jnp.arange(wr)[None, :]

    def gather_rows(t):
        tg = t.reshape(b, rows, GRID_W, B_HEADS, HEAD_DIM)[:, row_idx]
        return tg.reshape(b, rows, kw, B_HEADS, HEAD_DIM)

    kg, vg = gather_rows(k), gather_rows(v)
    qg = q.reshape(b, rows, GRID_W, B_HEADS, HEAD_DIM)
    s_nb = jnp.einsum('brchd,brjhd->bhrcj', qg, kg).astype(F32) * scale
    col = jnp.arange(GRID_W)
    col_start = jnp.clip(col - B_WIN_COLS // 2, 0, GRID_W - B_WIN_COLS)
    key_col = jnp.broadcast_to(col, (wr, GRID_W)).reshape(kw)
    key_row = jnp.repeat(row_idx, GRID_W, axis=1)
    col_ok = ((key_col[None, :] >= col_start[:, None])
              & (key_col[None, :] < col_start[:, None] + B_WIN_COLS))
    dr = jnp.clip(key_row - r[:, None] + B_WIN_ROWS - 1, 0, 2 * B_WIN_ROWS - 2)
    dc = jnp.clip(key_col[None, :] - col[:, None] + B_WIN_COLS - 1, 0, 2 * B_WIN_COLS - 2)
    bias = rpb[:, dr[:, None, :], dc[None, :, :]].astype(F32)
    s_nb = jnp.where(col_ok[None, None, None], s_nb + bias[None], NEG_INF)
    s_ctx = jnp.einsum('brchd,bjhd->bhrcj', qg, ck).astype(F32) * scale
    p = jax.nn.softmax(jnp.concatenate([s_nb, s_ctx], axis=-1), axis=-1)
    p_nb = p[..., :kw].astype(v.dtype)
    p_ctx = p[..., kw:].astype(v.dtype)
    o = (jnp.einsum('bhrcj,brjhd->brchd', p_nb, vg)
         + jnp.einsum('bhrcj,bjhd->brchd', p_ctx, cv))
    return o.reshape(b, n, B_HEADS * HEAD_DIM)


def attn_project(h, w_in):
    b, n = h.shape[:2]
    qa, ka, va, qb, kb, vb, gate = split_cols(h @ w_in, ATTN_SPLITS)
    heads = lambda t, nh: t.reshape(b, n, nh, HEAD_DIM)
    return (heads(qa, A_Q_HEADS), heads(ka, A_KV_HEADS), heads(va, A_KV_HEADS),
            heads(qb, B_HEADS), heads(kb, B_HEADS), heads(vb, B_HEADS), gate)


def attn_layer_context(h, w_in, sink, w_out):
    qa, ka, va, qb, kb, vb, gate = attn_project(h, w_in)
    oa = dense_attention(qa, ka, va, sink)
    ob = dense_attention(qb, kb, vb, None)
    o = jnp.concatenate([oa, ob], axis=-1) * jax.nn.silu(gate)
    return o @ w_out, ka, va, kb, vb


def attn_layer_latent(h, w_in, sink, rpb, w_out, ck_a, cv_a, ck_b, cv_b):
    qa, ka, va, qb, kb, vb, gate = attn_project(h, w_in)
    cos, sin = axial_rope(h.shape[1])
    qa = apply_rope(qa, cos, sin)
    ka = apply_rope(ka, cos, sin)
    oa = window_attention(qa, ka, va, ck_a, cv_a, sink)
    ob = neighbourhood_attention(qb, kb, vb, ck_b, cv_b, rpb)
    o = jnp.concatenate([oa, ob], axis=-1) * jax.nn.silu(gate)
    return o @ w_out


def pool_mixer(h, w_in, w_grp, scale, w_out):
    b, n = h.shape[:2]
    u, gate = jnp.split(h @ w_in, 2, axis=-1)
    ug = u.reshape(b, n, N_POOL_GROUPS, POOL_GROUP)
    cs = jnp.concatenate([jnp.zeros((b, 1, N_POOL_GROUPS, POOL_GROUP), F32),
                          jnp.cumsum(ug.astype(F32), axis=1)], axis=1)
    half = jnp.array(POOL_WINDOWS, dtype=jnp.int32) // 2
    t = jnp.arange(n)[:, None]
    lo = jnp.clip(t - half[None, :], 0, n)
    hi = jnp.clip(t + half[None, :], 0, n)
    gi = jnp.arange(N_POOL_GROUPS)[None, :]
    mean = (cs[:, hi, gi] - cs[:, lo, gi]) / (hi - lo).astype(F32)[None, :, :, None]
    pooled = mean.astype(u.dtype) - ug
    y = jnp.einsum('bngc,gcd->bngd', pooled, w_grp).reshape(b, n, POOL_WIDTH) * scale
    return (y * jax.nn.silu(gate)) @ w_out


def setup_inputs(seed: int = 0) -> dict:
    key = jax.random.key(seed)
    ks = jax.random.split(key, 20)
    nrm = lambda k, shape, s: jax.random.normal(k, shape, F32) * s
    d = D_MODEL
    return {
        'x_prompt': nrm(ks[0], (BATCH, SEQ, d), 1.0),
        'x_sample': nrm(ks[1], (DEC_BATCH, DEC_SEQ, d), 1.0),
        'c': nrm(ks[2], (DEC_BATCH, d), 1.0),
        'cache_a_k': nrm(ks[3], (DEC_BATCH, N_ATTN_LAYERS, PAST_LEN, A_KV_HEADS, HEAD_DIM), 1.0),
        'cache_a_v': nrm(ks[4], (DEC_BATCH, N_ATTN_LAYERS, PAST_LEN, A_KV_HEADS, HEAD_DIM), 1.0),
        'cache_b_k': nrm(ks[5], (DEC_BATCH, N_ATTN_LAYERS, PAST_LEN, B_HEADS, HEAD_DIM), 1.0),
        'cache_b_v': nrm(ks[6], (DEC_BATCH, N_ATTN_LAYERS, PAST_LEN, B_HEADS, HEAD_DIM), 1.0),
        'c_ctx': nrm(ks[7], (d,), 1.0),
        'w_ada': nrm(ks[8], (DEPTH, d, 3 * d), 0.5 * d ** -0.5),
        'b_ada': nrm(ks[9], (DEPTH, 3 * d), 0.02),
        'norm_g': 1.0 + nrm(ks[10], (DEPTH, d), 0.02),
        'w_in_attn': nrm(ks[11], (N_ATTN_LAYERS, d, ATTN_IN_WIDTH), d ** -0.5),
        'a_sink': nrm(ks[12], (N_ATTN_LAYERS, A_Q_HEADS), 0.5),
        'b_rpb': nrm(ks[13], (N_ATTN_LAYERS, B_HEADS, 2 * B_WIN_ROWS - 1, 2 * B_WIN_COLS - 1), 0.1),
        'w_out_attn': nrm(ks[14], (N_ATTN_LAYERS, MIX_WIDTH, d), MIX_WIDTH ** -0.5),
        'w_in_pool': nrm(ks[15], (N_POOL_LAYERS, d, 2 * POOL_WIDTH), d ** -0.5),
        'w_grp_pool': nrm(ks[16], (N_POOL_LAYERS, N_POOL_GROUPS, POOL_GROUP, POOL_GROUP), POOL_GROUP ** -0.5),
        'pool_scale': 1.0 + nrm(ks[17], (N_POOL_LAYERS, POOL_WIDTH), 0.02),
        'w_out_pool': nrm(ks[18], (N_POOL_LAYERS, POOL_WIDTH, d), POOL_WIDTH ** -0.5),
        'final_g': 1.0 + nrm(ks[19], (d,), 0.02),
    }


def reference(x_prompt, x_sample, c, cache_a_k, cache_a_v, cache_b_k, cache_b_v, c_ctx,
              w_ada, b_ada, norm_g, w_in_attn, a_sink, b_rpb, w_out_attn,
              w_in_pool, w_grp_pool, pool_scale, w_out_pool, final_g):
    xp, xs = x_prompt, x_sample
    cond_ctx = jax.nn.silu(c_ctx)
    cond_lat = jax.nn.silu(c)
    new_ak, new_av, new_bk, new_bv = [], [], [], []
    for layer in range(DEPTH):
        m_ctx = cond_ctx @ w_ada[layer] + b_ada[layer]
        m_lat = cond_lat @ w_ada[layer] + b_ada[layer]
        sh_c, sc_c, g_c = jnp.split(m_ctx, 3, axis=-1)
        sh_l, sc_l, g_l = jnp.split(m_lat[:, None, :], 3, axis=-1)
        hp = rmsnorm(xp, norm_g[layer]) * (1.0 + sc_c) + sh_c
        hs = rmsnorm(xs, norm_g[layer]) * (1.0 + sc_l) + sh_l
        i = layer // 2
        if layer % 2 == 0:
            out_p, ka, va, kb, vb = attn_layer_context(hp, w_in_attn[i], a_sink[i], w_out_attn[i])
            new_ak.append(ka)
            new_av.append(va)
            new_bk.append(kb)
            new_bv.append(vb)
            out_s = attn_layer_latent(hs, w_in_attn[i], a_sink[i], b_rpb[i], w_out_attn[i],
                                      cache_a_k[:, i], cache_a_v[:, i],
                                      cache_b_k[:, i], cache_b_v[:, i])
        else:
            out_p = pool_mixer(hp, w_in_pool[i], w_grp_pool[i], pool_scale[i], w_out_pool[i])
            out_s = pool_mixer(hs, w_in_pool[i], w_grp_pool[i], pool_scale[i], w_out_pool[i])
        xp = xp + g_c * out_p
        xs = xs + g_l * out_s
    y_prompt = rmsnorm(xp, final_g)
    y_sample = rmsnorm(xs, final_g)
    new_a_k = jnp.stack(new_ak, axis=1)
    new_a_v = jnp.stack(new_av, axis=1)
    new_b_k = jnp.stack(new_bk, axis=1)
    new_b_v = jnp.stack(new_bv, axis=1)
    return (y_prompt, y_sample, new_a_k, new_a_v, new_b_k, new_b_v)
```

```python
import numpy as np
from contextlib import ExitStack
import concourse.bass as bass
import concourse.mybir as mybir
from concourse.bass_utils import run_bass_kernel_spmd

F32 = mybir.dt.float32
BF16 = mybir.dt.bfloat16
AF = mybir.ActivationFunctionType
ALU = mybir.AluOpType

ENGS = ("pe", "act", "dve", "pool", "sp")
NEG = -30000.0
SCALE = 128 ** -0.5
EPS = 1e-6
ARENA_W = 9728
NSLAB = 3


class Slot:
    __slots__ = ("w", "r")

    def __init__(self):
        self.w = None
        self.r = {}


class Prog:
    def __init__(self, nc, stack):
        self.nc = nc
        self.stack = stack
        self.q = {e: [] for e in ENGS}
        self.cnt = {}
        self.sem = {}
        self.waited = {e: {} for e in ENGS}
        for e in ENGS:
            if e != "sp":
                self.new_sem("c_" + e)

    def new_sem(self, name):
        self.sem[name] = self.stack.enter_context(self.nc.semaphore(name))
        self.cnt[name] = 0
        return name

    def _wait(self, eng, tok):
        if tok is None:
            return
        sname, val = tok
        if self.waited[eng].get(sname, 0) >= val:
            return
        self.waited[eng][sname] = val
        sem = self.sem[sname]
        self.q[eng].append(lambda e, sem=sem, val=val: e.wait_ge(sem, val))

    def _deps(self, eng, reads, writes, extra):
        best = {}
        toks = list(extra)
        for s in reads:
            toks.append(s.w)
        for s in writes:
            toks.append(s.w)
            toks.extend(s.r.items())
        for t in toks:
            if t is not None and best.get(t[0], 0) < t[1]:
                best[t[0]] = t[1]
        for sname, val in best.items():
            self._wait(eng, (sname, val))

    def _commit(self, tok, reads, writes):
        for s in reads:
            if s.r.get(tok[0], 0) < tok[1]:
                s.r[tok[0]] = tok[1]
        for s in writes:
            s.w = tok
            s.r = {}

    def op(self, eng, fn, reads=(), writes=(), extra=()):
        self._deps(eng, reads, writes, extra)
        sname = "c_" + eng
        self.cnt[sname] += 1
        sem = self.sem[sname]
        self.q[eng].append(lambda e, fn=fn, sem=sem: fn(e).then_inc(sem, 1))
        tok = (sname, self.cnt[sname])
        self._commit(tok, reads, writes)
        return tok

    def dma(self, eng, sname, out, in_, reads=(), writes=(), extra=()):
        self._deps(eng, reads, writes, extra)
        self.cnt[sname] += 16
        sem = self.sem[sname]
        self.q[eng].append(lambda e, out=out, in_=in_, sem=sem: e.dma_start(out=out, in_=in_).then_inc(sem, 16))
        tok = (sname, self.cnt[sname])
        self._commit(tok, reads, writes)
        return tok

    def barrier(self):
        toks = [(s, c) for s, c in self.cnt.items() if c > 0]
        for e in ENGS:
            for t in toks:
                self._wait(e, t)

    def run(self):
        with self.nc.Block() as block:
            @block.tensor
            def _(e):
                for f in self.q["pe"]:
                    f(e)

            @block.scalar
            def _(e):
                for f in self.q["act"]:
                    f(e)

            @block.vector
            def _(e):
                for f in self.q["dve"]:
                    f(e)

            @block.gpsimd
            def _(e):
                for f in self.q["pool"]:
                    f(e)

            @block.sync
            def _(e):
                for f in self.q["sp"]:
                    f(e)


class Arena:
    def __init__(self, t, nwords):
        self.t, self.n, self.off = t, nwords, 0

    def reset(self):
        self.off = 0

    def alloc(self, shape, dt):
        nel = int(np.prod(shape[1:]))
        nw = (nel * (2 if dt == BF16 else 4) + 3) // 4
        nw = (nw + 1) // 2 * 2
        assert self.off + nw <= self.n, ("arena overflow", self.off, nw, self.n)
        v = self.t[:, self.off:self.off + nw]
        self.off += nw
        if dt == BF16:
            v = v.bitcast(BF16)
        v = v[:, 0:nel]
        if len(shape) == 3:
            v = v.rearrange("p (a b) -> p a b", a=shape[1])
        elif len(shape) == 4:
            v = v.rearrange("p (a b c) -> p a b c", a=shape[1], b=shape[2])
        return v


def build_nc(stop=None):
    nc = bass.Bass("TRN2", target_bir_lowering=False)

    def din(name, shape):
        return nc.dram_tensor(name, list(shape), F32, kind="ExternalInput").ap()

    def dout(name, shape):
        return nc.dram_tensor(name, list(shape), F32, kind="ExternalOutput").ap()

    xs_d, xp_d = din("xs", (1024, 4096)), din("xp", (1024, 4096))
    cond2_d = din("cond2", (128, 64))
    w_ada0_d = din("w_ada0", (4096, 12288))
    w_ada1_d = din("w_ada1", (96, 128, 4096))
    bT_d, gT_d = din("bT", (128, 192)), din("gT", (128, 64))
    fgT_d, pscT_d = din("fgT", (128, 32)), din("pscT", (128, 32))
    w_in_attn_d = din("w_in_attn", (104, 128, 4096))
    w_out_attn_d = din("w_out_attn", (32, 128, 4096))
    w_in_pool_d = din("w_in_pool", (64, 128, 4096))
    w_grp_d = din("w_grp", (32, 128, 1024))
    w_out_pool_d = din("w_out_pool", (32, 128, 4096))
    sinkb_d = din("sinkb", (128, 16))
    TB_d = din("TB", (16, 128, 896))
    cak_d, cav_d = din("cak", (512, 512)), din("cav", (512, 512))
    cbk_d, cbv_d = din("cbk", (512, 2048)), din("cbv", (512, 2048))
    ident_d, permT_d = din("ident", (128, 128)), din("permT", (128, 128))
    ropec_d, ropes_d = din("ropec", (128, 1024)), din("ropes", (128, 1024))
    amask_d = din("amask", (128, 256))
    rcS_d, rcP_d = din("rcS", (4, 1024)), din("rcP", (4, 1024))
    ys_d, yp_d = dout("ys", (1024, 4096)), dout("yp", (1024, 4096))
    nak_d, nav_d = dout("nak", (1024, 512)), dout("nav", (1024, 512))
    nbk_d, nbv_d = dout("nbk", (1024, 2048)), dout("nbv", (1024, 2048))
    scrA = nc.dram_tensor("scrA", [32, 128, 1024], F32).ap()
    scrB = nc.dram_tensor("scrB", [32, 128, 1024], F32).ap()
    scrC = nc.dram_tensor("scrC", [32, 128, 1024], F32).ap()
    scrD = nc.dram_tensor("scrD", [32, 128, 1024], F32).ap()

    with ExitStack() as st:
        P = Prog(nc, st)

        def sb(name, shape, dt):
            return st.enter_context(nc.sbuf_tensor(name, list(shape), dt))

        hT = sb("hT", (128, 32, 1024), BF16)
        og = sb("og", (128, 32, 1024), BF16)
        slab = [sb(f"slab{i}", (128, 4096), BF16) for i in range(NSLAB)]
        identf = sb("identf", (128, 128), F32)
        identb = sb("identb", (128, 128), BF16)
        onesb = sb("onesb", (128, 128), BF16)
        modt = sb("modt", (128, 2, 96, 2), F32)
        gm = sb("gm", (128, 2, 2, 32), F32)
        gTt = sb("gTt", (128, 2, 32), F32)
        fg = sb("fg", (128, 32), F32)
        psc = sb("psc", (128, 32), F32)
        bTt = sb("bTt", (128, 2, 96), F32)
        es = sb("es", (128, 16), F32)
        epsT = sb("epsT", (128, 1), F32)
        cb = sb("cb", (128, 32, 2), BF16)
        rstdT = [sb("rstdS", (128, 1024), F32), sb("rstdP", (128, 1024), F32)]
        arena_t = sb("arena", (128, ARENA_W), F32)
        AR = Arena(arena_t, ARENA_W)
        ps = [st.enter_context(nc.psum_tensor(f"ps{i}", [128, 512], F32)) for i in range(8)]
        S_ps = [Slot() for _ in range(8)]
        S_slab = [Slot() for _ in range(NSLAB)]
        s_slab = [P.new_sem(f"s_slab{i}") for i in range(NSLAB)]
        S_hT, S_og, S_c = Slot(), Slot(), Slot()
        S_rstdT = [Slot(), Slot()]
        S_scrA = [Slot() for _ in range(32)]
        S_scrB = [Slot() for _ in range(32)]
        S_scrC = [Slot() for _ in range(32)]
        S_scrD = [Slot() for _ in range(32)]
        CUR = {"ti": 0}
        s_c = P.new_sem("s_c")
        s_a = [P.new_sem(f"s_a{i}") for i in range(8)]
        s_b = [P.new_sem(f"s_b{i}") for i in range(6)]
        s_p = [P.new_sem(f"s_p{i}") for i in range(2)]
        G = {"slab_rr": 0, "gemm_rr": 0}

        og_f = og[:].rearrange("p a b -> p (a b)").bitcast(F32)
        og_b = og[:].rearrange("p a b -> p (a b)")

        def cp(eng, out, in_, reads, writes, scale=None):
            if eng == "act":
                if scale is None:
                    return P.op("act", lambda e: e.activation(out=out, in_=in_, func=AF.Copy), reads, writes)
                return P.op("act", lambda e: e.activation(out=out, in_=in_, func=AF.Copy, scale=scale), reads, writes)
            return P.op("dve", lambda e: e.tensor_copy(out=out, in_=in_), reads, writes)

        P.dma("sp", s_c, identf[:], ident_d[:, :], writes=[S_c])
        P.dma("sp", s_c, gTt[:].rearrange("p a b -> p (a b)"), gT_d[:, :], writes=[S_c])
        P.dma("sp", s_c, fg[:], fgT_d[:, :], writes=[S_c])
        P.dma("sp", s_c, psc[:], pscT_d[:, :], writes=[S_c])
        P.dma("sp", s_c, bTt[:].rearrange("p a b -> p (a b)"), bT_d[:, :], writes=[S_c])
        P.dma("sp", s_c, es[:], sinkb_d[:, :], writes=[S_c])
        P.op("dve", lambda e: e.memset(onesb[:], 1.0), writes=[S_c])
        P.op("dve", lambda e: e.memset(epsT[:], EPS), writes=[S_c])
        P.op("dve", lambda e: e.tensor_copy(out=identb[:], in_=identf[:]), reads=[S_c], writes=[S_c])
        P.op("act", lambda e: e.activation(out=es[:], in_=es[:], func=AF.Exp), reads=[S_c], writes=[S_c])
        P.barrier()

        def phase_ada_input():
            AR.reset()
            cf = AR.alloc((128, 64), F32)
            S_cf, S_cb, S_mrow = Slot(), Slot(), Slot()
            P.dma("sp", s_a[3], cf, cond2_d[:, :], writes=[S_cf])
            P.op("act", lambda e: e.activation(out=cb.rearrange("p a b -> p (a b)"), in_=cf, func=AF.Silu),
                 reads=[S_cf], writes=[S_cb, S_c])
            hT_f = hT[:].rearrange("p a b -> p (a b)").bitcast(F32)
            mrow = hT_f[0:2, 0:12288]
            xblk = [og_f[:, 0:4096], og_f[:, 4096:8192]]
            xst = og_f[:, 8192:12288].rearrange("p (a b) -> p a b", a=32)
            sq = og_b[:, 24576:28672].rearrange("p (a b) -> p a b", a=32)
            S_xb = [Slot(), Slot()]
            S_xg = [Slot() for _ in range(8)]
            S_sq = Slot()

            def input_block(ib):
                ti, b = divmod(ib, 8)
                xsrc = xs_d if ti == 0 else xp_d
                scr, S_scr = (scrA, S_scrA) if ti == 0 else (scrC, S_scrC)
                scr_v = scr.rearrange("j p t -> p j t")
                k = ib % 2
                P.dma("sp", s_a[k], xblk[k], xsrc[128 * b:128 * b + 128, :], writes=[S_xb[k]])
                for g in range(8):
                    bank = 4 + g % 2

                    def tr(e, k=k, g=g, bank=bank):
                        for j in range(4):
                            c = 4 * g + j
                            ins = e.transpose(out=ps[bank][:, 128 * j:128 * j + 128],
                                              in_=xblk[k][:, 128 * c:128 * c + 128], identity=identf[:])
                        return ins
                    P.op("pe", tr, reads=[S_xb[k]], writes=[S_ps[bank]])
                    cp("act" if g % 2 == 0 else "dve", xst[:, 4 * g:4 * g + 4, :],
                       ps[bank][:].rearrange("p (a b) -> p a b", a=4), [S_ps[bank]], [S_xg[g]])
                P.op("act", lambda e: e.activation(out=sq, in_=xst, func=AF.Square), reads=S_xg, writes=[S_sq])
                col = 128 * (b % 4)

                def ssq(e, col=col):
                    for c in range(32):
                        ins = e.matmul(ps[6][:, col:col + 128], lhsT=onesb[:], rhs=sq[:, c, :],
                                       start=(c == 0), stop=(c == 31))
                    return ins
                P.op("pe", ssq, reads=[S_sq], writes=[S_ps[6]])
                rstd_from(ps[6][:, col:col + 128], rstdT[ti][:, 128 * b:128 * b + 128], [S_ps[6]], ti)
                P.dma("sp", s_a[2], scr_v[:, :, 128 * b:128 * b + 128], xst, reads=S_xg, writes=S_scr)

            def mod_finish(l):
                def tr(e):
                    for j in range(96):
                        ins = e.transpose(out=ps[7][:, 2 * j:2 * j + 2], in_=mrow[:, 128 * j:128 * j + 128],
                                          identity=identf[0:2, 0:2])
                    return ins
                P.op("pe", tr, reads=[S_mrow], writes=[S_ps[7]])
                pv = ps[7][:, 0:192].rearrange("p (j v) -> p j v", v=2)
                for v in range(2):
                    P.op("dve", lambda e, v=v: e.tensor_tensor(
                        out=modt[:, l, :, v], in0=pv[:, :, v], in1=bTt[:, l, :], op=ALU.add),
                        reads=[S_ps[7]], writes=[S_c])
                gm_finish(l)

            q = 0
            nib = 0
            for l in range(1):
                for n in range(24):
                    pb = n % 2
                    for kq in range(4):
                        s = G["slab_rr"] % NSLAB
                        G["slab_rr"] += 1
                        sv = slab[s][:].rearrange("p (c n) -> p c n", n=512)
                        wv = w_ada0_d[1024 * kq:1024 * kq + 1024, 512 * n:512 * n + 512].rearrange(
                            "(c p) n -> p c n", p=128)
                        P.dma("pool", s_slab[s], sv, wv, writes=[S_slab[s]])

                        def mm(e, sv=sv, kq=kq, pb=pb):
                            for c in range(8):
                                ins = e.matmul(ps[pb][0:2, :], lhsT=cb[:, 8 * kq + c, :], rhs=sv[:, c, :],
                                               start=(kq == 0 and c == 0), stop=(kq == 3 and c == 7))
                            return ins
                        P.op("pe", mm, reads=[S_slab[s], S_cb], writes=[S_ps[pb]])
                    cp("act", mrow[:, 512 * n:512 * n + 512], ps[pb][0:2, :], [S_ps[pb]], [S_mrow])
                    q += 1
                    while nib < 16 and nib < (q * 2 + 2) // 3:
                        input_block(nib)
                        nib += 1
                mod_finish(l)
            P.barrier()

        def rstd_from(ps_ap, out_ap, rd, ti):
            P.op("act", lambda e: e.activation(out=out_ap, in_=ps_ap, func=AF.Sqrt, bias=epsT[:, 0:1],
                                               scale=1.0 / 4096.0), reads=rd, writes=[S_rstdT[ti]])
            P.op("dve", lambda e: e.reciprocal(out=out_ap, in_=out_ap), reads=[S_rstdT[ti]], writes=[S_rstdT[ti]])

        def gm_finish(l):
            for v in range(2):
                P.op("dve", lambda e, v=v: e.tensor_scalar(
                    out=gm[:, l, v, :], in0=modt[:, l, 32:64, v], scalar1=1.0, scalar2=None, op0=ALU.add),
                    reads=[S_c], writes=[S_c])
                P.op("dve", lambda e, v=v: e.tensor_tensor(
                    out=gm[:, l, v, :], in0=gm[:, l, v, :], in1=gTt[:, l, :], op=ALU.mult),
                    reads=[S_c], writes=[S_c])

        def gemm_stream(jobs):
            LOOK = NSLAB - 1
            nload = 0
            for i, job in enumerate(jobs):
                while nload < len(jobs) and nload <= i + LOOK - 1:
                    jb = jobs[nload]
                    s = G["slab_rr"] % NSLAB
                    G["slab_rr"] += 1
                    jb["s"] = s
                    nk = jb["nk"]
                    sv = slab[s][:, 0:nk * 128].rearrange("p (a b) -> p a b", b=1024)
                    wv = jb["w"].rearrange("p (a b) -> p a b", b=1024)
                    P.dma("pool", s_slab[s], sv, wv, writes=[S_slab[s]])
                    nload += 1
                s, nk, act = job["s"], job["nk"], job["act"]
                pair = 2 * (G["gemm_rr"] % 2)
                G["gemm_rr"] += 1
                sv = slab[s][:, 0:nk * 128].rearrange("p (c n) -> p c n", n=128)

                def mm(e, sv=sv, nk=nk, act=act, pair=pair):
                    for c in range(nk):
                        for hf in range(2):
                            ins = e.matmul(ps[pair + hf][:, :], lhsT=sv[:, c, :],
                                           rhs=act[:, c, 512 * hf:512 * hf + 512],
                                           start=(c == 0), stop=(c == nk - 1))
                    return ins
                if job.get("gemv"):
                    def mm(e, sv=sv, pair=pair):
                        for c in range(32):
                            ins = e.matmul(ps[pair][:, 0:2], lhsT=sv[:, c, :], rhs=cb[:, c, :],
                                           start=(c == 0), stop=(c == 31))
                        return ins
                P.op("pe", mm, reads=[S_slab[s]] + job["rd"], writes=[S_ps[pair], S_ps[pair + 1]])
                job["consume"](pair)

        def gen_norm(scr, S_scr, l, v, ti, alloc, sems, nbuf):
            xin = [alloc((128, 1024), F32) for _ in range(nbuf)]
            tmp = [alloc((128, 1024), F32) for _ in range(nbuf)]
            S_xin, S_tmp = [Slot() for _ in range(nbuf)], [Slot() for _ in range(nbuf)]
            for j in range(32):
                k = j % nbuf
                P.dma("sp", sems[k], xin[k], scr[j], reads=[S_scr[j]], writes=[S_xin[k]])
                P.op("dve", lambda e, k=k: e.tensor_tensor(out=tmp[k], in0=xin[k], in1=rstdT[ti][:], op=ALU.mult),
                     reads=[S_xin[k], S_rstdT[ti]], writes=[S_tmp[k]])
                P.op("act", lambda e, k=k, j=j: e.activation(out=hT[:, j, :], in_=tmp[k], func=AF.Identity,
                                                              scale=gm[:, l, v, j:j + 1],
                                                              bias=modt[:, l, j, v:v + 1]),
                     reads=[S_tmp[k]], writes=[S_hT])
                yield

        def phase_norm(scr, S_scr, l, v, ti):
            AR.reset()
            for _ in gen_norm(scr, S_scr, l, v, ti, AR.alloc, s_a[0:4], 4):
                pass
            P.barrier()

        def phase_outproj(W, scr_in, S_in, scr_out, S_out, l, v, ti, side=None):
            AR.reset()
            xin = [AR.alloc((128, 1024), F32) for _ in range(2)]
            xo = [AR.alloc((128, 1024), F32) for _ in range(2)]
            sq = [AR.alloc((128, 1024), BF16) for _ in range(2)]
            S_xin, S_xo, S_sq = [Slot(), Slot()], [Slot(), Slot()], [Slot(), Slot()]
            side = side() if side is not None else iter(())
            jobs = []
            for j in range(32):
                def consume(pair, j=j):
                    k = j % 2
                    P.dma("sp", s_a[k], xin[k], scr_in[j], reads=[S_in[j]], writes=[S_xin[k]])
                    for hf in range(2):
                        P.op("dve", lambda e, hf=hf, k=k: e.scalar_tensor_tensor(
                            out=xo[k][:, 512 * hf:512 * hf + 512], in0=ps[pair + hf][:, :],
                            scalar=modt[:, l, 64 + j, v:v + 1], in1=xin[k][:, 512 * hf:512 * hf + 512],
                            op0=ALU.mult, op1=ALU.add),
                            reads=[S_ps[pair + hf], S_xin[k]], writes=[S_xo[k]])
                    P.dma("sp", s_a[2 + k], scr_out[j], xo[k], reads=[S_xo[k]], writes=[S_out[j]])
                    P.op("act", lambda e, k=k: e.activation(out=sq[k], in_=xo[k], func=AF.Square),
                         reads=[S_xo[k]], writes=[S_sq[k]])

                    def ssq(e, k=k):
                        for hf in range(2):
                            ins = e.matmul(ps[6 + hf][:, :], lhsT=onesb[:], rhs=sq[k][:, 512 * hf:512 * hf + 512],
                                           start=(j == 0), stop=(j == 31))
                        return ins
                    P.op("pe", ssq, reads=[S_sq[k]], writes=[S_ps[6], S_ps[7]])
                    next(side, None)
                jobs.append(dict(w=W[j], nk=32, act=og[:], rd=[S_og], consume=consume))
            gemm_stream(jobs)
            for _ in side:
                pass
            for hf in range(2):
                rstd_from(ps[6 + hf][:, :], rstdT[ti][:, 512 * hf:512 * hf + 512], [S_ps[6 + hf]], ti)
            P.barrier()

        hT_f_all = hT[:].rearrange("p a b -> p (a b)").bitcast(F32)
        HAR = Arena(hT_f_all, 16384)

        def gen_final(scr, S_scr, yout, ti, sems):
            HAR.reset()
            xin = [HAR.alloc((128, 1024), F32) for _ in range(4)]
            tmp = [HAR.alloc((128, 1024), F32) for _ in range(4)]
            yst = [HAR.alloc((128, 8, 512), F32) for _ in range(2)]
            S_xin, S_tmp, S_y = [Slot() for _ in range(4)], [Slot() for _ in range(4)], [Slot(), Slot()]
            yv = yout.rearrange("(b p) d -> p b d", p=128)
            for j in range(32):
                k = j % 4
                q4, jj4 = divmod(j, 4)
                ky = q4 % 2
                P.dma("sp", sems[k], xin[k], scr[j], reads=[S_scr[j]], writes=[S_xin[k]])
                P.op("dve", lambda e, k=k: e.tensor_tensor(out=xin[k], in0=xin[k], in1=rstdT[ti][:], op=ALU.mult),
                     reads=[S_xin[k], S_rstdT[ti]], writes=[S_xin[k]])
                P.op("act", lambda e, k=k, j=j: e.activation(out=tmp[k], in_=xin[k], func=AF.Identity,
                                                              scale=fg[:, j:j + 1]),
                     reads=[S_xin[k]], writes=[S_tmp[k]])
                for g in range(2):
                    bank = 4 + g

                    def tr(e, k=k, g=g, bank=bank):
                        for jj in range(4):
                            b = 4 * g + jj
                            ins = e.transpose(out=ps[bank][:, 128 * jj:128 * jj + 128],
                                              in_=tmp[k][:, 128 * b:128 * b + 128], identity=identf[:])
                        return ins
                    P.op("pe", tr, reads=[S_tmp[k]], writes=[S_ps[bank]])
                    cp("act" if g == 0 else "dve", yst[ky][:, 4 * g:4 * g + 4, 128 * jj4:128 * jj4 + 128],
                       ps[bank][:].rearrange("p (a b) -> p a b", a=4), [S_ps[bank]], [S_y[ky]])
                if jj4 == 3:
                    P.dma("sp", sems[4 + ky], yv[:, :, 512 * q4:512 * q4 + 512], yst[ky], reads=[S_y[ky]])
                yield

        def phase_final(scr, S_scr, yout, ti):
            for _ in gen_final(scr, S_scr, yout, ti, s_b):
                pass
            P.barrier()

        def phase_pool(is_sample):
            AR.reset()
            nseq, L = (1, 1024) if is_sample else (4, 256)
            LP = L + 16
            pl = AR.alloc((128, 8, 1024), BF16)
            up = AR.alloc((128, nseq, LP), F32)
            ba = AR.alloc((128, nseq, LP), F32)
            bb = AR.alloc((128, nseq, LP), F32)
            rc = AR.alloc((128, 1024), F32)
            sgt = [AR.alloc((128, 1024), BF16) for _ in range(2)]
            S_pl, S_up, S_ba, S_bb, S_rc = Slot(), Slot(), Slot(), Slot(), Slot()
            S_sgt = [Slot(), Slot()]
            rc_d = rcS_d if is_sample else rcP_d
            for t in (up, ba, bb):
                P.op("dve", lambda e, t=t: e.memset(t, 0.0), writes=[S_up, S_ba, S_bb])
            rc3 = rc.rearrange("p (s l) -> p s l", s=nseq)
            TT = lambda o, a, b, op: (lambda e: e.tensor_tensor(out=o, in0=a, in1=b, op=op))
            jobs = []
            for gi in range(4):
                for c in range(8):
                    j = 8 * gi + c

                    def consume_u(pair, gi=gi, c=c):
                        if c == 0:
                            P.dma("sp", s_a[4], rc, rc_d[gi, :].partition_broadcast(128), writes=[S_rc])
                        for hf in range(2):
                            if is_sample:
                                o = up[:, 0, 8 + 512 * hf:8 + 512 * hf + 512]
                                i_ = ps[pair + hf][:, :]
                            else:
                                o = up[:, 2 * hf:2 * hf + 2, 8:8 + L]
                                i_ = ps[pair + hf][:].rearrange("p (s l) -> p s l", s=2)
                            cp("act", o, i_, [S_ps[pair + hf]], [S_up])
                        P.op("dve", TT(ba[:, :, 1:LP], up[:, :, 1:LP], up[:, :, 0:LP - 1], ALU.add),
                             reads=[S_up], writes=[S_ba])
                        fin, S_fin, oth, S_oth = ba, S_ba, bb, S_bb
                        if gi >= 1:
                            P.op("dve", TT(bb[:, :, 2:LP - 1], ba[:, :, 1:LP - 2], ba[:, :, 3:LP], ALU.add),
                                 reads=[S_ba], writes=[S_bb])
                            fin, S_fin, oth, S_oth = bb, S_bb, ba, S_ba
                        if gi >= 2:
                            P.op("dve", TT(ba[:, :, 4:LP - 3], bb[:, :, 2:LP - 5], bb[:, :, 6:LP - 1], ALU.add),
                                 reads=[S_bb], writes=[S_ba])
                            fin, S_fin, oth, S_oth = ba, S_ba, bb, S_bb
                        if gi >= 3:
                            P.op("dve", TT(bb[:, :, 8:LP - 8], ba[:, :, 4:LP - 12], ba[:, :, 12:LP - 4], ALU.add),
                                 reads=[S_ba], writes=[S_bb])
                            fin, S_fin, oth, S_oth = bb, S_bb, ba, S_ba
                        P.op("dve", TT(oth[:, :, 8:8 + L], fin[:, :, 8:8 + L], rc3, ALU.mult),
                             reads=[S_fin, S_rc], writes=[S_oth])
                        P.op("dve", TT(pl[:, c, :].rearrange("p (s l) -> p s l", s=nseq), oth[:, :, 8:8 + L],
                                       up[:, :, 8:8 + L], ALU.subtract),
                             reads=[S_oth, S_up], writes=[S_pl])
                    jobs.append(dict(w=w_in_pool_d[j], nk=32, act=hT[:], rd=[S_hT], consume=consume_u))
                for c in range(8):
                    j = 8 * gi + c

                    def consume_g(pair, j=j):
                        k = j % 2
                        for hf in range(2):
                            P.op("act", lambda e, hf=hf, k=k: e.activation(
                                out=sgt[k][:, 512 * hf:512 * hf + 512], in_=ps[pair + hf][:, :], func=AF.Silu),
                                reads=[S_ps[pair + hf]], writes=[S_sgt[k]])

                    def consume_y(pair, j=j):
                        k = j % 2
                        for hf in range(2):
                            P.op("dve", lambda e, hf=hf, k=k: e.scalar_tensor_tensor(
                                out=og[:, j, 512 * hf:512 * hf + 512], in0=ps[pair + hf][:, :],
                                scalar=psc[:, j:j + 1], in1=sgt[k][:, 512 * hf:512 * hf + 512],
                                op0=ALU.mult, op1=ALU.mult),
                                reads=[S_ps[pair + hf], S_sgt[k]], writes=[S_og])
                    jobs.append(dict(w=w_in_pool_d[32 + j], nk=32, act=hT[:], rd=[S_hT], consume=consume_g))
                    jobs.append(dict(w=w_grp_d[j], nk=8, act=pl, rd=[S_pl], consume=consume_y))
            gemm_stream(jobs)
            P.barrier()

        def phase_attn(is_sample):
            AR.reset()
            qT = AR.alloc((128, 1024), BF16)
            kT = AR.alloc((128, 1024), BF16)
            vTf = AR.alloc((128, 1024), F32)
            vE = AR.alloc((128, 8, 128), BF16)
            sg = AR.alloc((128, 1024), BF16)
            PT = [AR.alloc((128, 512), BF16) for _ in range(4)]
            n1 = AR.alloc((128, 512), F32)
            n2 = AR.alloc((128, 512), F32)
            S_qT, S_kT, S_vTf, S_vE, S_sg, S_n, S_n2 = Slot(), Slot(), Slot(), Slot(), Slot(), Slot(), Slot()
            S_PT = [Slot() for _ in range(4)]
            if is_sample:
                vO = AR.alloc((128, 7, 128), BF16)
                cKf = AR.alloc((128, 4, 128), F32)
                cKT = AR.alloc((128, 512), BF16)
                cV = AR.alloc((128, 4, 128), BF16)
                bt = AR.alloc((128, 14, 64), BF16)
                amk = AR.alloc((128, 256), BF16)
                S_vO, S_cKf, S_cKT, S_cV, S_bt = Slot(), Slot(), Slot(), Slot(), Slot()
                cosT = og_f[:, 12288:13312]
                sinT = og_f[:, 13312:14336]
                qf = og_f[:, 14336:14848]
                rt = og_f[:, 14848:15360]
                permf = og_f[:, 15360:15488]
                amf = og_f[:, 15488:15744]
                S_qf, S_rt = Slot(), Slot()
                P.dma("sp", s_c, cosT, ropec_d[:, :], writes=[S_c])
                P.dma("sp", s_c, sinT, ropes_d[:, :], writes=[S_c])
                P.dma("sp", s_c, permf, permT_d[:, :], writes=[S_c])
                P.dma("sp", s_c, amf, amask_d[:, :], writes=[S_c])
                P.op("dve", lambda e: e.tensor_copy(out=amk, in_=amf), reads=[S_c], writes=[S_c])
                P.barrier()
            else:
                kf = AR.alloc((128, 1024), F32)
                kst = AR.alloc((128, 8, 128), F32)
                vst = AR.alloc((128, 8, 128), F32)
                S_kf, S_kst, S_vst = Slot(), Slot(), Slot()

            def evac2(pair, dst, S_dst, scale=None, eng="act", excl=False):
                for hf in range(2):
                    cp(eng, dst[:, 512 * hf:512 * hf + 512], ps[pair + hf][:, :], [S_ps[pair + hf]],
                       [S_dst] + ([S_ps[pair + hf]] if excl else []), scale)

            def rope_evac(pair, dst, S_dst, scale):
                for hf in range(2):
                    cp("act", qf, ps[pair + hf][:, :], [S_ps[pair + hf]], [S_qf], scale)
                    P.op("pe", lambda e: e.matmul(ps[4][:, :], lhsT=permf, rhs=qf, start=True, stop=True),
                         reads=[S_qf], writes=[S_ps[4]])
                    P.op("dve", lambda e, hf=hf: e.tensor_tensor(out=rt, in0=qf, in1=cosT[:, 512 * hf:512 * hf + 512],
                                                                  op=ALU.mult), reads=[S_qf], writes=[S_rt])
                    P.op("dve", lambda e, hf=hf: e.tensor_tensor(out=qf, in0=ps[4][:, :],
                                                                  in1=sinT[:, 512 * hf:512 * hf + 512], op=ALU.mult),
                         reads=[S_ps[4]], writes=[S_qf])
                    P.op("dve", lambda e, hf=hf: e.tensor_tensor(out=dst[:, 512 * hf:512 * hf + 512], in0=rt, in1=qf,
                                                                  op=ALU.add), reads=[S_rt, S_qf], writes=[S_dst])

            def transposes(srcs, bank, rd):
                def tr(e):
                    for i, s_ in enumerate(srcs):
                        ins = e.transpose(out=ps[bank][:, 128 * i:128 * i + 128], in_=s_, identity=identf[:])
                    return ins
                P.op("pe", tr, reads=rd, writes=[S_ps[bank]])

            def v_transposes(out_d=None, col0=0):
                for g in range(2):
                    bank = 5 - g
                    transposes([vTf[:, 128 * (4 * g + i):128 * (4 * g + i) + 128] for i in range(4)], bank, [S_vTf])
                    pv = ps[bank][:].rearrange("p (a b) -> p a b", a=4)
                    if out_d is None:
                        cp("dve", vE[:, 4 * g:4 * g + 4, :], pv, [S_ps[bank]], [S_vE])
                    else:
                        cp("dve", vE[:, 4 * g:4 * g + 4, :], pv, [S_ps[bank]], [S_vE, S_ps[bank]])
                        cp("act", vst[:, 4 * g:4 * g + 4, :], pv, [S_ps[bank]], [S_vst, S_ps[bank]])
                if out_d is not None:
                    P.dma("sp", s_a[5], out_d.rearrange("(b p) d -> p b d", p=128)[:, :, col0:col0 + 128], vst,
                          reads=[S_vst])

            def k_out(out_d, col0):
                for g in range(2):
                    bank = 5 - g
                    transposes([kf[:, 128 * (4 * g + i):128 * (4 * g + i) + 128] for i in range(4)], bank, [S_kf])
                    cp("act", kst[:, 4 * g:4 * g + 4, :], ps[bank][:].rearrange("p (a b) -> p a b", a=4),
                       [S_ps[bank]], [S_kst])
                P.dma("sp", s_a[6], out_d.rearrange("(b p) d -> p b d", p=128)[:, :, col0:col0 + 128], kst,
                      reads=[S_kst])

            def vO_transposes():
                for g in range(2):
                    bank = 5 - g
                    n = 4 if g == 0 else 3
                    transposes([vTf[:, 64 + 128 * (4 * g + i):64 + 128 * (4 * g + i) + 128] for i in range(n)],
                               bank, [S_vTf])
                    cp("dve", vO[:, 4 * g:4 * g + n, :],
                       ps[bank][:, 0:128 * n].rearrange("p (a b) -> p a b", a=n), [S_ps[bank]], [S_vO])

            def load_cache(kd, vd, col0):
                P.dma("sp", s_a[0], cKf, kd.rearrange("(b p) d -> p b d", p=128)[:, :, col0:col0 + 128],
                      writes=[S_cKf])
                P.dma("pool", s_p[0], cV, vd.rearrange("(b p) d -> p b d", p=128)[:, :, col0:col0 + 128],
                      writes=[S_cV])
                transposes([cKf[:, b, :] for b in range(4)], 5, [S_cKf])
                cp("act", cKT, ps[5][:, :], [S_ps[5]], [S_cKT])

            def normalize(hf, chunk, sink_h):
                if sink_h is not None:
                    P.op("dve", lambda e: e.tensor_scalar(out=n1, in0=ps[7][:, :], scalar1=es[:, sink_h:sink_h + 1],
                                                          scalar2=None, op0=ALU.add), reads=[S_ps[7]], writes=[S_n])
                    P.op("act", lambda e: e.activation(out=n2, in_=ps[6][:, :], func=AF.Copy), reads=[S_ps[6]],
                         writes=[S_n2])
                    P.op("dve", lambda e: e.reciprocal(out=n1, in_=n1), reads=[S_n], writes=[S_n])
                else:
                    P.op("dve", lambda e: e.reciprocal(out=n1, in_=ps[7][:, :]), reads=[S_ps[7]], writes=[S_n])
                    P.op("act", lambda e: e.activation(out=n2, in_=ps[6][:, :], func=AF.Copy), reads=[S_ps[6]],
                         writes=[S_n2])
                P.op("dve", lambda e: e.tensor_tensor(out=n2, in0=n2, in1=n1, op=ALU.mult),
                     reads=[S_n], writes=[S_n2])
                P.op("dve", lambda e: e.tensor_tensor(out=og[:, chunk, 512 * hf:512 * hf + 512], in0=n2,
                                                      in1=sg[:, 512 * hf:512 * hf + 512], op=ALU.mult),
                     reads=[S_n2, S_sg], writes=[S_og])

            def pv_part(blocks_v, pt, S_pt, ocol, w, extra_rd, first=True, last=True):
                nb = len(blocks_v)

                def f(e):
                    for i, (vap, c0) in enumerate(blocks_v):
                        e.matmul(ps[6][:, ocol:ocol + w], lhsT=vap, rhs=pt[:, c0:c0 + w],
                                 start=(first and i == 0), stop=(last and i == nb - 1))
                    for i, (vap, c0) in enumerate(blocks_v):
                        ins = e.matmul(ps[7][:, ocol:ocol + w], lhsT=onesb[:], rhs=pt[:, c0:c0 + w],
                                       start=(first and i == 0), stop=(last and i == nb - 1))
                    return ins
                P.op("pe", f, reads=[S_pt] + extra_rd, writes=[S_ps[6], S_ps[7]])

            def run_stages(stages):
                n = len(stages)
                stages[0]["sf"]()
                for i in range(n):
                    stages[i]["ex"]()
                    if i + 1 < n:
                        stages[i + 1]["sf"]()
                    stages[i]["pv"]()
                    if stages[i].get("post"):
                        stages[i]["post"]()

            def attn_A_lat(h):
                stages = []
                for n in range(8):
                    blocks = []
                    if n > 0:
                        blocks.append((kT[:, 128 * (n - 1):128 * n], vE[:, n - 1, :], amk[:, 0:128]))
                    blocks.append((kT[:, 128 * n:128 * n + 128], vE[:, n, :], None))
                    if n < 7:
                        blocks.append((kT[:, 128 * (n + 1):128 * (n + 2)], vE[:, n + 1, :], amk[:, 128:256]))
                    for cbk_ in range(4):
                        blocks.append((cKT[:, 128 * cbk_:128 * cbk_ + 128], cV[:, cbk_, :], None))
                    for sidx in range(2):
                        bl = blocks[4 * sidx:4 * sidx + 4]
                        bank = 4 + sidx
                        pi = 2 * (n % 2) + sidx
                        wS = 128 * len(bl)

                        def sfn(bl=bl, n=n, bank=bank):
                            def sf(e):
                                for idx, (kap, vap, m) in enumerate(bl):
                                    o = ps[bank][:, 128 * idx:128 * idx + 128]
                                    ins = e.matmul(o, lhsT=kap, rhs=qT[:, 128 * n:128 * n + 128], start=True,
                                                   stop=(m is None))
                                    if m is not None:
                                        ins = e.matmul(o, lhsT=identb[:], rhs=m, start=False, stop=True)
                                return ins
                            P.op("pe", sf, reads=[S_kT, S_qT, S_cKT], writes=[S_ps[bank]])

                        def exn(bank=bank, pi=pi, wS=wS):
                            P.op("act", lambda e: e.activation(out=PT[pi][:, 0:wS], in_=ps[bank][:, 0:wS], func=AF.Exp),
                                 reads=[S_ps[bank]], writes=[S_PT[pi]])

                        def pvn(bl=bl, pi=pi, n=n, sidx=sidx):
                            pv_part([(vap, 128 * idx) for idx, (kap, vap, m) in enumerate(bl)], PT[pi], S_PT[pi],
                                    128 * (n % 4), 128, [S_vE, S_cV], first=(sidx == 0), last=(sidx == 1))
                        post = None
                        if sidx == 1 and n % 4 == 3:
                            post = (lambda n=n: normalize(n // 4, h, h))
                        stages.append(dict(sf=sfn, ex=exn, pv=pvn, post=post))
                run_stages(stages)

            def attn_B_lat(h):
                stages = []
                for r in range(16):
                    rs = min(max(r - 4, 0), 8)
                    sbk = 4 + r % 2
                    pk = r % 2

                    def sfn(r=r, rs=rs, sbk=sbk):
                        def sf(e):
                            for b in range(4):
                                o = ps[sbk][:, 64 * b:64 * b + 64]
                                t0 = 64 * (rs + 2 * b)
                                e.matmul(o, lhsT=kT[:, t0:t0 + 128], rhs=qT[:, 64 * r:64 * r + 64], start=True,
                                         stop=False)
                                e.matmul(o, lhsT=identb[:], rhs=bt[:, rs + 2 * b - r + 7, :], start=False, stop=True)
                            for c_ in range(4):
                                ins = e.matmul(ps[sbk][:, 256 + 64 * c_:256 + 64 * c_ + 64],
                                               lhsT=cKT[:, 128 * c_:128 * c_ + 128], rhs=qT[:, 64 * r:64 * r + 64],
                                               start=True, stop=True)
                            return ins
                        P.op("pe", sf, reads=[S_kT, S_qT, S_cKT, S_bt], writes=[S_ps[sbk]])

                    def exn(pk=pk, sbk=sbk):
                        P.op("act", lambda e: e.activation(out=PT[pk][:, 0:512], in_=ps[sbk][:, :], func=AF.Exp),
                             reads=[S_ps[sbk]], writes=[S_PT[pk]])

                    def pvn(r=r, rs=rs, pk=pk):
                        bl = []
                        for b in range(4):
                            vap = vE[:, (rs + 2 * b) // 2, :] if rs % 2 == 0 else vO[:, (rs - 1) // 2 + b, :]
                            bl.append((vap, 64 * b))
                        for c_ in range(4):
                            bl.append((cV[:, c_, :], 256 + 64 * c_))
                        pv_part(bl, PT[pk], S_PT[pk], 64 * (r % 8), 64, [S_vE, S_vO, S_cV])
                    post = (lambda r=r: normalize(r // 8, 16 + h, None)) if r % 8 == 7 else None
                    stages.append(dict(sf=sfn, ex=exn, pv=pvn, post=post))
                run_stages(stages)

            def attn_ctx(chunk, sink_h):
                stages = []
                for s_ in range(4):
                    sbk = 4 + s_ % 2
                    pk = s_ % 2

                    def sfn(s_=s_, sbk=sbk):
                        def sf(e):
                            for b in range(2):
                                ins = e.matmul(ps[sbk][:, 256 * b:256 * b + 256],
                                               lhsT=kT[:, 256 * s_ + 128 * b:256 * s_ + 128 * b + 128],
                                               rhs=qT[:, 256 * s_:256 * s_ + 256], start=True, stop=True)
                            return ins
                        P.op("pe", sf, reads=[S_kT, S_qT], writes=[S_ps[sbk]])

                    def exn(pk=pk, sbk=sbk):
                        P.op("act", lambda e: e.activation(out=PT[pk][:, 0:512], in_=ps[sbk][:, :], func=AF.Exp),
                             reads=[S_ps[sbk]], writes=[S_PT[pk]])

                    def pvn(s_=s_, pk=pk):
                        pv_part([(vE[:, 2 * s_ + b, :], 256 * b) for b in range(2)], PT[pk], S_PT[pk],
                                256 * (s_ % 2), 256, [S_vE])
                    post = (lambda s_=s_: normalize(s_ // 2, chunk, sink_h)) if s_ % 2 == 1 else None
                    stages.append(dict(sf=sfn, ex=exn, pv=pvn, post=post))
                run_stages(stages)

            W = w_in_attn_d
            jobs = []

            def J(col, consume):
                jobs.append(dict(w=W[col // 128], nk=32, act=hT[:], rd=[S_hT], consume=consume))

            for g in range(4):
                if is_sample:
                    def ck(pair, g=g):
                        load_cache(cak_d, cav_d, 128 * g)
                        rope_evac(pair, kT, S_kT, None)

                    def cv_(pair):
                        evac2(pair, vTf, S_vTf)
                        v_transposes()
                else:
                    def ck(pair, g=g):
                        evac2(pair, kT, S_kT, excl=True)
                        evac2(pair, kf, S_kf, eng="dve", excl=True)
                        k_out(nak_d, 128 * g)

                    def cv_(pair, g=g):
                        evac2(pair, vTf, S_vTf)
                        v_transposes(nav_d, 128 * g)
                J(2048 + 128 * g, ck)
                J(2560 + 128 * g, cv_)
                for hq in range(4):
                    h = 4 * g + hq
                    if is_sample:
                        def cq(pair):
                            rope_evac(pair, qT, S_qT, SCALE)
                    else:
                        def cq(pair):
                            evac2(pair, qT, S_qT, SCALE)

                    def cg(pair, h=h):
                        for hf in range(2):
                            P.op("act", lambda e, hf=hf: e.activation(out=sg[:, 512 * hf:512 * hf + 512],
                                                                       in_=ps[pair + hf][:, :], func=AF.Silu),
                                 reads=[S_ps[pair + hf]], writes=[S_sg])
                        if is_sample:
                            attn_A_lat(h)
                        else:
                            attn_ctx(h, h)
                    J(128 * h, cq)
                    J(9216 + 128 * h, cg)
            first_b = [True]
            for h in range(16):
                def ck(pair, h=h):
                    if is_sample:
                        if first_b[0]:
                            first_b[0] = False
                            P.barrier()
                        load_cache(cbk_d, cbv_d, 128 * h)
                        P.dma("pool", s_p[1], bt, TB_d[h].rearrange("p (a b) -> p a b", a=14), writes=[S_bt])
                        evac2(pair, kT, S_kT)
                    else:
                        evac2(pair, kT, S_kT, excl=True)
                        evac2(pair, kf, S_kf, eng="dve", excl=True)
                        k_out(nbk_d, 128 * h)

                def cv_(pair, h=h):
                    evac2(pair, vTf, S_vTf)
                    if is_sample:
                        v_transposes()
                        vO_transposes()
                    else:
                        v_transposes(nbv_d, 128 * h)

                def cq(pair):
                    evac2(pair, qT, S_qT, SCALE)

                def cg(pair, h=h):
                    for hf in range(2):
                        P.op("act", lambda e, hf=hf: e.activation(out=sg[:, 512 * hf:512 * hf + 512],
                                                                   in_=ps[pair + hf][:, :], func=AF.Silu),
                             reads=[S_ps[pair + hf]], writes=[S_sg])
                    if is_sample:
                        attn_B_lat(h)
                    else:
                        attn_ctx(16 + h, None)
                J(5120 + 128 * h, ck)
                J(7168 + 128 * h, cv_)
                J(3072 + 128 * h, cq)
                J(9216 + 2048 + 128 * h, cg)
            if is_sample:
                mixed = []
                for i, jb in enumerate(jobs):
                    mixed.append(jb)
                    if i < 96:
                        def cada(pair, j=i):
                            P.op("dve", lambda e: e.tensor_scalar(out=modt[:, 1, j, :], in0=ps[pair][:, 0:2],
                                                                  scalar1=bTt[:, 1, j:j + 1], scalar2=None,
                                                                  op0=ALU.add),
                                 reads=[S_ps[pair]], writes=[S_c])
                        mixed.append(dict(w=w_ada1_d[i], nk=32, act=None, rd=[S_c],
                                          consume=cada, gemv=True))
                jobs = mixed
            gemm_stream(jobs)
            if is_sample:
                gm_finish(1)
            P.barrier()

        def S_out1_P_norm0():
            phase_outproj(w_out_pool_d, scrB, S_scrB, scrA, S_scrA, 1, 1, 0,
                          side=lambda: gen_norm(scrC, S_scrC, 0, 0, 1, AR.alloc, s_b[0:2], 2))

        def P_out0_S_final():
            phase_outproj(w_out_attn_d, scrC, S_scrC, scrD, S_scrD, 0, 0, 1,
                          side=lambda: gen_final(scrA, S_scrA, ys_d, 0, s_b))

        steps = [
            ("ada", phase_ada_input),
            ("Snorm0", lambda: phase_norm(scrA, S_scrA, 0, 1, 0)),
            ("Sattn", lambda: phase_attn(True)),
            ("Sout0", lambda: phase_outproj(w_out_attn_d, scrA, S_scrA, scrB, S_scrB, 0, 1, 0)),
            ("Snorm1", lambda: phase_norm(scrB, S_scrB, 1, 1, 0)),
            ("Spool", lambda: phase_pool(True)),
            ("Sout1", S_out1_P_norm0),
            ("Pattn", lambda: phase_attn(False)),
            ("Pout0", P_out0_S_final),
            ("Pnorm1", lambda: phase_norm(scrD, S_scrD, 1, 0, 1)),
            ("Ppool", lambda: phase_pool(False)),
            ("Pout1", lambda: phase_outproj(w_out_pool_d, scrD, S_scrD, scrC, S_scrC, 1, 0, 1)),
            ("Pfinal", lambda: phase_final(scrC, S_scrC, yp_d, 1)),
        ]
        for name, fn in steps:
            fn()
            if stop == name:
                break
        P.barrier()
        P.cnt_report = dict(P.cnt)
        print("sem counts", P.cnt)
        P.run()
    return nc


def _consts():
    f32 = np.float32
    ident = np.eye(128, dtype=f32)
    permT = np.zeros((128, 128), f32)
    sign = np.zeros(128, f32)
    for m in range(128):
        a, rem = divmod(m, 64)
        if rem < 32:
            src, sg_ = a * 64 + 32 + rem, -1.0
        else:
            src, sg_ = a * 64 + rem - 32, 1.0
        permT[src, m] = 1.0
        sign[m] = sg_
    t = np.arange(1024)
    inv_freq = (f32(10000.0) ** (-np.arange(32, dtype=f32) / f32(32))).astype(f32)
    ang_r = (t // 64).astype(f32)[:, None] * inv_freq
    ang_c = (t % 64).astype(f32)[:, None] * inv_freq
    ang = np.concatenate([ang_r, ang_r, ang_c, ang_c], axis=-1).astype(f32)
    ropec = np.ascontiguousarray(np.cos(ang).T.astype(f32))
    ropes = np.ascontiguousarray((np.sin(ang).T * sign[:, None]).astype(f32))
    jj = np.arange(128)[:, None]
    qq = np.arange(128)[None, :]
    amask = np.concatenate([np.where(qq <= jj, 0.0, NEG), np.where(jj <= qq, 0.0, NEG)], axis=1).astype(f32)

    def rc(L, reps):
        out = np.zeros((4, L), f32)
        tt = np.arange(L)
        for gi, half in enumerate((1, 2, 4, 8)):
            lo = np.clip(tt - half, 0, L)
            hi = np.clip(tt + half, 0, L)
            out[gi] = (f32(1.0) / (hi - lo).astype(f32)).astype(f32)
        return np.ascontiguousarray(np.tile(out, (1, reps)))
    return dict(ident=ident, permT=permT, ropec=ropec, ropes=ropes, amask=amask, rcS=rc(1024, 1), rcP=rc(256, 4))


def _bias_table(rpb):
    c = np.arange(64)
    kc = np.arange(64)
    col_start = np.clip(c - 8, 0, 48)
    ok = (kc[:, None] >= col_start[None, :]) & (kc[:, None] < col_start[None, :] + 16)
    dc = np.clip(kc[:, None] - c[None, :] + 15, 0, 30)
    TB = np.empty((16, 2, 64, 14, 64), np.float32)
    for a in range(2):
        for dr0 in range(14):
            g = rpb[:, dr0 + a][:, dc]
            TB[:, a, :, dr0, :] = np.where(ok[None], g, np.float32(NEG))
    return np.ascontiguousarray(TB.reshape(16, 128, 14 * 64))


_NC_CACHE = {}
_NCORES = [8]
_PREP_ONLY = [False]


def kernel(x_prompt, x_sample, c, cache_a_k, cache_a_v, cache_b_k, cache_b_v, c_ctx,
           w_ada, b_ada, norm_g, w_in_attn, a_sink, b_rpb, w_out_attn,
           w_in_pool, w_grp_pool, pool_scale, w_out_pool, final_g):
    f = lambda a: np.ascontiguousarray(np.asarray(a, dtype=np.float32))
    x_prompt, x_sample, c, c_ctx = f(x_prompt), f(x_sample), f(c), f(c_ctx)
    cache_a_k, cache_a_v, cache_b_k, cache_b_v = f(cache_a_k), f(cache_a_v), f(cache_b_k), f(cache_b_v)
    w_ada, b_ada, norm_g = f(w_ada), f(b_ada), f(norm_g)
    consts = _consts()
    def tile_w(w2d):
        K, N = w2d.shape
        return np.ascontiguousarray(w2d.reshape(K // 128, 128, N // 128, 128).transpose(2, 1, 0, 3)).reshape(
            N // 128, 128, (K // 128) * 128)
    wg = f(w_grp_pool).reshape(4, 1024, 1024)
    shared = dict(
        w_ada0=np.ascontiguousarray(w_ada[0]), w_ada1=tile_w(w_ada[1]),
        bT=f(b_ada.reshape(2, 96, 128).transpose(2, 0, 1).reshape(128, 192)),
        gT=f(norm_g.reshape(2, 32, 128).transpose(2, 0, 1).reshape(128, 64)),
        fgT=f(np.asarray(final_g, np.float32).reshape(32, 128).T),
        pscT=f(np.asarray(pool_scale, np.float32).reshape(32, 128).T),
        w_in_attn=tile_w(f(w_in_attn).reshape(4096, 13312)),
        w_out_attn=tile_w(f(w_out_attn).reshape(4096, 4096)),
        w_in_pool=tile_w(f(w_in_pool).reshape(4096, 8192)),
        w_grp=np.concatenate([tile_w(wg[g]) for g in range(4)], axis=0),
        w_out_pool=tile_w(f(w_out_pool).reshape(4096, 4096)),
        sinkb=f(np.broadcast_to(np.asarray(a_sink, np.float32).reshape(1, 16), (128, 16))),
        TB=_bias_table(np.asarray(b_rpb, np.float32)[0]),
        **consts,
    )
    in_maps = []
    for i in range(_NCORES[0]):
        cond = np.stack([c_ctx, c[i]], axis=0)
        cond2 = f(cond.reshape(2, 32, 128).transpose(2, 1, 0).reshape(128, 64))
        m = dict(shared)
        m.update(
            xs=f(x_sample[i]), xp=f(x_prompt[4 * i:4 * i + 4].reshape(1024, 4096)), cond2=cond2,
            cak=f(cache_a_k[i, 0].reshape(512, 512)), cav=f(cache_a_v[i, 0].reshape(512, 512)),
            cbk=f(cache_b_k[i, 0].reshape(512, 2048)), cbv=f(cache_b_v[i, 0].reshape(512, 2048)),
        )
        in_maps.append(m)
    if _PREP_ONLY[0]:
        return in_maps
    if "nc" not in _NC_CACHE:
        _NC_CACHE["nc"] = build_nc()
    res = run_bass_kernel_spmd(_NC_CACHE["nc"], in_maps, core_ids=list(range(8)))
    R = res.results
    y_sample = np.stack([R[i]["ys"] for i in range(8)], axis=0)
    y_prompt = np.concatenate([R[i]["yp"].reshape(4, 256, 4096) for i in range(8)], axis=0)
    cat = lambda k, nh: np.concatenate([R[i][k].reshape(4, 1, 256, nh, 128) for i in range(8)], axis=0)
    return (y_prompt, y_sample, cat("nak", 4), cat("nav", 4), cat("nbk", 16), cat("nbv", 16))
```

```python
import numpy as np
from contextlib import ExitStack
import concourse.bass as bass
import concourse.mybir as mybir
from concourse.bass_utils import run_bass_kernel_spmd

F32 = mybir.dt.float32
BF16 = mybir.dt.bfloat16
AF = mybir.ActivationFunctionType
ALU = mybir.AluOpType

ENGS = ("pe", "act", "dve", "pool", "sp")
NEG = -30000.0
SCALE = 128 ** -0.5
EPS = 1e-6
ARENA_W = 9728
NSLAB = 3


class Slot:
    __slots__ = ("w", "r")

    def __init__(self):
        self.w = None
        self.r = {}


class Prog:
    def __init__(self, nc, stack):
        self.nc = nc
        self.stack = stack
        self.q = {e: [] for e in ENGS}
        self.cnt = {}
        self.sem = {}
        self.waited = {e: {} for e in ENGS}
        for e in ENGS:
            if e != "sp":
                self.new_sem("c_" + e)

    def new_sem(self, name):
        self.sem[name] = self.stack.enter_context(self.nc.semaphore(name))
        self.cnt[name] = 0
        return name

    def _wait(self, eng, tok):
        if tok is None:
            return
        sname, val = tok
        if self.waited[eng].get(sname, 0) >= val:
            return
        self.waited[eng][sname] = val
        sem = self.sem[sname]
        self.q[eng].append(lambda e, sem=sem, val=val: e.wait_ge(sem, val))

    def _deps(self, eng, reads, writes, extra):
        best = {}
        toks = list(extra)
        for s in reads:
            toks.append(s.w)
        for s in writes:
            toks.append(s.w)
            toks.extend(s.r.items())
        for t in toks:
            if t is not None and best.get(t[0], 0) < t[1]:
                best[t[0]] = t[1]
        for sname, val in best.items():
            self._wait(eng, (sname, val))

    def _commit(self, tok, reads, writes):
        for s in reads:
            if s.r.get(tok[0], 0) < tok[1]:
                s.r[tok[0]] = tok[1]
        for s in writes:
            s.w = tok
            s.r = {}

    def op(self, eng, fn, reads=(), writes=(), extra=()):
        self._deps(eng, reads, writes, extra)
        sname = "c_" + eng
        self.cnt[sname] += 1
        sem = self.sem[sname]
        self.q[eng].append(lambda e, fn=fn, sem=sem: fn(e).then_inc(sem, 1))
        tok = (sname, self.cnt[sname])
        self._commit(tok, reads, writes)
        return tok

    def dma(self, eng, sname, out, in_, reads=(), writes=(), extra=()):
        self._deps(eng, reads, writes, extra)
        self.cnt[sname] += 16
        sem = self.sem[sname]
        self.q[eng].append(lambda e, out=out, in_=in_, sem=sem: e.dma_start(out=out, in_=in_).then_inc(sem, 16))
        tok = (sname, self.cnt[sname])
        self._commit(tok, reads, writes)
        return tok

    def barrier(self):
        toks = [(s, c) for s, c in self.cnt.items() if c > 0]
        for e in ENGS:
            for t in toks:
                self._wait(e, t)

    def run(self):
        with self.nc.Block() as block:
            @block.tensor
            def _(e):
                for f in self.q["pe"]:
                    f(e)

            @block.scalar
            def _(e):
                for f in self.q["act"]:
                    f(e)

            @block.vector
            def _(e):
                for f in self.q["dve"]:
                    f(e)

            @block.gpsimd
            def _(e):
                for f in self.q["pool"]:
                    f(e)

            @block.sync
            def _(e):
                for f in self.q["sp"]:
                    f(e)


class Arena:
    def __init__(self, t, nwords):
        self.t, self.n, self.off = t, nwords, 0

    def reset(self):
        self.off = 0

    def alloc(self, shape, dt):
        nel = int(np.prod(shape[1:]))
        nw = (nel * (2 if dt == BF16 else 4) + 3) // 4
        nw = (nw + 1) // 2 * 2
        assert self.off + nw <= self.n, ("arena overflow", self.off, nw, self.n)
        v = self.t[:, self.off:self.off + nw]
        self.off += nw
        if dt == BF16:
            v = v.bitcast(BF16)
        v = v[:, 0:nel]
        if len(shape) == 3:
            v = v.rearrange("p (a b) -> p a b", a=shape[1])
        elif len(shape) == 4:
            v = v.rearrange("p (a b c) -> p a b c", a=shape[1], b=shape[2])
        return v


def build_nc(stop=None):
    nc = bass.Bass("TRN2", target_bir_lowering=False)

    def din(name, shape):
        return nc.dram_tensor(name, list(shape), F32, kind="ExternalInput").ap()

    def dout(name, shape):
        return nc.dram_tensor(name, list(shape), F32, kind="ExternalOutput").ap()

    xs_d, xp_d = din("xs", (1024, 4096)), din("xp", (1024, 4096))
    cond2_d = din("cond2", (128, 64))
    w_ada0_d = din("w_ada0", (4096, 12288))
    w_ada1_d = din("w_ada1", (96, 128, 4096))
    bT_d, gT_d = din("bT", (128, 192)), din("gT", (128, 64))
    fgT_d, pscT_d = din("fgT", (128, 32)), din("pscT", (128, 32))
    w_in_attn_d = din("w_in_attn", (104, 128, 4096))
    w_out_attn_d = din("w_out_attn", (32, 128, 4096))
    w_in_pool_d = din("w_in_pool", (64, 128, 4096))
    w_grp_d = din("w_grp", (32, 128, 1024))
    w_out_pool_d = din("w_out_pool", (32, 128, 4096))
    sinkb_d = din("sinkb", (128, 16))
    TB_d = din("TB", (16, 128, 896))
    cak_d, cav_d = din("cak", (512, 512)), din("cav", (512, 512))
    cbk_d, cbv_d = din("cbk", (512, 2048)), din("cbv", (512, 2048))
    ident_d, permT_d = din("ident", (128, 128)), din("permT", (128, 128))
    ropec_d, ropes_d = din("ropec", (128, 1024)), din("ropes", (128, 1024))
    amask_d = din("amask", (128, 256))
    rcS_d, rcP_d = din("rcS", (4, 1024)), din("rcP", (4, 1024))
    ys_d, yp_d = dout("ys", (1024, 4096)), dout("yp", (1024, 4096))
    nak_d, nav_d = dout("nak", (1024, 512)), dout("nav", (1024, 512))
    nbk_d, nbv_d = dout("nbk", (1024, 2048)), dout("nbv", (1024, 2048))
    scrA = nc.dram_tensor("scrA", [32, 128, 1024], F32).ap()
    scrB = nc.dram_tensor("scrB", [32, 128, 1024], F32).ap()
    scrC = nc.dram_tensor("scrC", [32, 128, 1024], F32).ap()
    scrD = nc.dram_tensor("scrD", [32, 128, 1024], F32).ap()

    with ExitStack() as st:
        P = Prog(nc, st)

        def sb(name, shape, dt):
            return st.enter_context(nc.sbuf_tensor(name, list(shape), dt))

        hT = sb("hT", (128, 32, 1024), BF16)
        og = sb("og", (128, 32, 1024), BF16)
        slab = [sb(f"slab{i}", (128, 4096), BF16) for i in range(NSLAB)]
        identf = sb("identf", (128, 128), F32)
        identb = sb("identb", (128, 128), BF16)
        onesb = sb("onesb", (128, 128), BF16)
        modt = sb("modt", (128, 2, 96, 2), F32)
        gm = sb("gm", (128, 2, 2, 32), F32)
        gTt = sb("gTt", (128, 2, 32), F32)
        fg = sb("fg", (128, 32), F32)
        psc = sb("psc", (128, 32), F32)
        bTt = sb("bTt", (128, 2, 96), F32)
        es = sb("es", (128, 16), F32)
        epsT = sb("epsT", (128, 1), F32)
        cb = sb("cb", (128, 32, 2), BF16)
        rstdT = [sb("rstdS", (128, 1024), F32), sb("rstdP", (128, 1024), F32)]
        arena_t = sb("arena", (128, ARENA_W), F32)
        AR = Arena(arena_t, ARENA_W)
        ps = [st.enter_context(nc.psum_tensor(f"ps{i}", [128, 512], F32)) for i in range(8)]
        S_ps = [Slot() for _ in range(8)]
        S_slab = [Slot() for _ in range(NSLAB)]
        s_slab = [P.new_sem(f"s_slab{i}") for i in range(NSLAB)]
        S_hT, S_og, S_c = Slot(), Slot(), Slot()
        S_rstdT = [Slot(), Slot()]
        S_scrA = [Slot() for _ in range(32)]
        S_scrB = [Slot() for _ in range(32)]
        S_scrC = [Slot() for _ in range(32)]
        S_scrD = [Slot() for _ in range(32)]
        CUR = {"ti": 0}
        s_c = P.new_sem("s_c")
        s_a = [P.new_sem(f"s_a{i}") for i in range(8)]
        s_b = [P.new_sem(f"s_b{i}") for i in range(6)]
        s_p = [P.new_sem(f"s_p{i}") for i in range(2)]
        G = {"slab_rr": 0, "gemm_rr": 0}

        og_f = og[:].rearrange("p a b -> p (a b)").bitcast(F32)
        og_b = og[:].rearrange("p a b -> p (a b)")

        def cp(eng, out, in_, reads, writes, scale=None):
            if eng == "act":
                if scale is None:
                    return P.op("act", lambda e: e.activation(out=out, in_=in_, func=AF.Copy), reads, writes)
                return P.op("act", lambda e: e.activation(out=out, in_=in_, func=AF.Copy, scale=scale), reads, writes)
            return P.op("dve", lambda e: e.tensor_copy(out=out, in_=in_), reads, writes)

        P.dma("sp", s_c, identf[:], ident_d[:, :], writes=[S_c])
        P.dma("sp", s_c, gTt[:].rearrange("p a b -> p (a b)"), gT_d[:, :], writes=[S_c])
        P.dma("sp", s_c, fg[:], fgT_d[:, :], writes=[S_c])
        P.dma("sp", s_c, psc[:], pscT_d[:, :], writes=[S_c])
        P.dma("sp", s_c, bTt[:].rearrange("p a b -> p (a b)"), bT_d[:, :], writes=[S_c])
        P.dma("sp", s_c, es[:], sinkb_d[:, :], writes=[S_c])
        P.op("dve", lambda e: e.memset(onesb[:], 1.0), writes=[S_c])
        P.op("dve", lambda e: e.memset(epsT[:], EPS), writes=[S_c])
        P.op("dve", lambda e: e.tensor_copy(out=identb[:], in_=identf[:]), reads=[S_c], writes=[S_c])
        P.op("act", lambda e: e.activation(out=es[:], in_=es[:], func=AF.Exp), reads=[S_c], writes=[S_c])
        P.barrier()

        def phase_ada_input():
            AR.reset()
            cf = AR.alloc((128, 64), F32)
            S_cf, S_cb, S_mrow = Slot(), Slot(), Slot()
            P.dma("sp", s_a[3], cf, cond2_d[:, :], writes=[S_cf])
            P.op("act", lambda e: e.activation(out=cb.rearrange("p a b -> p (a b)"), in_=cf, func=AF.Silu),
                 reads=[S_cf], writes=[S_cb, S_c])
            hT_f = hT[:].rearrange("p a b -> p (a b)").bitcast(F32)
            mrow = hT_f[0:2, 0:12288]
            xblk = [og_f[:, 0:4096], og_f[:, 4096:8192]]
            xst = og_f[:, 8192:12288].rearrange("p (a b) -> p a b", a=32)
            sq = og_b[:, 24576:28672].rearrange("p (a b) -> p a b", a=32)
            S_xb = [Slot(), Slot()]
            S_xg = [Slot() for _ in range(8)]
            S_sq = Slot()

            def input_block(ib):
                ti, b = divmod(ib, 8)
                xsrc = xs_d if ti == 0 else xp_d
                scr, S_scr = (scrA, S_scrA) if ti == 0 else (scrC, S_scrC)
                scr_v = scr.rearrange("j p t -> p j t")
                k = ib % 2
                P.dma("sp", s_a[k], xblk[k], xsrc[128 * b:128 * b + 128, :], writes=[S_xb[k]])
                for g in range(8):
                    bank = 4 + g % 2

                    def tr(e, k=k, g=g, bank=bank):
                        for j in range(4):
                            c = 4 * g + j
                            ins = e.transpose(out=ps[bank][:, 128 * j:128 * j + 128],
                                              in_=xblk[k][:, 128 * c:128 * c + 128], identity=identf[:])
                        return ins
                    P.op("pe", tr, reads=[S_xb[k]], writes=[S_ps[bank]])
                    cp("act" if g % 2 == 0 else "dve", xst[:, 4 * g:4 * g + 4, :],
                       ps[bank][:].rearrange("p (a b) -> p a b", a=4), [S_ps[bank]], [S_xg[g]])
                P.op("act", lambda e: e.activation(out=sq, in_=xst, func=AF.Square), reads=S_xg, writes=[S_sq])
                col = 128 * (b % 4)

                def ssq(e, col=col):
                    for c in range(32):
                        ins = e.matmul(ps[6][:, col:col + 128], lhsT=onesb[:], rhs=sq[:, c, :],
                                       start=(c == 0), stop=(c == 31))
                    return ins
                P.op("pe", ssq, reads=[S_sq], writes=[S_ps[6]])
                rstd_from(ps[6][:, col:col + 128], rstdT[ti][:, 128 * b:128 * b + 128], [S_ps[6]], ti)
                P.dma("sp", s_a[2], scr_v[:, :, 128 * b:128 * b + 128], xst, reads=S_xg, writes=S_scr)

            def mod_finish(l):
                def tr(e):
                    for j in range(96):
                        ins = e.transpose(out=ps[7][:, 2 * j:2 * j + 2], in_=mrow[:, 128 * j:128 * j + 128],
                                          identity=identf[0:2, 0:2])
                    return ins
                P.op("pe", tr, reads=[S_mrow], writes=[S_ps[7]])
                pv = ps[7][:, 0:192].rearrange("p (j v) -> p j v", v=2)
                for v in range(2):
                    P.op("dve", lambda e, v=v: e.tensor_tensor(
                        out=modt[:, l, :, v], in0=pv[:, :, v], in1=bTt[:, l, :], op=ALU.add),
                        reads=[S_ps[7]], writes=[S_c])
                gm_finish(l)

            q = 0
            nib = 0
            for l in range(1):
                for n in range(24):
                    pb = n % 2
                    for kq in range(4):
                        s = G["slab_rr"] % NSLAB
                        G["slab_rr"] += 1
                        sv = slab[s][:].rearrange("p (c n) -> p c n", n=512)
                        wv = w_ada0_d[1024 * kq:1024 * kq + 1024, 512 * n:512 * n + 512].rearrange(
                            "(c p) n -> p c n", p=128)
                        P.dma("pool", s_slab[s], sv, wv, writes=[S_slab[s]])

                        def mm(e, sv=sv, kq=kq, pb=pb):
                            for c in range(8):
                                ins = e.matmul(ps[pb][0:2, :], lhsT=cb[:, 8 * kq + c, :], rhs=sv[:, c, :],
                                               start=(kq == 0 and c == 0), stop=(kq == 3 and c == 7))
                            return ins
                        P.op("pe", mm, reads=[S_slab[s], S_cb], writes=[S_ps[pb]])
                    cp("act", mrow[:, 512 * n:512 * n + 512], ps[pb][0:2, :], [S_ps[pb]], [S_mrow])
                    q += 1
                    while nib < 16 and nib < (q * 2 + 2) // 3:
                        input_block(nib)
                        nib += 1
                mod_finish(l)
            P.barrier()

        def rstd_from(ps_ap, out_ap, rd, ti):
            P.op("act", lambda e: e.activation(out=out_ap, in_=ps_ap, func=AF.Sqrt, bias=epsT[:, 0:1],
                                               scale=1.0 / 4096.0), reads=rd, writes=[S_rstdT[ti]])
            P.op("dve", lambda e: e.reciprocal(out=out_ap, in_=out_ap), reads=[S_rstdT[ti]], writes=[S_rstdT[ti]])

        def gm_finish(l):
            for v in range(2):
                P.op("dve", lambda e, v=v: e.tensor_scalar(
                    out=gm[:, l, v, :], in0=modt[:, l, 32:64, v], scalar1=1.0, scalar2=None, op0=ALU.add),
                    reads=[S_c], writes=[S_c])
                P.op("dve", lambda e, v=v: e.tensor_tensor(
                    out=gm[:, l, v, :], in0=gm[:, l, v, :], in1=gTt[:, l, :], op=ALU.mult),
                    reads=[S_c], writes=[S_c])

        def gemm_stream(jobs):
            LOOK = NSLAB - 1
            nload = 0
            for i, job in enumerate(jobs):
                while nload < len(jobs) and nload <= i + LOOK - 1:
                    jb = jobs[nload]
                    s = G["slab_rr"] % NSLAB
                    G["slab_rr"] += 1
                    jb["s"] = s
                    nk = jb["nk"]
                    sv = slab[s][:, 0:nk * 128].rearrange("p (a b) -> p a b", b=1024)
                    wv = jb["w"].rearrange("p (a b) -> p a b", b=1024)
                    P.dma("pool", s_slab[s], sv, wv, writes=[S_slab[s]])
                    nload += 1
                s, nk, act = job["s"], job["nk"], job["act"]
                pair = 2 * (G["gemm_rr"] % 2)
                G["gemm_rr"] += 1
                sv = slab[s][:, 0:nk * 128].rearrange("p (c n) -> p c n", n=128)

                def mm(e, sv=sv, nk=nk, act=act, pair=pair):
                    for c in range(nk):
                        for hf in range(2):
                            ins = e.matmul(ps[pair + hf][:, :], lhsT=sv[:, c, :],
                                           rhs=act[:, c, 512 * hf:512 * hf + 512],
                                           start=(c == 0), stop=(c == nk - 1))
                    return ins
                if job.get("gemv"):
                    def mm(e, sv=sv, pair=pair):
                        for c in range(32):
                            ins = e.matmul(ps[pair][:, 0:2], lhsT=sv[:, c, :], rhs=cb[:, c, :],
                                           start=(c == 0), stop=(c == 31))
                        return ins
                P.op("pe", mm, reads=[S_slab[s]] + job["rd"], writes=[S_ps[pair], S_ps[pair + 1]])
                job["consume"](pair)

        def gen_norm(scr, S_scr, l, v, ti, alloc, sems, nbuf):
            xin = [alloc((128, 1024), F32) for _ in range(nbuf)]
            tmp = [alloc((128, 1024), F32) for _ in range(nbuf)]
            S_xin, S_tmp = [Slot() for _ in range(nbuf)], [Slot() for _ in range(nbuf)]
            for j in range(32):
                k = j % nbuf
                P.dma("sp", sems[k], xin[k], scr[j], reads=[S_scr[j]], writes=[S_xin[k]])
                P.op("dve", lambda e, k=k: e.tensor_tensor(out=tmp[k], in0=xin[k], in1=rstdT[ti][:], op=ALU.mult),
                     reads=[S_xin[k], S_rstdT[ti]], writes=[S_tmp[k]])
                P.op("act", lambda e, k=k, j=j: e.activation(out=hT[:, j, :], in_=tmp[k], func=AF.Identity,
                                                              scale=gm[:, l, v, j:j + 1],
                                                              bias=modt[:, l, j, v:v + 1]),
                     reads=[S_tmp[k]], writes=[S_hT])
                yield

        def phase_norm(scr, S_scr, l, v, ti):
            AR.reset()
            for _ in gen_norm(scr, S_scr, l, v, ti, AR.alloc, s_a[0:4], 4):
                pass
            P.barrier()

        def phase_outproj(W, scr_in, S_in, scr_out, S_out, l, v, ti, side=None):
            AR.reset()
            xin = [AR.alloc((128, 1024), F32) for _ in range(2)]
            xo = [AR.alloc((128, 1024), F32) for _ in range(2)]
            sq = [AR.alloc((128, 1024), BF16) for _ in range(2)]
            S_xin, S_xo, S_sq = [Slot(), Slot()], [Slot(), Slot()], [Slot(), Slot()]
            side = side() if side is not None else iter(())
            pending = []
            jobs = []
            for j in range(32):
                def consume(pair, j=j):
                    k = j % 2
                    P.dma("sp", s_a[k], xin[k], scr_in[j], reads=[S_in[j]], writes=[S_xin[k]])
                    for hf in range(2):
                        P.op("dve", lambda e, hf=hf, k=k: e.scalar_tensor_tensor(
                            out=xo[k][:, 512 * hf:512 * hf + 512], in0=ps[pair + hf][:, :],
                            scalar=modt[:, l, 64 + j, v:v + 1], in1=xin[k][:, 512 * hf:512 * hf + 512],
                            op0=ALU.mult, op1=ALU.add),
                            reads=[S_ps[pair + hf], S_xin[k]], writes=[S_xo[k]])
                    P.dma("sp", s_a[2 + k], scr_out[j], xo[k], reads=[S_xo[k]], writes=[S_out[j]])
                    P.op("act", lambda e, k=k: e.activation(out=sq[k], in_=xo[k], func=AF.Square),
                         reads=[S_xo[k]], writes=[S_sq[k]])

                    def ssq(e, k=k):
                        for hf in range(2):
                            ins = e.matmul(ps[6 + hf][:, :], lhsT=onesb[:], rhs=sq[k][:, 512 * hf:512 * hf + 512],
                                           start=(j == 0), stop=(j == 31))
                        return ins
                    while pending:
                        pending.pop(0)()
                    pending.append(lambda k=k, ssq=ssq: P.op("pe", ssq, reads=[S_sq[k]], writes=[S_ps[6], S_ps[7]]))
                    next(side, None)
                jobs.append(dict(w=W[j], nk=32, act=og[:], rd=[S_og], consume=consume))
            gemm_stream(jobs)
            while pending:
                pending.pop(0)()
            for _ in side:
                pass
            for hf in range(2):
                rstd_from(ps[6 + hf][:, :], rstdT[ti][:, 512 * hf:512 * hf + 512], [S_ps[6 + hf]], ti)
            P.barrier()

        hT_f_all = hT[:].rearrange("p a b -> p (a b)").bitcast(F32)
        HAR = Arena(hT_f_all, 16384)

        def gen_final(scr, S_scr, yout, ti, sems):
            HAR.reset()
            xin = [HAR.alloc((128, 1024), F32) for _ in range(4)]
            tmp = [HAR.alloc((128, 1024), F32) for _ in range(4)]
            yst = [HAR.alloc((128, 8, 512), F32) for _ in range(2)]
            S_xin, S_tmp, S_y = [Slot() for _ in range(4)], [Slot() for _ in range(4)], [Slot(), Slot()]
            yv = yout.rearrange("(b p) d -> p b d", p=128)
            def stage_a(j):
                k = j % 4
                P.dma("sp", sems[k], xin[k], scr[j], reads=[S_scr[j]], writes=[S_xin[k]])
                P.op("dve", lambda e: e.tensor_tensor(out=xin[k], in0=xin[k], in1=rstdT[ti][:], op=ALU.mult),
                     reads=[S_xin[k], S_rstdT[ti]], writes=[S_xin[k]])
                P.op("act", lambda e: e.activation(out=tmp[k], in_=xin[k], func=AF.Identity, scale=fg[:, j:j + 1]),
                     reads=[S_xin[k]], writes=[S_tmp[k]])

            def stage_b(j):
                k = j % 4
                q4, jj4 = divmod(j, 4)
                ky = q4 % 2
                for g in range(2):
                    bank = 4 + g

                    def tr(e, g=g, bank=bank):
                        for jj in range(4):
                            b = 4 * g + jj
                            ins = e.transpose(out=ps[bank][:, 128 * jj:128 * jj + 128],
                                              in_=tmp[k][:, 128 * b:128 * b + 128], identity=identf[:])
                        return ins
                    P.op("pe", tr, reads=[S_tmp[k]], writes=[S_ps[bank]])
                    cp("act" if g == 0 else "dve", yst[ky][:, 4 * g:4 * g + 4, 128 * jj4:128 * jj4 + 128],
                       ps[bank][:].rearrange("p (a b) -> p a b", a=4), [S_ps[bank]], [S_y[ky]])
                if jj4 == 3:
                    P.dma("sp", sems[4 + ky], yv[:, :, 512 * q4:512 * q4 + 512], yst[ky], reads=[S_y[ky]])

            for i in range(34):
                if i < 32:
                    stage_a(i)
                if i >= 2:
                    stage_b(i - 2)
                yield

        def phase_final(scr, S_scr, yout, ti):
            for _ in gen_final(scr, S_scr, yout, ti, s_b):
                pass
            P.barrier()

        def phase_pool(is_sample):
            AR.reset()
            nseq, L = (1, 1024) if is_sample else (4, 256)
            LP = L + 16
            pl = AR.alloc((128, 8, 1024), BF16)
            up = AR.alloc((128, nseq, LP), F32)
            ba = AR.alloc((128, nseq, LP), F32)
            bb = AR.alloc((128, nseq, LP), F32)
            rc = AR.alloc((128, 1024), F32)
            sgt = [AR.alloc((128, 1024), BF16) for _ in range(2)]
            S_pl, S_up, S_ba, S_bb, S_rc = Slot(), Slot(), Slot(), Slot(), Slot()
            S_sgt = [Slot(), Slot()]
            rc_d = rcS_d if is_sample else rcP_d
            for t in (up, ba, bb):
                P.op("dve", lambda e, t=t: e.memset(t, 0.0), writes=[S_up, S_ba, S_bb])
            rc3 = rc.rearrange("p (s l) -> p s l", s=nseq)
            TT = lambda o, a, b, op: (lambda e: e.tensor_tensor(out=o, in0=a, in1=b, op=op))
            jobs = []
            for gi in range(4):
                for c in range(8):
                    j = 8 * gi + c

                    def consume_u(pair, gi=gi, c=c):
                        if c == 0:
                            P.dma("sp", s_a[4], rc, rc_d[gi, :].partition_broadcast(128), writes=[S_rc])
                        for hf in range(2):
                            if is_sample:
                                o = up[:, 0, 8 + 512 * hf:8 + 512 * hf + 512]
                                i_ = ps[pair + hf][:, :]
                            else:
                                o = up[:, 2 * hf:2 * hf + 2, 8:8 + L]
                                i_ = ps[pair + hf][:].rearrange("p (s l) -> p s l", s=2)
                            cp("act", o, i_, [S_ps[pair + hf]], [S_up])
                        P.op("dve", TT(ba[:, :, 1:LP], up[:, :, 1:LP], up[:, :, 0:LP - 1], ALU.add),
                             reads=[S_up], writes=[S_ba])
                        fin, S_fin, oth, S_oth = ba, S_ba, bb, S_bb
                        if gi >= 1:
                            P.op("dve", TT(bb[:, :, 2:LP - 1], ba[:, :, 1:LP - 2], ba[:, :, 3:LP], ALU.add),
                                 reads=[S_ba], writes=[S_bb])
                            fin, S_fin, oth, S_oth = bb, S_bb, ba, S_ba
                        if gi >= 2:
                            P.op("dve", TT(ba[:, :, 4:LP - 3], bb[:, :, 2:LP - 5], bb[:, :, 6:LP - 1], ALU.add),
                                 reads=[S_bb], writes=[S_ba])
                            fin, S_fin, oth, S_oth = ba, S_ba, bb, S_bb
                        if gi >= 3:
                            P.op("dve", TT(bb[:, :, 8:LP - 8], ba[:, :, 4:LP - 12], ba[:, :, 12:LP - 4], ALU.add),
                                 reads=[S_ba], writes=[S_bb])
                            fin, S_fin, oth, S_oth = bb, S_bb, ba, S_ba
                        P.op("dve", TT(oth[:, :, 8:8 + L], fin[:, :, 8:8 + L], rc3, ALU.mult),
                             reads=[S_fin, S_rc], writes=[S_oth])
                        P.op("dve", TT(pl[:, c, :].rearrange("p (s l) -> p s l", s=nseq), oth[:, :, 8:8 + L],
                                       up[:, :, 8:8 + L], ALU.subtract),
                             reads=[S_oth, S_up], writes=[S_pl])
                    jobs.append(dict(w=w_in_pool_d[j], nk=32, act=hT[:], rd=[S_hT], consume=consume_u))
                for c in range(8):
                    j = 8 * gi + c

                    def consume_g(pair, j=j):
                        k = j % 2
                        for hf in range(2):
                            P.op("act", lambda e, hf=hf, k=k: e.activation(
                                out=sgt[k][:, 512 * hf:512 * hf + 512], in_=ps[pair + hf][:, :], func=AF.Silu),
                                reads=[S_ps[pair + hf]], writes=[S_sgt[k]])

                    def consume_y(pair, j=j):
                        k = j % 2
                        for hf in range(2):
                            P.op("dve", lambda e, hf=hf, k=k: e.scalar_tensor_tensor(
                                out=og[:, j, 512 * hf:512 * hf + 512], in0=ps[pair + hf][:, :],
                                scalar=psc[:, j:j + 1], in1=sgt[k][:, 512 * hf:512 * hf + 512],
                                op0=ALU.mult, op1=ALU.mult),
                                reads=[S_ps[pair + hf], S_sgt[k]], writes=[S_og])
                    jobs.append(dict(w=w_in_pool_d[32 + j], nk=32, act=hT[:], rd=[S_hT], consume=consume_g))
                    jobs.append(dict(w=w_grp_d[j], nk=8, act=pl, rd=[S_pl], consume=consume_y))
            gemm_stream(jobs)
            P.barrier()

        def phase_attn(is_sample):
            AR.reset()
            qT = AR.alloc((128, 1024), BF16)
            kT = AR.alloc((128, 1024), BF16)
            vTf = AR.alloc((128, 1024), F32)
            vE = AR.alloc((128, 8, 128), BF16)
            sg = AR.alloc((128, 1024), BF16)
            PT = [AR.alloc((128, 512), BF16) for _ in range(4)]
            n1 = AR.alloc((128, 512), F32)
            n2 = AR.alloc((128, 512), F32)
            S_qT, S_kT, S_vTf, S_vE, S_sg, S_n, S_n2 = Slot(), Slot(), Slot(), Slot(), Slot(), Slot(), Slot()
            S_PT = [Slot() for _ in range(4)]
            if is_sample:
                vO = AR.alloc((128, 7, 128), BF16)
                cKf = AR.alloc((128, 4, 128), F32)
                cKT = AR.alloc((128, 512), BF16)
                cV = AR.alloc((128, 4, 128), BF16)
                bt = AR.alloc((128, 14, 64), BF16)
                amk = AR.alloc((128, 256), BF16)
                S_vO, S_cKf, S_cKT, S_cV, S_bt = Slot(), Slot(), Slot(), Slot(), Slot()
                cosT = og_f[:, 12288:13312]
                sinT = og_f[:, 13312:14336]
                qf = og_f[:, 14336:14848]
                rt = og_f[:, 14848:15360]
                permf = og_f[:, 15360:15488]
                amf = og_f[:, 15488:15744]
                S_qf, S_rt = Slot(), Slot()
                P.dma("sp", s_c, cosT, ropec_d[:, :], writes=[S_c])
                P.dma("sp", s_c, sinT, ropes_d[:, :], writes=[S_c])
                P.dma("sp", s_c, permf, permT_d[:, :], writes=[S_c])
                P.dma("sp", s_c, amf, amask_d[:, :], writes=[S_c])
                P.op("dve", lambda e: e.tensor_copy(out=amk, in_=amf), reads=[S_c], writes=[S_c])
                P.barrier()
            else:
                kf = AR.alloc((128, 1024), F32)
                kst = AR.alloc((128, 8, 128), F32)
                vst = AR.alloc((128, 8, 128), F32)
                S_kf, S_kst, S_vst = Slot(), Slot(), Slot()

            def evac2(pair, dst, S_dst, scale=None, eng="act", excl=False):
                for hf in range(2):
                    cp(eng, dst[:, 512 * hf:512 * hf + 512], ps[pair + hf][:, :], [S_ps[pair + hf]],
                       [S_dst] + ([S_ps[pair + hf]] if excl else []), scale)

            def rope_evac(pair, dst, S_dst, scale):
                for hf in range(2):
                    cp("act", qf, ps[pair + hf][:, :], [S_ps[pair + hf]], [S_qf], scale)
                    P.op("pe", lambda e: e.matmul(ps[4][:, :], lhsT=permf, rhs=qf, start=True, stop=True),
                         reads=[S_qf], writes=[S_ps[4]])
                    P.op("dve", lambda e, hf=hf: e.tensor_tensor(out=rt, in0=qf, in1=cosT[:, 512 * hf:512 * hf + 512],
                                                                  op=ALU.mult), reads=[S_qf], writes=[S_rt])
                    P.op("dve", lambda e, hf=hf: e.tensor_tensor(out=qf, in0=ps[4][:, :],
                                                                  in1=sinT[:, 512 * hf:512 * hf + 512], op=ALU.mult),
                         reads=[S_ps[4]], writes=[S_qf])
                    P.op("dve", lambda e, hf=hf: e.tensor_tensor(out=dst[:, 512 * hf:512 * hf + 512], in0=rt, in1=qf,
                                                                  op=ALU.add), reads=[S_rt, S_qf], writes=[S_dst])

            def transposes(srcs, bank, rd):
                def tr(e):
                    for i, s_ in enumerate(srcs):
                        ins = e.transpose(out=ps[bank][:, 128 * i:128 * i + 128], in_=s_, identity=identf[:])
                    return ins
                P.op("pe", tr, reads=rd, writes=[S_ps[bank]])

            def v_transposes(out_d=None, col0=0):
                for g in range(2):
                    bank = 5 - g
                    transposes([vTf[:, 128 * (4 * g + i):128 * (4 * g + i) + 128] for i in range(4)], bank, [S_vTf])
                    pv = ps[bank][:].rearrange("p (a b) -> p a b", a=4)
                    if out_d is None:
                        cp("dve", vE[:, 4 * g:4 * g + 4, :], pv, [S_ps[bank]], [S_vE])
                    else:
                        cp("dve", vE[:, 4 * g:4 * g + 4, :], pv, [S_ps[bank]], [S_vE, S_ps[bank]])
                        cp("act", vst[:, 4 * g:4 * g + 4, :], pv, [S_ps[bank]], [S_vst, S_ps[bank]])
                if out_d is not None:
                    P.dma("sp", s_a[5], out_d.rearrange("(b p) d -> p b d", p=128)[:, :, col0:col0 + 128], vst,
                          reads=[S_vst])

            def k_out(out_d, col0):
                for g in range(2):
                    bank = 5 - g
                    transposes([kf[:, 128 * (4 * g + i):128 * (4 * g + i) + 128] for i in range(4)], bank, [S_kf])
                    cp("act", kst[:, 4 * g:4 * g + 4, :], ps[bank][:].rearrange("p (a b) -> p a b", a=4),
                       [S_ps[bank]], [S_kst])
                P.dma("sp", s_a[6], out_d.rearrange("(b p) d -> p b d", p=128)[:, :, col0:col0 + 128], kst,
                      reads=[S_kst])

            def vO_transposes():
                for g in range(2):
                    bank = 5 - g
                    n = 4 if g == 0 else 3
                    transposes([vTf[:, 64 + 128 * (4 * g + i):64 + 128 * (4 * g + i) + 128] for i in range(n)],
                               bank, [S_vTf])
                    cp("dve", vO[:, 4 * g:4 * g + n, :],
                       ps[bank][:, 0:128 * n].rearrange("p (a b) -> p a b", a=n), [S_ps[bank]], [S_vO])

            def load_cache(kd, vd, col0):
                P.dma("sp", s_a[0], cKf, kd.rearrange("(b p) d -> p b d", p=128)[:, :, col0:col0 + 128],
                      writes=[S_cKf])
                P.dma("pool", s_p[0], cV, vd.rearrange("(b p) d -> p b d", p=128)[:, :, col0:col0 + 128],
                      writes=[S_cV])
                transposes([cKf[:, b, :] for b in range(4)], 5, [S_cKf])
                cp("act", cKT, ps[5][:, :], [S_ps[5]], [S_cKT])

            def normalize(hf, chunk, sink_h):
                if sink_h is not None:
                    P.op("dve", lambda e: e.tensor_scalar(out=n1, in0=ps[7][:, :], scalar1=es[:, sink_h:sink_h + 1],
                                                          scalar2=None, op0=ALU.add), reads=[S_ps[7]], writes=[S_n])
                    P.op("act", lambda e: e.activation(out=n2, in_=ps[6][:, :], func=AF.Copy), reads=[S_ps[6]],
                         writes=[S_n2])
                    P.op("dve", lambda e: e.reciprocal(out=n1, in_=n1), reads=[S_n], writes=[S_n])
                else:
                    P.op("dve", lambda e: e.reciprocal(out=n1, in_=ps[7][:, :]), reads=[S_ps[7]], writes=[S_n])
                    P.op("act", lambda e: e.activation(out=n2, in_=ps[6][:, :], func=AF.Copy), reads=[S_ps[6]],
                         writes=[S_n2])
                P.op("dve", lambda e: e.tensor_tensor(out=n2, in0=n2, in1=n1, op=ALU.mult),
                     reads=[S_n], writes=[S_n2])
                P.op("dve", lambda e: e.tensor_tensor(out=og[:, chunk, 512 * hf:512 * hf + 512], in0=n2,
                                                      in1=sg[:, 512 * hf:512 * hf + 512], op=ALU.mult),
                     reads=[S_n2, S_sg], writes=[S_og])

            def pv_part(blocks_v, pt, S_pt, ocol, w, extra_rd, first=True, last=True):
                nb = len(blocks_v)

                def f(e):
                    for i, (vap, c0) in enumerate(blocks_v):
                        e.matmul(ps[6][:, ocol:ocol + w], lhsT=vap, rhs=pt[:, c0:c0 + w],
                                 start=(first and i == 0), stop=(last and i == nb - 1))
                    for i, (vap, c0) in enumerate(blocks_v):
                        ins = e.matmul(ps[7][:, ocol:ocol + w], lhsT=onesb[:], rhs=pt[:, c0:c0 + w],
                                       start=(first and i == 0), stop=(last and i == nb - 1))
                    return ins
                P.op("pe", f, reads=[S_pt] + extra_rd, writes=[S_ps[6], S_ps[7]])

            def run_stages(stages):
                n = len(stages)
                stages[0]["sf"]()
                for i in range(n):
                    stages[i]["ex"]()
                    if i + 1 < n:
                        stages[i + 1]["sf"]()
                    stages[i]["pv"]()
                    if stages[i].get("post"):
                        stages[i]["post"]()

            def attn_A_lat(h):
                stages = []
                for n in range(8):
                    blocks = []
                    if n > 0:
                        blocks.append((kT[:, 128 * (n - 1):128 * n], vE[:, n - 1, :], amk[:, 0:128]))
                    blocks.append((kT[:, 128 * n:128 * n + 128], vE[:, n, :], None))
                    if n < 7:
                        blocks.append((kT[:, 128 * (n + 1):128 * (n + 2)], vE[:, n + 1, :], amk[:, 128:256]))
                    for cbk_ in range(4):
                        blocks.append((cKT[:, 128 * cbk_:128 * cbk_ + 128], cV[:, cbk_, :], None))
                    for sidx in range(2):
                        bl = blocks[4 * sidx:4 * sidx + 4]
                        bank = 4 + sidx
                        pi = 2 * (n % 2) + sidx
                        wS = 128 * len(bl)

                        def sfn(bl=bl, n=n, bank=bank):
                            def sf(e):
                                for idx, (kap, vap, m) in enumerate(bl):
                                    o = ps[bank][:, 128 * idx:128 * idx + 128]
                                    ins = e.matmul(o, lhsT=kap, rhs=qT[:, 128 * n:128 * n + 128], start=True,
                                                   stop=(m is None))
                                    if m is not None:
                                        ins = e.matmul(o, lhsT=identb[:], rhs=m, start=False, stop=True)
                                return ins
                            P.op("pe", sf, reads=[S_kT, S_qT, S_cKT], writes=[S_ps[bank]])

                        def exn(bank=bank, pi=pi, wS=wS):
                            P.op("act", lambda e: e.activation(out=PT[pi][:, 0:wS], in_=ps[bank][:, 0:wS], func=AF.Exp),
                                 reads=[S_ps[bank]], writes=[S_PT[pi]])

                        def pvn(bl=bl, pi=pi, n=n, sidx=sidx):
                            pv_part([(vap, 128 * idx) for idx, (kap, vap, m) in enumerate(bl)], PT[pi], S_PT[pi],
                                    128 * (n % 4), 128, [S_vE, S_cV], first=(sidx == 0), last=(sidx == 1))
                        post = None
                        if sidx == 1 and n % 4 == 3:
                            post = (lambda n=n: normalize(n // 4, h, h))
                        stages.append(dict(sf=sfn, ex=exn, pv=pvn, post=post))
                run_stages(stages)

            def attn_B_lat(h):
                stages = []
                for r in range(16):
                    rs = min(max(r - 4, 0), 8)
                    sbk = 4 + r % 2
                    pk = r % 2

                    def sfn(r=r, rs=rs, sbk=sbk):
                        def sf(e):
                            for b in range(4):
                                o = ps[sbk][:, 64 * b:64 * b + 64]
                                t0 = 64 * (rs + 2 * b)
                                e.matmul(o, lhsT=kT[:, t0:t0 + 128], rhs=qT[:, 64 * r:64 * r + 64], start=True,
                                         stop=False)
                                e.matmul(o, lhsT=identb[:], rhs=bt[:, rs + 2 * b - r + 7, :], start=False, stop=True)
                            for c_ in range(4):
                                ins = e.matmul(ps[sbk][:, 256 + 64 * c_:256 + 64 * c_ + 64],
                                               lhsT=cKT[:, 128 * c_:128 * c_ + 128], rhs=qT[:, 64 * r:64 * r + 64],
                                               start=True, stop=True)
                            return ins
                        P.op("pe", sf, reads=[S_kT, S_qT, S_cKT, S_bt], writes=[S_ps[sbk]])

                    def exn(pk=pk, sbk=sbk):
                        P.op("act", lambda e: e.activation(out=PT[pk][:, 0:512], in_=ps[sbk][:, :], func=AF.Exp),
                             reads=[S_ps[sbk]], writes=[S_PT[pk]])

                    def pvn(r=r, rs=rs, pk=pk):
                        bl = []
                        for b in range(4):
                            vap = vE[:, (rs + 2 * b) // 2, :] if rs % 2 == 0 else vO[:, (rs - 1) // 2 + b, :]
                            bl.append((vap, 64 * b))
                        for c_ in range(4):
                            bl.append((cV[:, c_, :], 256 + 64 * c_))
                        pv_part(bl, PT[pk], S_PT[pk], 64 * (r % 8), 64, [S_vE, S_vO, S_cV])
                    post = (lambda r=r: normalize(r // 8, 16 + h, None)) if r % 8 == 7 else None
                    stages.append(dict(sf=sfn, ex=exn, pv=pvn, post=post))
                run_stages(stages)

            def attn_ctx(chunk, sink_h):
                stages = []
                for s_ in range(4):
                    sbk = 4 + s_ % 2
                    pk = s_ % 2

                    def sfn(s_=s_, sbk=sbk):
                        def sf(e):
                            for b in range(2):
                                ins = e.matmul(ps[sbk][:, 256 * b:256 * b + 256],
                                               lhsT=kT[:, 256 * s_ + 128 * b:256 * s_ + 128 * b + 128],
                                               rhs=qT[:, 256 * s_:256 * s_ + 256], start=True, stop=True)
                            return ins
                        P.op("pe", sf, reads=[S_kT, S_qT], writes=[S_ps[sbk]])

                    def exn(pk=pk, sbk=sbk):
                        P.op("act", lambda e: e.activation(out=PT[pk][:, 0:512], in_=ps[sbk][:, :], func=AF.Exp),
                             reads=[S_ps[sbk]], writes=[S_PT[pk]])

                    def pvn(s_=s_, pk=pk):
                        pv_part([(vE[:, 2 * s_ + b, :], 256 * b) for b in range(2)], PT[pk], S_PT[pk],
                                256 * (s_ % 2), 256, [S_vE])
                    post = (lambda s_=s_: normalize(s_ // 2, chunk, sink_h)) if s_ % 2 == 1 else None
                    stages.append(dict(sf=sfn, ex=exn, pv=pvn, post=post))
                run_stages(stages)

            W = w_in_attn_d
            jobs = []

            def J(col, consume):
                jobs.append(dict(w=W[col // 128], nk=32, act=hT[:], rd=[S_hT], consume=consume))

            for g in range(4):
                if is_sample:
                    def ck(pair, g=g):
                        load_cache(cak_d, cav_d, 128 * g)
                        rope_evac(pair, kT, S_kT, None)

                    def cv_(pair):
                        evac2(pair, vTf, S_vTf)
                        v_transposes()
                else:
                    def ck(pair, g=g):
                        evac2(pair, kT, S_kT, excl=True)
                        evac2(pair, kf, S_kf, eng="dve", excl=True)
                        k_out(nak_d, 128 * g)

                    def cv_(pair, g=g):
                        evac2(pair, vTf, S_vTf)
                        v_transposes(nav_d, 128 * g)
                J(2048 + 128 * g, ck)
                J(2560 + 128 * g, cv_)
                for hq in range(4):
                    h = 4 * g + hq
                    if is_sample:
                        def cq(pair):
                            rope_evac(pair, qT, S_qT, SCALE)
                    else:
                        def cq(pair):
                            evac2(pair, qT, S_qT, SCALE)

                    def cg(pair, h=h):
                        for hf in range(2):
                            P.op("act", lambda e, hf=hf: e.activation(out=sg[:, 512 * hf:512 * hf + 512],
                                                                       in_=ps[pair + hf][:, :], func=AF.Silu),
                                 reads=[S_ps[pair + hf]], writes=[S_sg])
                        if is_sample:
                            attn_A_lat(h)
                        else:
                            attn_ctx(h, h)
                    J(128 * h, cq)
                    J(9216 + 128 * h, cg)
            first_b = [True]
            for h in range(16):
                def ck(pair, h=h):
                    if is_sample:
                        if first_b[0]:
                            first_b[0] = False
                            P.barrier()
                        load_cache(cbk_d, cbv_d, 128 * h)
                        P.dma("pool", s_p[1], bt, TB_d[h].rearrange("p (a b) -> p a b", a=14), writes=[S_bt])
                        evac2(pair, kT, S_kT)
                    else:
                        evac2(pair, kT, S_kT, excl=True)
                        evac2(pair, kf, S_kf, eng="dve", excl=True)
                        k_out(nbk_d, 128 * h)

                def cv_(pair, h=h):
                    evac2(pair, vTf, S_vTf)
                    if is_sample:
                        v_transposes()
                        vO_transposes()
                    else:
                        v_transposes(nbv_d, 128 * h)

                def cq(pair):
                    evac2(pair, qT, S_qT, SCALE)

                def cg(pair, h=h):
                    for hf in range(2):
                        P.op("act", lambda e, hf=hf: e.activation(out=sg[:, 512 * hf:512 * hf + 512],
                                                                   in_=ps[pair + hf][:, :], func=AF.Silu),
                             reads=[S_ps[pair + hf]], writes=[S_sg])
                    if is_sample:
                        attn_B_lat(h)
                    else:
                        attn_ctx(16 + h, None)
                J(5120 + 128 * h, ck)
                J(7168 + 128 * h, cv_)
                J(3072 + 128 * h, cq)
                J(9216 + 2048 + 128 * h, cg)
            if is_sample:
                mixed = []
                for i, jb in enumerate(jobs):
                    mixed.append(jb)
                    if i < 96:
                        def cada(pair, j=i):
                            P.op("dve", lambda e: e.tensor_scalar(out=modt[:, 1, j, :], in0=ps[pair][:, 0:2],
                                                                  scalar1=bTt[:, 1, j:j + 1], scalar2=None,
                                                                  op0=ALU.add),
                                 reads=[S_ps[pair]], writes=[S_c])
                        mixed.append(dict(w=w_ada1_d[i], nk=32, act=None, rd=[S_c],
                                          consume=cada, gemv=True))
                jobs = mixed
            gemm_stream(jobs)
            if is_sample:
                gm_finish(1)
            P.barrier()

        def S_out1_P_norm0():
            phase_outproj(w_out_pool_d, scrB, S_scrB, scrA, S_scrA, 1, 1, 0,
                          side=lambda: gen_norm(scrC, S_scrC, 0, 0, 1, AR.alloc, s_b[0:2], 2))

        def P_out0_S_final():
            phase_outproj(w_out_attn_d, scrC, S_scrC, scrD, S_scrD, 0, 0, 1,
                          side=lambda: gen_final(scrA, S_scrA, ys_d, 0, s_b))

        steps = [
            ("ada", phase_ada_input),
            ("Snorm0", lambda: phase_norm(scrA, S_scrA, 0, 1, 0)),
            ("Sattn", lambda: phase_attn(True)),
            ("Sout0", lambda: phase_outproj(w_out_attn_d, scrA, S_scrA, scrB, S_scrB, 0, 1, 0)),
            ("Snorm1", lambda: phase_norm(scrB, S_scrB, 1, 1, 0)),
            ("Spool", lambda: phase_pool(True)),
            ("Sout1", S_out1_P_norm0),
            ("Pattn", lambda: phase_attn(False)),
            ("Pout0", P_out0_S_final),
            ("Pnorm1", lambda: phase_norm(scrD, S_scrD, 1, 0, 1)),
            ("Ppool", lambda: phase_pool(False)),
            ("Pout1", lambda: phase_outproj(w_out_pool_d, scrD, S_scrD, scrC, S_scrC, 1, 0, 1)),
            ("Pfinal", lambda: phase_final(scrC, S_scrC, yp_d, 1)),
        ]
        for name, fn in steps:
            fn()
            if stop == name:
                break
        P.barrier()
        P.cnt_report = dict(P.cnt)
        print("sem counts", P.cnt)
        P.run()
    return nc


def _consts():
    f32 = np.float32
    ident = np.eye(128, dtype=f32)
    permT = np.zeros((128, 128), f32)
    sign = np.zeros(128, f32)
    for m in range(128):
        a, rem = divmod(m, 64)
        if rem < 32:
            src, sg_ = a * 64 + 32 + rem, -1.0
        else:
            src, sg_ = a * 64 + rem - 32, 1.0
        permT[src, m] = 1.0
        sign[m] = sg_
    t = np.arange(1024)
    inv_freq = (f32(10000.0) ** (-np.arange(32, dtype=f32) / f32(32))).astype(f32)
    ang_r = (t // 64).astype(f32)[:, None] * inv_freq
    ang_c = (t % 64).astype(f32)[:, None] * inv_freq
    ang = np.concatenate([ang_r, ang_r, ang_c, ang_c], axis=-1).astype(f32)
    ropec = np.ascontiguousarray(np.cos(ang).T.astype(f32))
    ropes = np.ascontiguousarray((np.sin(ang).T * sign[:, None]).astype(f32))
    jj = np.arange(128)[:, None]
    qq = np.arange(128)[None, :]
    amask = np.concatenate([np.where(qq <= jj, 0.0, NEG), np.where(jj <= qq, 0.0, NEG)], axis=1).astype(f32)

    def rc(L, reps):
        out = np.zeros((4, L), f32)
        tt = np.arange(L)
        for gi, half in enumerate((1, 2, 4, 8)):
            lo = np.clip(tt - half, 0, L)
            hi = np.clip(tt + half, 0, L)
            out[gi] = (f32(1.0) / (hi - lo).astype(f32)).astype(f32)
        return np.ascontiguousarray(np.tile(out, (1, reps)))
    return dict(ident=ident, permT=permT, ropec=ropec, ropes=ropes, amask=amask, rcS=rc(1024, 1), rcP=rc(256, 4))


def _bias_table(rpb):
    c = np.arange(64)
    kc = np.arange(64)
    col_start = np.clip(c - 8, 0, 48)
    ok = (kc[:, None] >= col_start[None, :]) & (kc[:, None] < col_start[None, :] + 16)
    dc = np.clip(kc[:, None] - c[None, :] + 15, 0, 30)
    TB = np.empty((16, 2, 64, 14, 64), np.float32)
    for a in range(2):
        for dr0 in range(14):
            g = rpb[:, dr0 + a][:, dc]
            TB[:, a, :, dr0, :] = np.where(ok[None], g, np.float32(NEG))
    return np.ascontiguousarray(TB.reshape(16, 128, 14 * 64))


_NC_CACHE = {}
_NCORES = [8]
_PREP_ONLY = [False]


def kernel(x_prompt, x_sample, c, cache_a_k, cache_a_v, cache_b_k, cache_b_v, c_ctx,
           w_ada, b_ada, norm_g, w_in_attn, a_sink, b_rpb, w_out_attn,
           w_in_pool, w_grp_pool, pool_scale, w_out_pool, final_g):
    f = lambda a: np.ascontiguousarray(np.asarray(a, dtype=np.float32))
    x_prompt, x_sample, c, c_ctx = f(x_prompt), f(x_sample), f(c), f(c_ctx)
    cache_a_k, cache_a_v, cache_b_k, cache_b_v = f(cache_a_k), f(cache_a_v), f(cache_b_k), f(cache_b_v)
    w_ada, b_ada, norm_g = f(w_ada), f(b_ada), f(norm_g)
    consts = _consts()
    def tile_w(w2d):
        K, N = w2d.shape
        return np.ascontiguousarray(w2d.reshape(K // 128, 128, N // 128, 128).transpose(2, 1, 0, 3)).reshape(
            N // 128, 128, (K // 128) * 128)
    wg = f(w_grp_pool).reshape(4, 1024, 1024)
    shared = dict(
        w_ada0=np.ascontiguousarray(w_ada[0]), w_ada1=tile_w(w_ada[1]),
        bT=f(b_ada.reshape(2, 96, 128).transpose(2, 0, 1).reshape(128, 192)),
        gT=f(norm_g.reshape(2, 32, 128).transpose(2, 0, 1).reshape(128, 64)),
        fgT=f(np.asarray(final_g, np.float32).reshape(32, 128).T),
        pscT=f(np.asarray(pool_scale, np.float32).reshape(32, 128).T),
        w_in_attn=tile_w(f(w_in_attn).reshape(4096, 13312)),
        w_out_attn=tile_w(f(w_out_attn).reshape(4096, 4096)),
        w_in_pool=tile_w(f(w_in_pool).reshape(4096, 8192)),
        w_grp=np.concatenate([tile_w(wg[g]) for g in range(4)], axis=0),
        w_out_pool=tile_w(f(w_out_pool).reshape(4096, 4096)),
        sinkb=f(np.broadcast_to(np.asarray(a_sink, np.float32).reshape(1, 16), (128, 16))),
        TB=_bias_table(np.asarray(b_rpb, np.float32)[0]),
        **consts,
    )
    in_maps = []
    for i in range(_NCORES[0]):
        cond = np.stack([c_ctx, c[i]], axis=0)
        cond2 = f(cond.reshape(2, 32, 128).transpose(2, 1, 0).reshape(128, 64))
        m = dict(shared)
        m.update(
            xs=f(x_sample[i]), xp=f(x_prompt[4 * i:4 * i + 4].reshape(1024, 4096)), cond2=cond2,
            cak=f(cache_a_k[i, 0].reshape(512, 512)), cav=f(cache_a_v[i, 0].reshape(512, 512)),
            cbk=f(cache_b_k[i, 0].reshape(512, 2048)), cbv=f(cache_b_v[i, 0].reshape(512, 2048)),
        )
        in_maps.append(m)
    if _PREP_ONLY[0]:
        return in_maps
    if "nc" not in _NC_CACHE:
        _NC_CACHE["nc"] = build_nc()
    res = run_bass_kernel_spmd(_NC_CACHE["nc"], in_maps, core_ids=list(range(8)))
    R = res.results
    y_sample = np.stack([R[i]["ys"] for i in range(8)], axis=0)
    y_prompt = np.concatenate([R[i]["yp"].reshape(4, 256, 4096) for i in range(8)], axis=0)
    cat = lambda k, nh: np.concatenate([R[i][k].reshape(4, 1, 256, nh, 128) for i in range(8)], axis=0)
    return (y_prompt, y_sample, cat("nak", 4), cat("nav", 4), cat("nbk", 16), cat("nbv", 16))
```

```python
import numpy as np
from contextlib import ExitStack
import concourse.bass as bass
import concourse.mybir as mybir
from concourse.bass_utils import run_bass_kernel_spmd

F32 = mybir.dt.float32
BF16 = mybir.dt.bfloat16
AF = mybir.ActivationFunctionType
ALU = mybir.AluOpType

ENGS = ("pe", "act", "dve", "pool", "sp")
NEG = -30000.0
SCALE = 128 ** -0.5
EPS = 1e-6
ARENA_W = 9728
NSLAB = 3


class Slot:
    __slots__ = ("w", "r")

    def __init__(self):
        self.w = None
        self.r = {}


class Prog:
    def __init__(self, nc, stack):
        self.nc = nc
        self.stack = stack
        self.q = {e: [] for e in ENGS}
        self.cnt = {}
        self.sem = {}
        self.waited = {e: {} for e in ENGS}
        for e in ENGS:
            if e != "sp":
                self.new_sem("c_" + e)

    def new_sem(self, name):
        self.sem[name] = self.stack.enter_context(self.nc.semaphore(name))
        self.cnt[name] = 0
        return name

    def _wait(self, eng, tok):
        if tok is None:
            return
        sname, val = tok
        if self.waited[eng].get(sname, 0) >= val:
            return
        self.waited[eng][sname] = val
        sem = self.sem[sname]
        self.q[eng].append(lambda e, sem=sem, val=val: e.wait_ge(sem, val))

    def _deps(self, eng, reads, writes, extra):
        best = {}
        toks = list(extra)
        for s in reads:
            toks.append(s.w)
        for s in writes:
            toks.append(s.w)
            toks.extend(s.r.items())
        for t in toks:
            if t is not None and best.get(t[0], 0) < t[1]:
                best[t[0]] = t[1]
        for sname, val in best.items():
            self._wait(eng, (sname, val))

    def _commit(self, tok, reads, writes):
        for s in reads:
            if s.r.get(tok[0], 0) < tok[1]:
                s.r[tok[0]] = tok[1]
        for s in writes:
            s.w = tok
            s.r = {}

    def op(self, eng, fn, reads=(), writes=(), extra=()):
        self._deps(eng, reads, writes, extra)
        sname = "c_" + eng
        self.cnt[sname] += 1
        sem = self.sem[sname]
        self.q[eng].append(lambda e, fn=fn, sem=sem: fn(e).then_inc(sem, 1))
        tok = (sname, self.cnt[sname])
        self._commit(tok, reads, writes)
        return tok

    def dma(self, eng, sname, out, in_, reads=(), writes=(), extra=()):
        self._deps(eng, reads, writes, extra)
        self.cnt[sname] += 16
        sem = self.sem[sname]
        self.q[eng].append(lambda e, out=out, in_=in_, sem=sem: e.dma_start(out=out, in_=in_).then_inc(sem, 16))
        tok = (sname, self.cnt[sname])
        self._commit(tok, reads, writes)
        return tok

    def barrier(self):
        toks = [(s, c) for s, c in self.cnt.items() if c > 0]
        for e in ENGS:
            for t in toks:
                self._wait(e, t)

    def run(self):
        with self.nc.Block() as block:
            @block.tensor
            def _(e):
                for f in self.q["pe"]:
                    f(e)

            @block.scalar
            def _(e):
                for f in self.q["act"]:
                    f(e)

            @block.vector
            def _(e):
                for f in self.q["dve"]:
                    f(e)

            @block.gpsimd
            def _(e):
                for f in self.q["pool"]:
                    f(e)

            @block.sync
            def _(e):
                for f in self.q["sp"]:
                    f(e)


class Arena:
    def __init__(self, t, nwords):
        self.t, self.n, self.off = t, nwords, 0

    def reset(self):
        self.off = 0

    def alloc(self, shape, dt):
        nel = int(np.prod(shape[1:]))
        nw = (nel * (2 if dt == BF16 else 4) + 3) // 4
        nw = (nw + 1) // 2 * 2
        assert self.off + nw <= self.n, ("arena overflow", self.off, nw, self.n)
        v = self.t[:, self.off:self.off + nw]
        self.off += nw
        if dt == BF16:
            v = v.bitcast(BF16)
        v = v[:, 0:nel]
        if len(shape) == 3:
            v = v.rearrange("p (a b) -> p a b", a=shape[1])
        elif len(shape) == 4:
            v = v.rearrange("p (a b c) -> p a b c", a=shape[1], b=shape[2])
        return v


def build_nc(stop=None):
    nc = bass.Bass("TRN2", target_bir_lowering=False)

    def din(name, shape):
        return nc.dram_tensor(name, list(shape), F32, kind="ExternalInput").ap()

    def dout(name, shape):
        return nc.dram_tensor(name, list(shape), F32, kind="ExternalOutput").ap()

    xs_d, xp_d = din("xs", (1024, 4096)), din("xp", (1024, 4096))
    cond2_d = din("cond2", (128, 64))
    w_ada0_d = din("w_ada0", (4096, 12288))
    w_ada1_d = din("w_ada1", (96, 128, 4096))
    bT_d, gT_d = din("bT", (128, 192)), din("gT", (128, 64))
    fgT_d, pscT_d = din("fgT", (128, 32)), din("pscT", (128, 32))
    w_in_attn_d = din("w_in_attn", (104, 128, 4096))
    w_out_attn_d = din("w_out_attn", (32, 128, 4096))
    w_in_pool_d = din("w_in_pool", (64, 128, 4096))
    w_grp_d = din("w_grp", (32, 128, 1024))
    w_out_pool_d = din("w_out_pool", (32, 128, 4096))
    sinkb_d = din("sinkb", (128, 16))
    TB_d = din("TB", (16, 128, 896))
    cak_d, cav_d = din("cak", (512, 512)), din("cav", (512, 512))
    cbk_d, cbv_d = din("cbk", (512, 2048)), din("cbv", (512, 2048))
    ident_d, permT_d = din("ident", (128, 128)), din("permT", (128, 128))
    ropec_d, ropes_d = din("ropec", (128, 1024)), din("ropes", (128, 1024))
    amask_d = din("amask", (128, 256))
    rcS_d, rcP_d = din("rcS", (4, 1024)), din("rcP", (4, 1024))
    ys_d, yp_d = dout("ys", (1024, 4096)), dout("yp", (1024, 4096))
    nak_d, nav_d = dout("nak", (1024, 512)), dout("nav", (1024, 512))
    nbk_d, nbv_d = dout("nbk", (1024, 2048)), dout("nbv", (1024, 2048))
    scrA = nc.dram_tensor("scrA", [32, 128, 1024], F32).ap()
    scrB = nc.dram_tensor("scrB", [32, 128, 1024], F32).ap()
    scrC = nc.dram_tensor("scrC", [32, 128, 1024], F32).ap()
    scrD = nc.dram_tensor("scrD", [32, 128, 1024], F32).ap()

    with ExitStack() as st:
        P = Prog(nc, st)

        def sb(name, shape, dt):
            return st.enter_context(nc.sbuf_tensor(name, list(shape), dt))

        hT = sb("hT", (128, 32, 1024), BF16)
        og = sb("og", (128, 32, 1024), BF16)
        slab = [sb(f"slab{i}", (128, 4096), BF16) for i in range(NSLAB)]
        identf = sb("identf", (128, 128), F32)
        identb = sb("identb", (128, 128), BF16)
        onesb = sb("onesb", (128, 128), BF16)
        modt = sb("modt", (128, 2, 96, 2), F32)
        gm = sb("gm", (128, 2, 2, 32), F32)
        gTt = sb("gTt", (128, 2, 32), F32)
        fg = sb("fg", (128, 32), F32)
        psc = sb("psc", (128, 32), F32)
        bTt = sb("bTt", (128, 2, 96), F32)
        es = sb("es", (128, 16), F32)
        epsT = sb("epsT", (128, 1), F32)
        cb = sb("cb", (128, 32, 2), BF16)
        rstdT = [sb("rstdS", (128, 1024), F32), sb("rstdP", (128, 1024), F32)]
        arena_t = sb("arena", (128, ARENA_W), F32)
        AR = Arena(arena_t, ARENA_W)
        ps = [st.enter_context(nc.psum_tensor(f"ps{i}", [128, 512], F32)) for i in range(8)]
        S_ps = [Slot() for _ in range(8)]
        S_slab = [Slot() for _ in range(NSLAB)]
        s_slab = [P.new_sem(f"s_slab{i}") for i in range(NSLAB)]
        S_hT, S_og, S_c = Slot(), Slot(), Slot()
        S_rstdT = [Slot(), Slot()]
        S_scrA = [Slot() for _ in range(32)]
        S_scrB = [Slot() for _ in range(32)]
        S_scrC = [Slot() for _ in range(32)]
        S_scrD = [Slot() for _ in range(32)]
        CUR = {"ti": 0}
        s_c = P.new_sem("s_c")
        s_a = [P.new_sem(f"s_a{i}") for i in range(8)]
        s_b = [P.new_sem(f"s_b{i}") for i in range(6)]
        s_p = [P.new_sem(f"s_p{i}") for i in range(2)]
        G = {"slab_rr": 0, "gemm_rr": 0}

        og_f = og[:].rearrange("p a b -> p (a b)").bitcast(F32)
        og_b = og[:].rearrange("p a b -> p (a b)")

        def cp(eng, out, in_, reads, writes, scale=None):
            if eng == "act":
                if scale is None:
                    return P.op("act", lambda e: e.activation(out=out, in_=in_, func=AF.Copy), reads, writes)
                return P.op("act", lambda e: e.activation(out=out, in_=in_, func=AF.Copy, scale=scale), reads, writes)
            return P.op("dve", lambda e: e.tensor_copy(out=out, in_=in_), reads, writes)

        P.dma("sp", s_c, identf[:], ident_d[:, :], writes=[S_c])
        P.dma("sp", s_c, gTt[:].rearrange("p a b -> p (a b)"), gT_d[:, :], writes=[S_c])
        P.dma("sp", s_c, fg[:], fgT_d[:, :], writes=[S_c])
        P.dma("sp", s_c, psc[:], pscT_d[:, :], writes=[S_c])
        P.dma("sp", s_c, bTt[:].rearrange("p a b -> p (a b)"), bT_d[:, :], writes=[S_c])
        P.dma("sp", s_c, es[:], sinkb_d[:, :], writes=[S_c])
        P.op("dve", lambda e: e.memset(onesb[:], 1.0), writes=[S_c])
        P.op("dve", lambda e: e.memset(epsT[:], EPS), writes=[S_c])
        P.op("dve", lambda e: e.tensor_copy(out=identb[:], in_=identf[:]), reads=[S_c], writes=[S_c])
        P.op("act", lambda e: e.activation(out=es[:], in_=es[:], func=AF.Exp), reads=[S_c], writes=[S_c])
        P.barrier()

        def phase_ada_input():
            AR.reset()
            cf = AR.alloc((128, 64), F32)
            S_cf, S_cb, S_mrow = Slot(), Slot(), Slot()
            P.dma("sp", s_a[3], cf, cond2_d[:, :], writes=[S_cf])
            P.op("act", lambda e: e.activation(out=cb.rearrange("p a b -> p (a b)"), in_=cf, func=AF.Silu),
                 reads=[S_cf], writes=[S_cb, S_c])
            hT_f = hT[:].rearrange("p a b -> p (a b)").bitcast(F32)
            mrow = hT_f[0:2, 0:12288]
            xblk = [og_f[:, 0:4096], og_f[:, 4096:8192]]
            xst = og_f[:, 8192:12288].rearrange("p (a b) -> p a b", a=32)
            sq = og_b[:, 24576:28672].rearrange("p (a b) -> p a b", a=32)
            S_xb = [Slot(), Slot()]
            S_xg = [Slot() for _ in range(8)]
            S_sq = Slot()

            def input_block(ib):
                ti, b = divmod(ib, 8)
                xsrc = xs_d if ti == 0 else xp_d
                scr, S_scr = (scrA, S_scrA) if ti == 0 else (scrC, S_scrC)
                scr_v = scr.rearrange("j p t -> p j t")
                k = ib % 2
                P.dma("sp", s_a[k], xblk[k], xsrc[128 * b:128 * b + 128, :], writes=[S_xb[k]])
                for g in range(8):
                    bank = 4 + g % 2

                    def tr(e, k=k, g=g, bank=bank):
                        for j in range(4):
                            c = 4 * g + j
                            ins = e.transpose(out=ps[bank][:, 128 * j:128 * j + 128],
                                              in_=xblk[k][:, 128 * c:128 * c + 128], identity=identf[:])
                        return ins
                    P.op("pe", tr, reads=[S_xb[k]], writes=[S_ps[bank]])
                    cp("act" if g % 2 == 0 else "dve", xst[:, 4 * g:4 * g + 4, :],
                       ps[bank][:].rearrange("p (a b) -> p a b", a=4), [S_ps[bank]], [S_xg[g]])
                P.op("act", lambda e: e.activation(out=sq, in_=xst, func=AF.Square), reads=S_xg, writes=[S_sq])
                col = 128 * (b % 4)

                def ssq(e, col=col):
                    for c in range(32):
                        ins = e.matmul(ps[6][:, col:col + 128], lhsT=onesb[:], rhs=sq[:, c, :],
                                       start=(c == 0), stop=(c == 31))
                    return ins
                P.op("pe", ssq, reads=[S_sq], writes=[S_ps[6]])
                rstd_from(ps[6][:, col:col + 128], rstdT[ti][:, 128 * b:128 * b + 128], [S_ps[6]], ti)
                P.dma("sp", s_a[2], scr_v[:, :, 128 * b:128 * b + 128], xst, reads=S_xg, writes=S_scr)

            def mod_finish(l):
                def tr(e):
                    for j in range(96):
                        ins = e.transpose(out=ps[7][:, 2 * j:2 * j + 2], in_=mrow[:, 128 * j:128 * j + 128],
                                          identity=identf[0:2, 0:2])
                    return ins
                P.op("pe", tr, reads=[S_mrow], writes=[S_ps[7]])
                pv = ps[7][:, 0:192].rearrange("p (j v) -> p j v", v=2)
                for v in range(2):
                    P.op("dve", lambda e, v=v: e.tensor_tensor(
                        out=modt[:, l, :, v], in0=pv[:, :, v], in1=bTt[:, l, :], op=ALU.add),
                        reads=[S_ps[7]], writes=[S_c])
                gm_finish(l)

            q = 0
            nib = 0
            for l in range(1):
                for n in range(24):
                    pb = n % 2
                    for kq in range(4):
                        s = G["slab_rr"] % NSLAB
                        G["slab_rr"] += 1
                        sv = slab[s][:].rearrange("p (c n) -> p c n", n=512)
                        wv = w_ada0_d[1024 * kq:1024 * kq + 1024, 512 * n:512 * n + 512].rearrange(
                            "(c p) n -> p c n", p=128)
                        P.dma("pool", s_slab[s], sv, wv, writes=[S_slab[s]])

                        def mm(e, sv=sv, kq=kq, pb=pb):
                            for c in range(8):
                                ins = e.matmul(ps[pb][0:2, :], lhsT=cb[:, 8 * kq + c, :], rhs=sv[:, c, :],
                                               start=(kq == 0 and c == 0), stop=(kq == 3 and c == 7))
                            return ins
                        P.op("pe", mm, reads=[S_slab[s], S_cb], writes=[S_ps[pb]])
                    cp("act", mrow[:, 512 * n:512 * n + 512], ps[pb][0:2, :], [S_ps[pb]], [S_mrow])
                    q += 1
                    while nib < 16 and nib < (q * 2 + 2) // 3:
                        input_block(nib)
                        nib += 1
                mod_finish(l)
            P.barrier()

        def rstd_from(ps_ap, out_ap, rd, ti):
            P.op("act", lambda e: e.activation(out=out_ap, in_=ps_ap, func=AF.Sqrt, bias=epsT[:, 0:1],
                                               scale=1.0 / 4096.0), reads=rd, writes=[S_rstdT[ti]])
            P.op("dve", lambda e: e.reciprocal(out=out_ap, in_=out_ap), reads=[S_rstdT[ti]], writes=[S_rstdT[ti]])

        def gm_finish(l):
            for v in range(2):
                P.op("dve", lambda e, v=v: e.tensor_scalar(
                    out=gm[:, l, v, :], in0=modt[:, l, 32:64, v], scalar1=1.0, scalar2=None, op0=ALU.add),
                    reads=[S_c], writes=[S_c])
                P.op("dve", lambda e, v=v: e.tensor_tensor(
                    out=gm[:, l, v, :], in0=gm[:, l, v, :], in1=gTt[:, l, :], op=ALU.mult),
                    reads=[S_c], writes=[S_c])

        def gemm_stream(jobs):
            n = len(jobs)
            st_ = {"nload": 0}

            def load_upto(k):
                while st_["nload"] < n and st_["nload"] <= k:
                    jb = jobs[st_["nload"]]
                    s = G["slab_rr"] % NSLAB
                    G["slab_rr"] += 1
                    jb["s"] = s
                    nk = jb["nk"]
                    sv = slab[s][:, 0:nk * 128].rearrange("p (a b) -> p a b", b=1024)
                    wv = jb["w"].rearrange("p (a b) -> p a b", b=1024)
                    P.dma("pool", s_slab[s], sv, wv, writes=[S_slab[s]])
                    st_["nload"] += 1

            def emit_mm(i):
                job = jobs[i]
                s, nk, act = job["s"], job["nk"], job["act"]
                pair = 2 * (G["gemm_rr"] % 2)
                G["gemm_rr"] += 1
                job["pair"] = pair
                sv = slab[s][:, 0:nk * 128].rearrange("p (c n) -> p c n", n=128)

                def mm(e, sv=sv, nk=nk, act=act, pair=pair):
                    for c in range(nk):
                        for hf in range(2):
                            ins = e.matmul(ps[pair + hf][:, :], lhsT=sv[:, c, :],
                                           rhs=act[:, c, 512 * hf:512 * hf + 512],
                                           start=(c == 0), stop=(c == nk - 1))
                    return ins
                if job.get("gemv"):
                    def mm(e, sv=sv, pair=pair):
                        for c in range(32):
                            ins = e.matmul(ps[pair][:, 0:2], lhsT=sv[:, c, :], rhs=cb[:, c, :],
                                           start=(c == 0), stop=(c == 31))
                        return ins
                P.op("pe", mm, reads=[S_slab[s]] + job["rd"], writes=[S_ps[pair], S_ps[pair + 1]])

            load_upto(1)
            emit_mm(0)
            for i in range(n):
                load_upto(i + 2)
                if i + 1 < n:
                    emit_mm(i + 1)
                jobs[i]["consume"](jobs[i]["pair"])

        def gen_norm(scr, S_scr, l, v, ti, alloc, sems, nbuf):
            xin = [alloc((128, 1024), F32) for _ in range(nbuf)]
            tmp = [alloc((128, 1024), F32) for _ in range(nbuf)]
            S_xin, S_tmp = [Slot() for _ in range(nbuf)], [Slot() for _ in range(nbuf)]
            for j in range(32):
                k = j % nbuf
                P.dma("sp", sems[k], xin[k], scr[j], reads=[S_scr[j]], writes=[S_xin[k]])
                P.op("dve", lambda e, k=k: e.tensor_tensor(out=tmp[k], in0=xin[k], in1=rstdT[ti][:], op=ALU.mult),
                     reads=[S_xin[k], S_rstdT[ti]], writes=[S_tmp[k]])
                P.op("act", lambda e, k=k, j=j: e.activation(out=hT[:, j, :], in_=tmp[k], func=AF.Identity,
                                                              scale=gm[:, l, v, j:j + 1],
                                                              bias=modt[:, l, j, v:v + 1]),
                     reads=[S_tmp[k]], writes=[S_hT])
                yield

        def phase_norm(scr, S_scr, l, v, ti):
            AR.reset()
            for _ in gen_norm(scr, S_scr, l, v, ti, AR.alloc, s_a[0:4], 4):
                pass
            P.barrier()

        def phase_outproj(W, scr_in, S_in, scr_out, S_out, l, v, ti, side=None):
            AR.reset()
            xin = [AR.alloc((128, 1024), F32) for _ in range(2)]
            xo = [AR.alloc((128, 1024), F32) for _ in range(2)]
            sq = [AR.alloc((128, 1024), BF16) for _ in range(2)]
            S_xin, S_xo, S_sq = [Slot(), Slot()], [Slot(), Slot()], [Slot(), Slot()]
            side = side() if side is not None else iter(())
            pending = []
            jobs = []
            for j in range(32):
                def consume(pair, j=j):
                    k = j % 2
                    P.dma("sp", s_a[k], xin[k], scr_in[j], reads=[S_in[j]], writes=[S_xin[k]])
                    for hf in range(2):
                        P.op("dve", lambda e, hf=hf, k=k: e.scalar_tensor_tensor(
                            out=xo[k][:, 512 * hf:512 * hf + 512], in0=ps[pair + hf][:, :],
                            scalar=modt[:, l, 64 + j, v:v + 1], in1=xin[k][:, 512 * hf:512 * hf + 512],
                            op0=ALU.mult, op1=ALU.add),
                            reads=[S_ps[pair + hf], S_xin[k]], writes=[S_xo[k]])
                    P.dma("sp", s_a[2 + k], scr_out[j], xo[k], reads=[S_xo[k]], writes=[S_out[j]])
                    P.op("act", lambda e, k=k: e.activation(out=sq[k], in_=xo[k], func=AF.Square),
                         reads=[S_xo[k]], writes=[S_sq[k]])

                    def ssq(e, k=k):
                        for hf in range(2):
                            ins = e.matmul(ps[6 + hf][:, :], lhsT=onesb[:], rhs=sq[k][:, 512 * hf:512 * hf + 512],
                                           start=(j == 0), stop=(j == 31))
                        return ins
                    while pending:
                        pending.pop(0)()
                    pending.append(lambda k=k, ssq=ssq: P.op("pe", ssq, reads=[S_sq[k]], writes=[S_ps[6], S_ps[7]]))
                    next(side, None)
                jobs.append(dict(w=W[j], nk=32, act=og[:], rd=[S_og], consume=consume))
            gemm_stream(jobs)
            while pending:
                pending.pop(0)()
            for _ in side:
                pass
            for hf in range(2):
                rstd_from(ps[6 + hf][:, :], rstdT[ti][:, 512 * hf:512 * hf + 512], [S_ps[6 + hf]], ti)
            P.barrier()

        hT_f_all = hT[:].rearrange("p a b -> p (a b)").bitcast(F32)
        HAR = Arena(hT_f_all, 16384)

        def gen_final(scr, S_scr, yout, ti, sems):
            HAR.reset()
            xin = [HAR.alloc((128, 1024), F32) for _ in range(4)]
            tmp = [HAR.alloc((128, 1024), F32) for _ in range(4)]
            yst = [HAR.alloc((128, 8, 512), F32) for _ in range(2)]
            S_xin, S_tmp, S_y = [Slot() for _ in range(4)], [Slot() for _ in range(4)], [Slot(), Slot()]
            yv = yout.rearrange("(b p) d -> p b d", p=128)
            def stage_a(j):
                k = j % 4
                P.dma("sp", sems[k], xin[k], scr[j], reads=[S_scr[j]], writes=[S_xin[k]])
                P.op("dve", lambda e: e.tensor_tensor(out=xin[k], in0=xin[k], in1=rstdT[ti][:], op=ALU.mult),
                     reads=[S_xin[k], S_rstdT[ti]], writes=[S_xin[k]])
                P.op("act", lambda e: e.activation(out=tmp[k], in_=xin[k], func=AF.Identity, scale=fg[:, j:j + 1]),
                     reads=[S_xin[k]], writes=[S_tmp[k]])

            def stage_b(j):
                k = j % 4
                q4, jj4 = divmod(j, 4)
                ky = q4 % 2
                for g in range(2):
                    bank = 4 + g

                    def tr(e, g=g, bank=bank):
                        for jj in range(4):
                            b = 4 * g + jj
                            ins = e.transpose(out=ps[bank][:, 128 * jj:128 * jj + 128],
                                              in_=tmp[k][:, 128 * b:128 * b + 128], identity=identf[:])
                        return ins
                    P.op("pe", tr, reads=[S_tmp[k]], writes=[S_ps[bank]])
                    cp("act" if g == 0 else "dve", yst[ky][:, 4 * g:4 * g + 4, 128 * jj4:128 * jj4 + 128],
                       ps[bank][:].rearrange("p (a b) -> p a b", a=4), [S_ps[bank]], [S_y[ky]])
                if jj4 == 3:
                    P.dma("sp", sems[4 + ky], yv[:, :, 512 * q4:512 * q4 + 512], yst[ky], reads=[S_y[ky]])

            for i in range(34):
                if i < 32:
                    stage_a(i)
                if i >= 2:
                    stage_b(i - 2)
                yield

        def phase_final(scr, S_scr, yout, ti):
            for _ in gen_final(scr, S_scr, yout, ti, s_b):
                pass
            P.barrier()

        def phase_pool(is_sample):
            AR.reset()
            nseq, L = (1, 1024) if is_sample else (4, 256)
            LP = L + 16
            pl = AR.alloc((128, 8, 1024), BF16)
            up = AR.alloc((128, nseq, LP), F32)
            ba = AR.alloc((128, nseq, LP), F32)
            bb = AR.alloc((128, nseq, LP), F32)
            rc = AR.alloc((128, 1024), F32)
            sgt = [AR.alloc((128, 1024), BF16) for _ in range(2)]
            S_pl, S_up, S_ba, S_bb, S_rc = Slot(), Slot(), Slot(), Slot(), Slot()
            S_sgt = [Slot(), Slot()]
            rc_d = rcS_d if is_sample else rcP_d
            for t in (up, ba, bb):
                P.op("dve", lambda e, t=t: e.memset(t, 0.0), writes=[S_up, S_ba, S_bb])
            rc3 = rc.rearrange("p (s l) -> p s l", s=nseq)
            TT = lambda o, a, b, op: (lambda e: e.tensor_tensor(out=o, in0=a, in1=b, op=op))
            jobs = []
            for gi in range(4):
                for c in range(8):
                    j = 8 * gi + c

                    def consume_u(pair, gi=gi, c=c):
                        if c == 0:
                            P.dma("sp", s_a[4], rc, rc_d[gi, :].partition_broadcast(128), writes=[S_rc])
                        for hf in range(2):
                            if is_sample:
                                o = up[:, 0, 8 + 512 * hf:8 + 512 * hf + 512]
                                i_ = ps[pair + hf][:, :]
                            else:
                                o = up[:, 2 * hf:2 * hf + 2, 8:8 + L]
                                i_ = ps[pair + hf][:].rearrange("p (s l) -> p s l", s=2)
                            cp("act", o, i_, [S_ps[pair + hf]], [S_up])
                        P.op("dve", TT(ba[:, :, 1:LP], up[:, :, 1:LP], up[:, :, 0:LP - 1], ALU.add),
                             reads=[S_up], writes=[S_ba])
                        fin, S_fin, oth, S_oth = ba, S_ba, bb, S_bb
                        if gi >= 1:
                            P.op("dve", TT(bb[:, :, 2:LP - 1], ba[:, :, 1:LP - 2], ba[:, :, 3:LP], ALU.add),
                                 reads=[S_ba], writes=[S_bb])
                            fin, S_fin, oth, S_oth = bb, S_bb, ba, S_ba
                        if gi >= 2:
                            P.op("dve", TT(ba[:, :, 4:LP - 3], bb[:, :, 2:LP - 5], bb[:, :, 6:LP - 1], ALU.add),
                                 reads=[S_bb], writes=[S_ba])
                            fin, S_fin, oth, S_oth = ba, S_ba, bb, S_bb
                        if gi >= 3:
                            P.op("dve", TT(bb[:, :, 8:LP - 8], ba[:, :, 4:LP - 12], ba[:, :, 12:LP - 4], ALU.add),
                                 reads=[S_ba], writes=[S_bb])
                            fin, S_fin, oth, S_oth = bb, S_bb, ba, S_ba
                        P.op("dve", TT(oth[:, :, 8:8 + L], fin[:, :, 8:8 + L], rc3, ALU.mult),
                             reads=[S_fin, S_rc], writes=[S_oth])
                        P.op("dve", TT(pl[:, c, :].rearrange("p (s l) -> p s l", s=nseq), oth[:, :, 8:8 + L],
                                       up[:, :, 8:8 + L], ALU.subtract),
                             reads=[S_oth, S_up], writes=[S_pl])
                    jobs.append(dict(w=w_in_pool_d[j], nk=32, act=hT[:], rd=[S_hT], consume=consume_u))
                for c in range(8):
                    j = 8 * gi + c

                    def consume_g(pair, j=j):
                        k = j % 2
                        for hf in range(2):
                            P.op("act", lambda e, hf=hf, k=k: e.activation(
                                out=sgt[k][:, 512 * hf:512 * hf + 512], in_=ps[pair + hf][:, :], func=AF.Silu),
                                reads=[S_ps[pair + hf]], writes=[S_sgt[k]])

                    def consume_y(pair, j=j):
                        k = j % 2
                        for hf in range(2):
                            P.op("dve", lambda e, hf=hf, k=k: e.scalar_tensor_tensor(
                                out=og[:, j, 512 * hf:512 * hf + 512], in0=ps[pair + hf][:, :],
                                scalar=psc[:, j:j + 1], in1=sgt[k][:, 512 * hf:512 * hf + 512],
                                op0=ALU.mult, op1=ALU.mult),
                                reads=[S_ps[pair + hf], S_sgt[k]], writes=[S_og])
                    jobs.append(dict(w=w_in_pool_d[32 + j], nk=32, act=hT[:], rd=[S_hT], consume=consume_g))
                    jobs.append(dict(w=w_grp_d[j], nk=8, act=pl, rd=[S_pl], consume=consume_y))
            gemm_stream(jobs)
            P.barrier()

        def phase_attn(is_sample):
            AR.reset()
            qT = AR.alloc((128, 1024), BF16)
            kT = AR.alloc((128, 1024), BF16)
            vTf = AR.alloc((128, 1024), F32)
            vE = AR.alloc((128, 8, 128), BF16)
            sg = AR.alloc((128, 1024), BF16)
            PT = [AR.alloc((128, 512), BF16) for _ in range(4)]
            n1 = AR.alloc((128, 512), F32)
            n2 = AR.alloc((128, 512), F32)
            S_qT, S_kT, S_vTf, S_vE, S_sg, S_n, S_n2 = Slot(), Slot(), Slot(), Slot(), Slot(), Slot(), Slot()
            S_PT = [Slot() for _ in range(4)]
            if is_sample:
                vO = AR.alloc((128, 7, 128), BF16)
                cKf = AR.alloc((128, 4, 128), F32)
                cKT = AR.alloc((128, 512), BF16)
                cV = AR.alloc((128, 4, 128), BF16)
                bt = AR.alloc((128, 14, 64), BF16)
                amk = AR.alloc((128, 256), BF16)
                S_vO, S_cKf, S_cKT, S_cV, S_bt = Slot(), Slot(), Slot(), Slot(), Slot()
                cosT = og_f[:, 12288:13312]
                sinT = og_f[:, 13312:14336]
                qf = og_f[:, 14336:14848]
                rt = og_f[:, 14848:15360]
                permf = og_f[:, 15360:15488]
                amf = og_f[:, 15488:15744]
                S_qf, S_rt = Slot(), Slot()
                P.dma("sp", s_c, cosT, ropec_d[:, :], writes=[S_c])
                P.dma("sp", s_c, sinT, ropes_d[:, :], writes=[S_c])
                P.dma("sp", s_c, permf, permT_d[:, :], writes=[S_c])
                P.dma("sp", s_c, amf, amask_d[:, :], writes=[S_c])
                P.op("dve", lambda e: e.tensor_copy(out=amk, in_=amf), reads=[S_c], writes=[S_c])
                P.barrier()
            else:
                kf = AR.alloc((128, 1024), F32)
                kst = AR.alloc((128, 8, 128), F32)
                vst = AR.alloc((128, 8, 128), F32)
                S_kf, S_kst, S_vst = Slot(), Slot(), Slot()

            def evac2(pair, dst, S_dst, scale=None, eng="act", excl=False):
                for hf in range(2):
                    cp(eng, dst[:, 512 * hf:512 * hf + 512], ps[pair + hf][:, :], [S_ps[pair + hf]],
                       [S_dst] + ([S_ps[pair + hf]] if excl else []), scale)

            def rope_evac(pair, dst, S_dst, scale):
                for hf in range(2):
                    cp("act", qf, ps[pair + hf][:, :], [S_ps[pair + hf]], [S_qf], scale)
                    P.op("pe", lambda e: e.matmul(ps[4][:, :], lhsT=permf, rhs=qf, start=True, stop=True),
                         reads=[S_qf], writes=[S_ps[4]])
                    P.op("dve", lambda e, hf=hf: e.tensor_tensor(out=rt, in0=qf, in1=cosT[:, 512 * hf:512 * hf + 512],
                                                                  op=ALU.mult), reads=[S_qf], writes=[S_rt])
                    P.op("dve", lambda e, hf=hf: e.tensor_tensor(out=qf, in0=ps[4][:, :],
                                                                  in1=sinT[:, 512 * hf:512 * hf + 512], op=ALU.mult),
                         reads=[S_ps[4]], writes=[S_qf])
                    P.op("dve", lambda e, hf=hf: e.tensor_tensor(out=dst[:, 512 * hf:512 * hf + 512], in0=rt, in1=qf,
                                                                  op=ALU.add), reads=[S_rt, S_qf], writes=[S_dst])

            def transposes(srcs, bank, rd):
                def tr(e):
                    for i, s_ in enumerate(srcs):
                        ins = e.transpose(out=ps[bank][:, 128 * i:128 * i + 128], in_=s_, identity=identf[:])
                    return ins
                P.op("pe", tr, reads=rd, writes=[S_ps[bank]])

            def v_transposes(out_d=None, col0=0):
                for g in range(2):
                    bank = 5 - g
                    transposes([vTf[:, 128 * (4 * g + i):128 * (4 * g + i) + 128] for i in range(4)], bank, [S_vTf])
                    pv = ps[bank][:].rearrange("p (a b) -> p a b", a=4)
                    if out_d is None:
                        cp("dve", vE[:, 4 * g:4 * g + 4, :], pv, [S_ps[bank]], [S_vE])
                    else:
                        cp("dve", vE[:, 4 * g:4 * g + 4, :], pv, [S_ps[bank]], [S_vE, S_ps[bank]])
                        cp("act", vst[:, 4 * g:4 * g + 4, :], pv, [S_ps[bank]], [S_vst, S_ps[bank]])
                if out_d is not None:
                    P.dma("sp", s_a[5], out_d.rearrange("(b p) d -> p b d", p=128)[:, :, col0:col0 + 128], vst,
                          reads=[S_vst])

            def k_out(out_d, col0):
                for g in range(2):
                    bank = 5 - g
                    transposes([kf[:, 128 * (4 * g + i):128 * (4 * g + i) + 128] for i in range(4)], bank, [S_kf])
                    cp("act", kst[:, 4 * g:4 * g + 4, :], ps[bank][:].rearrange("p (a b) -> p a b", a=4),
                       [S_ps[bank]], [S_kst])
                P.dma("sp", s_a[6], out_d.rearrange("(b p) d -> p b d", p=128)[:, :, col0:col0 + 128], kst,
                      reads=[S_kst])

            def vO_transposes():
                for g in range(2):
                    bank = 5 - g
                    n = 4 if g == 0 else 3
                    transposes([vTf[:, 64 + 128 * (4 * g + i):64 + 128 * (4 * g + i) + 128] for i in range(n)],
                               bank, [S_vTf])
                    cp("dve", vO[:, 4 * g:4 * g + n, :],
                       ps[bank][:, 0:128 * n].rearrange("p (a b) -> p a b", a=n), [S_ps[bank]], [S_vO])

            def load_cache(kd, vd, col0):
                P.dma("sp", s_a[0], cKf, kd.rearrange("(b p) d -> p b d", p=128)[:, :, col0:col0 + 128],
                      writes=[S_cKf])
                P.dma("pool", s_p[0], cV, vd.rearrange("(b p) d -> p b d", p=128)[:, :, col0:col0 + 128],
                      writes=[S_cV])
                transposes([cKf[:, b, :] for b in range(4)], 5, [S_cKf])
                cp("act", cKT, ps[5][:, :], [S_ps[5]], [S_cKT])

            def normalize(hf, chunk, sink_h):
                if sink_h is not None:
                    P.op("dve", lambda e: e.tensor_scalar(out=n1, in0=ps[7][:, :], scalar1=es[:, sink_h:sink_h + 1],
                                                          scalar2=None, op0=ALU.add), reads=[S_ps[7]], writes=[S_n])
                    P.op("act", lambda e: e.activation(out=n2, in_=ps[6][:, :], func=AF.Copy), reads=[S_ps[6]],
                         writes=[S_n2])
                    P.op("dve", lambda e: e.reciprocal(out=n1, in_=n1), reads=[S_n], writes=[S_n])
                else:
                    P.op("dve", lambda e: e.reciprocal(out=n1, in_=ps[7][:, :]), reads=[S_ps[7]], writes=[S_n])
                    P.op("act", lambda e: e.activation(out=n2, in_=ps[6][:, :], func=AF.Copy), reads=[S_ps[6]],
                         writes=[S_n2])
                P.op("dve", lambda e: e.tensor_tensor(out=n2, in0=n2, in1=n1, op=ALU.mult),
                     reads=[S_n], writes=[S_n2])
                P.op("dve", lambda e: e.tensor_tensor(out=og[:, chunk, 512 * hf:512 * hf + 512], in0=n2,
                                                      in1=sg[:, 512 * hf:512 * hf + 512], op=ALU.mult),
                     reads=[S_n2, S_sg], writes=[S_og])

            def pv_part(blocks_v, pt, S_pt, ocol, w, extra_rd, first=True, last=True):
                nb = len(blocks_v)

                def f(e):
                    for i, (vap, c0) in enumerate(blocks_v):
                        e.matmul(ps[6][:, ocol:ocol + w], lhsT=vap, rhs=pt[:, c0:c0 + w],
                                 start=(first and i == 0), stop=(last and i == nb - 1))
                    for i, (vap, c0) in enumerate(blocks_v):
                        ins = e.matmul(ps[7][:, ocol:ocol + w], lhsT=onesb[:], rhs=pt[:, c0:c0 + w],
                                       start=(first and i == 0), stop=(last and i == nb - 1))
                    return ins
                P.op("pe", f, reads=[S_pt] + extra_rd, writes=[S_ps[6], S_ps[7]])

            def run_stages(stages):
                n = len(stages)
                stages[0]["sf"]()
                for i in range(n):
                    stages[i]["ex"]()
                    if i + 1 < n:
                        stages[i + 1]["sf"]()
                    stages[i]["pv"]()
                    if stages[i].get("post"):
                        stages[i]["post"]()

            def attn_A_lat(h):
                stages = []
                for n in range(8):
                    blocks = []
                    if n > 0:
                        blocks.append((kT[:, 128 * (n - 1):128 * n], vE[:, n - 1, :], amk[:, 0:128]))
                    blocks.append((kT[:, 128 * n:128 * n + 128], vE[:, n, :], None))
                    if n < 7:
                        blocks.append((kT[:, 128 * (n + 1):128 * (n + 2)], vE[:, n + 1, :], amk[:, 128:256]))
                    for cbk_ in range(4):
                        blocks.append((cKT[:, 128 * cbk_:128 * cbk_ + 128], cV[:, cbk_, :], None))
                    for sidx in range(2):
                        bl = blocks[4 * sidx:4 * sidx + 4]
                        bank = 4 + sidx
                        pi = 2 * (n % 2) + sidx
                        wS = 128 * len(bl)

                        def sfn(bl=bl, n=n, bank=bank):
                            def sf(e):
                                for idx, (kap, vap, m) in enumerate(bl):
                                    o = ps[bank][:, 128 * idx:128 * idx + 128]
                                    ins = e.matmul(o, lhsT=kap, rhs=qT[:, 128 * n:128 * n + 128], start=True,
                                                   stop=(m is None))
                                    if m is not None:
                                        ins = e.matmul(o, lhsT=identb[:], rhs=m, start=False, stop=True)
                                return ins
                            P.op("pe", sf, reads=[S_kT, S_qT, S_cKT], writes=[S_ps[bank]])

                        def exn(bank=bank, pi=pi, wS=wS):
                            P.op("act", lambda e: e.activation(out=PT[pi][:, 0:wS], in_=ps[bank][:, 0:wS], func=AF.Exp),
                                 reads=[S_ps[bank]], writes=[S_PT[pi]])

                        def pvn(bl=bl, pi=pi, n=n, sidx=sidx):
                            pv_part([(vap, 128 * idx) for idx, (kap, vap, m) in enumerate(bl)], PT[pi], S_PT[pi],
                                    128 * (n % 4), 128, [S_vE, S_cV], first=(sidx == 0), last=(sidx == 1))
                        post = None
                        if sidx == 1 and n % 4 == 3:
                            post = (lambda n=n: normalize(n // 4, h, h))
                        stages.append(dict(sf=sfn, ex=exn, pv=pvn, post=post))
                run_stages(stages)

            def attn_B_lat(h):
                stages = []
                for r in range(16):
                    rs = min(max(r - 4, 0), 8)
                    sbk = 4 + r % 2
                    pk = r % 2

                    def sfn(r=r, rs=rs, sbk=sbk):
                        def sf(e):
                            for b in range(4):
                                o = ps[sbk][:, 64 * b:64 * b + 64]
                                t0 = 64 * (rs + 2 * b)
                                e.matmul(o, lhsT=kT[:, t0:t0 + 128], rhs=qT[:, 64 * r:64 * r + 64], start=True,
                                         stop=False)
                                e.matmul(o, lhsT=identb[:], rhs=bt[:, rs + 2 * b - r + 7, :], start=False, stop=True)
                            for c_ in range(4):
                                ins = e.matmul(ps[sbk][:, 256 + 64 * c_:256 + 64 * c_ + 64],
                                               lhsT=cKT[:, 128 * c_:128 * c_ + 128], rhs=qT[:, 64 * r:64 * r + 64],
                                               start=True, stop=True)
                            return ins
                        P.op("pe", sf, reads=[S_kT, S_qT, S_cKT, S_bt], writes=[S_ps[sbk]])

                    def exn(pk=pk, sbk=sbk):
                        P.op("act", lambda e: e.activation(out=PT[pk][:, 0:512], in_=ps[sbk][:, :], func=AF.Exp),
                             reads=[S_ps[sbk]], writes=[S_PT[pk]])

                    def pvn(r=r, rs=rs, pk=pk):
                        bl = []
                        for b in range(4):
                            vap = vE[:, (rs + 2 * b) // 2, :] if rs % 2 == 0 else vO[:, (rs - 1) // 2 + b, :]
                            bl.append((vap, 64 * b))
                        for c_ in range(4):
                            bl.append((cV[:, c_, :], 256 + 64 * c_))
                        pv_part(bl, PT[pk], S_PT[pk], 64 * (r % 8), 64, [S_vE, S_vO, S_cV])
                    post = (lambda r=r: normalize(r // 8, 16 + h, None)) if r % 8 == 7 else None
                    stages.append(dict(sf=sfn, ex=exn, pv=pvn, post=post))
                run_stages(stages)

            def attn_ctx(chunk, sink_h):
                stages = []
                for s_ in range(4):
                    sbk = 4 + s_ % 2
                    pk = s_ % 2

                    def sfn(s_=s_, sbk=sbk):
                        def sf(e):
                            for b in range(2):
                                ins = e.matmul(ps[sbk][:, 256 * b:256 * b + 256],
                                               lhsT=kT[:, 256 * s_ + 128 * b:256 * s_ + 128 * b + 128],
                                               rhs=qT[:, 256 * s_:256 * s_ + 256], start=True, stop=True)
                            return ins
                        P.op("pe", sf, reads=[S_kT, S_qT], writes=[S_ps[sbk]])

                    def exn(pk=pk, sbk=sbk):
                        P.op("act", lambda e: e.activation(out=PT[pk][:, 0:512], in_=ps[sbk][:, :], func=AF.Exp),
                             reads=[S_ps[sbk]], writes=[S_PT[pk]])

                    def pvn(s_=s_, pk=pk):
                        pv_part([(vE[:, 2 * s_ + b, :], 256 * b) for b in range(2)], PT[pk], S_PT[pk],
                                256 * (s_ % 2), 256, [S_vE])
                    post = (lambda s_=s_: normalize(s_ // 2, chunk, sink_h)) if s_ % 2 == 1 else None
                    stages.append(dict(sf=sfn, ex=exn, pv=pvn, post=post))
                run_stages(stages)

            W = w_in_attn_d
            jobs = []

            def J(col, consume):
                jobs.append(dict(w=W[col // 128], nk=32, act=hT[:], rd=[S_hT], consume=consume))

            for g in range(4):
                if is_sample:
                    def ck(pair, g=g):
                        load_cache(cak_d, cav_d, 128 * g)
                        rope_evac(pair, kT, S_kT, None)

                    def cv_(pair):
                        evac2(pair, vTf, S_vTf)
                        v_transposes()
                else:
                    def ck(pair, g=g):
                        evac2(pair, kT, S_kT, excl=True)
                        evac2(pair, kf, S_kf, eng="dve", excl=True)
                        k_out(nak_d, 128 * g)

                    def cv_(pair, g=g):
                        evac2(pair, vTf, S_vTf)
                        v_transposes(nav_d, 128 * g)
                J(2048 + 128 * g, ck)
                J(2560 + 128 * g, cv_)
                for hq in range(4):
                    h = 4 * g + hq
                    if is_sample:
                        def cq(pair):
                            rope_evac(pair, qT, S_qT, SCALE)
                    else:
                        def cq(pair):
                            evac2(pair, qT, S_qT, SCALE)

                    def cg(pair, h=h):
                        for hf in range(2):
                            P.op("act", lambda e, hf=hf: e.activation(out=sg[:, 512 * hf:512 * hf + 512],
                                                                       in_=ps[pair + hf][:, :], func=AF.Silu),
                                 reads=[S_ps[pair + hf]], writes=[S_sg])
                        if is_sample:
                            attn_A_lat(h)
                        else:
                            attn_ctx(h, h)
                    J(128 * h, cq)
                    J(9216 + 128 * h, cg)
            first_b = [True]
            for h in range(16):
                def ck(pair, h=h):
                    if is_sample:
                        if first_b[0]:
                            first_b[0] = False
                            P.barrier()
                        load_cache(cbk_d, cbv_d, 128 * h)
                        P.dma("pool", s_p[1], bt, TB_d[h].rearrange("p (a b) -> p a b", a=14), writes=[S_bt])
                        evac2(pair, kT, S_kT)
                    else:
                        evac2(pair, kT, S_kT, excl=True)
                        evac2(pair, kf, S_kf, eng="dve", excl=True)
                        k_out(nbk_d, 128 * h)

                def cv_(pair, h=h):
                    evac2(pair, vTf, S_vTf)
                    if is_sample:
                        v_transposes()
                        vO_transposes()
                    else:
                        v_transposes(nbv_d, 128 * h)

                def cq(pair):
                    evac2(pair, qT, S_qT, SCALE)

                def cg(pair, h=h):
                    for hf in range(2):
                        P.op("act", lambda e, hf=hf: e.activation(out=sg[:, 512 * hf:512 * hf + 512],
                                                                   in_=ps[pair + hf][:, :], func=AF.Silu),
                             reads=[S_ps[pair + hf]], writes=[S_sg])
                    if is_sample:
                        attn_B_lat(h)
                    else:
                        attn_ctx(16 + h, None)
                J(5120 + 128 * h, ck)
                J(7168 + 128 * h, cv_)
                J(3072 + 128 * h, cq)
                J(9216 + 2048 + 128 * h, cg)
            if is_sample:
                mixed = []
                for i, jb in enumerate(jobs):
                    mixed.append(jb)
                    if i < 96:
                        def cada(pair, j=i):
                            P.op("dve", lambda e: e.tensor_scalar(out=modt[:, 1, j, :], in0=ps[pair][:, 0:2],
                                                                  scalar1=bTt[:, 1, j:j + 1], scalar2=None,
                                                                  op0=ALU.add),
                                 reads=[S_ps[pair]], writes=[S_c])
                        mixed.append(dict(w=w_ada1_d[i], nk=32, act=None, rd=[S_c],
                                          consume=cada, gemv=True))
                jobs = mixed
            gemm_stream(jobs)
            if is_sample:
                gm_finish(1)
            P.barrier()

        def S_out1_P_norm0():
            phase_outproj(w_out_pool_d, scrB, S_scrB, scrA, S_scrA, 1, 1, 0,
                          side=lambda: gen_norm(scrC, S_scrC, 0, 0, 1, AR.alloc, s_b[0:2], 2))

        def P_out0_S_final():
            phase_outproj(w_out_attn_d, scrC, S_scrC, scrD, S_scrD, 0, 0, 1,
                          side=lambda: gen_final(scrA, S_scrA, ys_d, 0, s_b))

        steps = [
            ("ada", phase_ada_input),
            ("Snorm0", lambda: phase_norm(scrA, S_scrA, 0, 1, 0)),
            ("Sattn", lambda: phase_attn(True)),
            ("Sout0", lambda: phase_outproj(w_out_attn_d, scrA, S_scrA, scrB, S_scrB, 0, 1, 0)),
            ("Snorm1", lambda: phase_norm(scrB, S_scrB, 1, 1, 0)),
            ("Spool", lambda: phase_pool(True)),
            ("Sout1", S_out1_P_norm0),
            ("Pattn", lambda: phase_attn(False)),
            ("Pout0", P_out0_S_final),
            ("Pnorm1", lambda: phase_norm(scrD, S_scrD, 1, 0, 1)),
            ("Ppool", lambda: phase_pool(False)),
            ("Pout1", lambda: phase_outproj(w_out_pool_d, scrD, S_scrD, scrC, S_scrC, 1, 0, 1)),
            ("Pfinal", lambda: phase_final(scrC, S_scrC, yp_d, 1)),
        ]
        for name, fn in steps:
            fn()
            if stop == name:
                break
        P.barrier()
        P.cnt_report = dict(P.cnt)
        print("sem counts", P.cnt)
        P.run()
    return nc


def _consts():
    f32 = np.float32
    ident = np.eye(128, dtype=f32)
    permT = np.zeros((128, 128), f32)
    sign = np.zeros(128, f32)
    for m in range(128):
        a, rem = divmod(m, 64)
        if rem < 32:
            src, sg_ = a * 64 + 32 + rem, -1.0
        else:
            src, sg_ = a * 64 + rem - 32, 1.0
        permT[src, m] = 1.0
        sign[m] = sg_
    t = np.arange(1024)
    inv_freq = (f32(10000.0) ** (-np.arange(32, dtype=f32) / f32(32))).astype(f32)
    ang_r = (t // 64).astype(f32)[:, None] * inv_freq
    ang_c = (t % 64).astype(f32)[:, None] * inv_freq
    ang = np.concatenate([ang_r, ang_r, ang_c, ang_c], axis=-1).astype(f32)
    ropec = np.ascontiguousarray(np.cos(ang).T.astype(f32))
    ropes = np.ascontiguousarray((np.sin(ang).T * sign[:, None]).astype(f32))
    jj = np.arange(128)[:, None]
    qq = np.arange(128)[None, :]
    amask = np.concatenate([np.where(qq <= jj, 0.0, NEG), np.where(jj <= qq, 0.0, NEG)], axis=1).astype(f32)

    def rc(L, reps):
        out = np.zeros((4, L), f32)
        tt = np.arange(L)
        for gi, half in enumerate((1, 2, 4, 8)):
            lo = np.clip(tt - half, 0, L)
            hi = np.clip(tt + half, 0, L)
            out[gi] = (f32(1.0) / (hi - lo).astype(f32)).astype(f32)
        return np.ascontiguousarray(np.tile(out, (1, reps)))
    return dict(ident=ident, permT=permT, ropec=ropec, ropes=ropes, amask=amask, rcS=rc(1024, 1), rcP=rc(256, 4))


def _bias_table(rpb):
    c = np.arange(64)
    kc = np.arange(64)
    col_start = np.clip(c - 8, 0, 48)
    ok = (kc[:, None] >= col_start[None, :]) & (kc[:, None] < col_start[None, :] + 16)
    dc = np.clip(kc[:, None] - c[None, :] + 15, 0, 30)
    TB = np.empty((16, 2, 64, 14, 64), np.float32)
    for a in range(2):
        for dr0 in range(14):
            g = rpb[:, dr0 + a][:, dc]
            TB[:, a, :, dr0, :] = np.where(ok[None], g, np.float32(NEG))
    return np.ascontiguousarray(TB.reshape(16, 128, 14 * 64))


_NC_CACHE = {}
_NCORES = [8]
_PREP_ONLY = [False]


def kernel(x_prompt, x_sample, c, cache_a_k, cache_a_v, cache_b_k, cache_b_v, c_ctx,
           w_ada, b_ada, norm_g, w_in_attn, a_sink, b_rpb, w_out_attn,
           w_in_pool, w_grp_pool, pool_scale, w_out_pool, final_g):
    f = lambda a: np.ascontiguousarray(np.asarray(a, dtype=np.float32))
    x_prompt, x_sample, c, c_ctx = f(x_prompt), f(x_sample), f(c), f(c_ctx)
    cache_a_k, cache_a_v, cache_b_k, cache_b_v = f(cache_a_k), f(cache_a_v), f(cache_b_k), f(cache_b_v)
    w_ada, b_ada, norm_g = f(w_ada), f(b_ada), f(norm_g)
    consts = _consts()
    def tile_w(w2d):
        K, N = w2d.shape
        return np.ascontiguousarray(w2d.reshape(K // 128, 128, N // 128, 128).transpose(2, 1, 0, 3)).reshape(
            N // 128, 128, (K // 128) * 128)
    wg = f(w_grp_pool).reshape(4, 1024, 1024)
    shared = dict(
        w_ada0=np.ascontiguousarray(w_ada[0]), w_ada1=tile_w(w_ada[1]),
        bT=f(b_ada.reshape(2, 96, 128).transpose(2, 0, 1).reshape(128, 192)),
        gT=f(norm_g.reshape(2, 32, 128).transpose(2, 0, 1).reshape(128, 64)),
        fgT=f(np.asarray(final_g, np.float32).reshape(32, 128).T),
        pscT=f(np.asarray(pool_scale, np.float32).reshape(32, 128).T),
        w_in_attn=tile_w(f(w_in_attn).reshape(4096, 13312)),
        w_out_attn=tile_w(f(w_out_attn).reshape(4096, 4096)),
        w_in_pool=tile_w(f(w_in_pool).reshape(4096, 8192)),
        w_grp=np.concatenate([tile_w(wg[g]) for g in range(4)], axis=0),
        w_out_pool=tile_w(f(w_out_pool).reshape(4096, 4096)),
        sinkb=f(np.broadcast_to(np.asarray(a_sink, np.float32).reshape(1, 16), (128, 16))),
        TB=_bias_table(np.asarray(b_rpb, np.float32)[0]),
        **consts,
    )
    in_maps = []
    for i in range(_NCORES[0]):
        cond = np.stack([c_ctx, c[i]], axis=0)
        cond2 = f(cond.reshape(2, 32, 128).transpose(2, 1, 0).reshape(128, 64))
        m = dict(shared)
        m.update(
            xs=f(x_sample[i]), xp=f(x_prompt[4 * i:4 * i + 4].reshape(1024, 4096)), cond2=cond2,
            cak=f(cache_a_k[i, 0].reshape(512, 512)), cav=f(cache_a_v[i, 0].reshape(512, 512)),
            cbk=f(cache_b_k[i, 0].reshape(512, 2048)), cbv=f(cache_b_v[i, 0].reshape(512, 2048)),
        )
        in_maps.append(m)
    if _PREP_ONLY[0]:
        return in_maps
    if "nc" not in _NC_CACHE:
        _NC_CACHE["nc"] = build_nc()
    res = run_bass_kernel_spmd(_NC_CACHE["nc"], in_maps, core_ids=list(range(8)))
    R = res.results
    y_sample = np.stack([R[i]["ys"] for i in range(8)], axis=0)
    y_prompt = np.concatenate([R[i]["yp"].reshape(4, 256, 4096) for i in range(8)], axis=0)
    cat = lambda k, nh: np.concatenate([R[i][k].reshape(4, 1, 256, nh, 128) for i in range(8)], axis=0)
    return (y_prompt, y_sample, cat("nak", 4), cat("nav", 4), cat("nbk", 16), cat("nbv", 16))
```

```python
import numpy as np
from contextlib import ExitStack
import concourse.bass as bass
import concourse.mybir as mybir
from concourse.bass_utils import run_bass_kernel_spmd

F32 = mybir.dt.float32
BF16 = mybir.dt.bfloat16
AF = mybir.ActivationFunctionType
ALU = mybir.AluOpType

ENGS = ("pe", "act", "dve", "pool", "sp")
NEG = -30000.0
SCALE = 128 ** -0.5
EPS = 1e-6
ARENA_W = 9728
NSLAB = 3


class Slot:
    __slots__ = ("w", "r")

    def __init__(self):
        self.w = None
        self.r = {}


class Prog:
    def __init__(self, nc, stack):
        self.nc = nc
        self.stack = stack
        self.q = {e: [] for e in ENGS}
        self.cnt = {}
        self.sem = {}
        self.waited = {e: {} for e in ENGS}
        for e in ENGS:
            if e != "sp":
                self.new_sem("c_" + e)

    def new_sem(self, name):
        self.sem[name] = self.stack.enter_context(self.nc.semaphore(name))
        self.cnt[name] = 0
        return name

    def _wait(self, eng, tok):
        if tok is None:
            return
        sname, val = tok
        if self.waited[eng].get(sname, 0) >= val:
            return
        self.waited[eng][sname] = val
        sem = self.sem[sname]
        self.q[eng].append(lambda e, sem=sem, val=val: e.wait_ge(sem, val))

    def _deps(self, eng, reads, writes, extra):
        best = {}
        toks = list(extra)
        for s in reads:
            toks.append(s.w)
        for s in writes:
            toks.append(s.w)
            toks.extend(s.r.items())
        for t in toks:
            if t is not None and best.get(t[0], 0) < t[1]:
                best[t[0]] = t[1]
        for sname, val in best.items():
            self._wait(eng, (sname, val))

    def _commit(self, tok, reads, writes):
        for s in reads:
            if s.r.get(tok[0], 0) < tok[1]:
                s.r[tok[0]] = tok[1]
        for s in writes:
            s.w = tok
            s.r = {}

    def op(self, eng, fn, reads=(), writes=(), extra=()):
        self._deps(eng, reads, writes, extra)
        sname = "c_" + eng
        self.cnt[sname] += 1
        sem = self.sem[sname]
        self.q[eng].append(lambda e, fn=fn, sem=sem: fn(e).then_inc(sem, 1))
        tok = (sname, self.cnt[sname])
        self._commit(tok, reads, writes)
        return tok

    def dma(self, eng, sname, out, in_, reads=(), writes=(), extra=()):
        self._deps(eng, reads, writes, extra)
        self.cnt[sname] += 16
        sem = self.sem[sname]
        self.q[eng].append(lambda e, out=out, in_=in_, sem=sem: e.dma_start(out=out, in_=in_).then_inc(sem, 16))
        tok = (sname, self.cnt[sname])
        self._commit(tok, reads, writes)
        return tok

    def barrier(self):
        toks = [(s, c) for s, c in self.cnt.items() if c > 0]
        for e in ENGS:
            for t in toks:
                self._wait(e, t)

    def run(self):
        with self.nc.Block() as block:
            @block.tensor
            def _(e):
                for f in self.q["pe"]:
                    f(e)

            @block.scalar
            def _(e):
                for f in self.q["act"]:
                    f(e)

            @block.vector
            def _(e):
                for f in self.q["dve"]:
                    f(e)

            @block.gpsimd
            def _(e):
                for f in self.q["pool"]:
                    f(e)

            @block.sync
            def _(e):
                for f in self.q["sp"]:
                    f(e)


class Arena:
    def __init__(self, t, nwords):
        self.t, self.n, self.off = t, nwords, 0

    def reset(self):
        self.off = 0

    def alloc(self, shape, dt):
        nel = int(np.prod(shape[1:]))
        nw = (nel * (2 if dt == BF16 else 4) + 3) // 4
        nw = (nw + 1) // 2 * 2
        assert self.off + nw <= self.n, ("arena overflow", self.off, nw, self.n)
        v = self.t[:, self.off:self.off + nw]
        self.off += nw
        if dt == BF16:
            v = v.bitcast(BF16)
        v = v[:, 0:nel]
        if len(shape) == 3:
            v = v.rearrange("p (a b) -> p a b", a=shape[1])
        elif len(shape) == 4:
            v = v.rearrange("p (a b c) -> p a b c", a=shape[1], b=shape[2])
        return v


def build_nc(stop=None):
    nc = bass.Bass("TRN2", target_bir_lowering=False)

    def din(name, shape):
        return nc.dram_tensor(name, list(shape), F32, kind="ExternalInput").ap()

    def dout(name, shape):
        return nc.dram_tensor(name, list(shape), F32, kind="ExternalOutput").ap()

    xs_d, xp_d = din("xs", (1024, 4096)), din("xp", (1024, 4096))
    cond2_d = din("cond2", (128, 64))
    w_ada0_d = din("w_ada0", (4096, 12288))
    w_ada1_d = din("w_ada1", (96, 128, 4096))
    bT_d, gT_d = din("bT", (128, 192)), din("gT", (128, 64))
    fgT_d, pscT_d = din("fgT", (128, 32)), din("pscT", (128, 32))
    w_in_attn_d = din("w_in_attn", (104, 128, 4096))
    w_out_attn_d = din("w_out_attn", (32, 128, 4096))
    w_in_pool_d = din("w_in_pool", (64, 128, 4096))
    w_grp_d = din("w_grp", (32, 128, 1024))
    w_out_pool_d = din("w_out_pool", (32, 128, 4096))
    sinkb_d = din("sinkb", (128, 16))
    TB_d = din("TB", (16, 128, 896))
    cak_d, cav_d = din("cak", (512, 512)), din("cav", (512, 512))
    cbk_d, cbv_d = din("cbk", (512, 2048)), din("cbv", (512, 2048))
    ident_d, permT_d = din("ident", (128, 128)), din("permT", (128, 128))
    ropec_d, ropes_d = din("ropec", (128, 1024)), din("ropes", (128, 1024))
    amask_d = din("amask", (128, 256))
    rcS_d, rcP_d = din("rcS", (4, 1024)), din("rcP", (4, 1024))
    ys_d, yp_d = dout("ys", (1024, 4096)), dout("yp", (1024, 4096))
    nak_d, nav_d = dout("nak", (1024, 512)), dout("nav", (1024, 512))
    nbk_d, nbv_d = dout("nbk", (1024, 2048)), dout("nbv", (1024, 2048))
    scrA = nc.dram_tensor("scrA", [32, 128, 1024], F32).ap()
    scrB = nc.dram_tensor("scrB", [32, 128, 1024], F32).ap()
    scrC = nc.dram_tensor("scrC", [32, 128, 1024], F32).ap()
    scrD = nc.dram_tensor("scrD", [32, 128, 1024], F32).ap()

    with ExitStack() as st:
        P = Prog(nc, st)

        def sb(name, shape, dt):
            return st.enter_context(nc.sbuf_tensor(name, list(shape), dt))

        hT = sb("hT", (128, 32, 1024), BF16)
        og = sb("og", (128, 32, 1024), BF16)
        slab = [sb(f"slab{i}", (128, 4096), BF16) for i in range(NSLAB)]
        identf = sb("identf", (128, 128), F32)
        identb = sb("identb", (128, 128), BF16)
        onesb = sb("onesb", (128, 128), BF16)
        modt = sb("modt", (128, 2, 96, 2), F32)
        gm = sb("gm", (128, 2, 2, 32), F32)
        gTt = sb("gTt", (128, 2, 32), F32)
        fg = sb("fg", (128, 32), F32)
        psc = sb("psc", (128, 32), F32)
        bTt = sb("bTt", (128, 2, 96), F32)
        es = sb("es", (128, 16), F32)
        epsT = sb("epsT", (128, 1), F32)
        cb = sb("cb", (128, 32, 2), BF16)
        rstdT = [sb("rstdS", (128, 1024), F32), sb("rstdP", (128, 1024), F32)]
        arena_t = sb("arena", (128, ARENA_W), F32)
        AR = Arena(arena_t, ARENA_W)
        ps = [st.enter_context(nc.psum_tensor(f"ps{i}", [128, 512], F32)) for i in range(8)]
        S_ps = [Slot() for _ in range(8)]
        S_slab = [Slot() for _ in range(NSLAB)]
        s_slab = [P.new_sem(f"s_slab{i}") for i in range(NSLAB)]
        S_hT, S_og, S_c = Slot(), Slot(), Slot()
        S_rstdT = [Slot(), Slot()]
        S_scrA = [Slot() for _ in range(32)]
        S_scrB = [Slot() for _ in range(32)]
        S_scrC = [Slot() for _ in range(32)]
        S_scrD = [Slot() for _ in range(32)]
        CUR = {"ti": 0}
        s_c = P.new_sem("s_c")
        s_a = [P.new_sem(f"s_a{i}") for i in range(8)]
        s_b = [P.new_sem(f"s_b{i}") for i in range(6)]
        s_p = [P.new_sem(f"s_p{i}") for i in range(2)]
        G = {"slab_rr": 0, "gemm_rr": 0}

        og_f = og[:].rearrange("p a b -> p (a b)").bitcast(F32)
        og_b = og[:].rearrange("p a b -> p (a b)")

        def cp(eng, out, in_, reads, writes, scale=None):
            if eng == "act":
                if scale is None:
                    return P.op("act", lambda e: e.activation(out=out, in_=in_, func=AF.Copy), reads, writes)
                return P.op("act", lambda e: e.activation(out=out, in_=in_, func=AF.Copy, scale=scale), reads, writes)
            return P.op("dve", lambda e: e.tensor_copy(out=out, in_=in_), reads, writes)

        P.dma("sp", s_c, identf[:], ident_d[:, :], writes=[S_c])
        P.dma("sp", s_c, gTt[:].rearrange("p a b -> p (a b)"), gT_d[:, :], writes=[S_c])
        P.dma("sp", s_c, fg[:], fgT_d[:, :], writes=[S_c])
        P.dma("sp", s_c, psc[:], pscT_d[:, :], writes=[S_c])
        P.dma("sp", s_c, bTt[:].rearrange("p a b -> p (a b)"), bT_d[:, :], writes=[S_c])
        P.dma("sp", s_c, es[:], sinkb_d[:, :], writes=[S_c])
        P.op("dve", lambda e: e.memset(onesb[:], 1.0), writes=[S_c])
        P.op("dve", lambda e: e.memset(epsT[:], EPS), writes=[S_c])
        P.op("dve", lambda e: e.tensor_copy(out=identb[:], in_=identf[:]), reads=[S_c], writes=[S_c])
        P.op("act", lambda e: e.activation(out=es[:], in_=es[:], func=AF.Exp), reads=[S_c], writes=[S_c])
        P.barrier()

        def phase_ada_input():
            AR.reset()
            cf = AR.alloc((128, 64), F32)
            S_cf, S_cb, S_mrow = Slot(), Slot(), Slot()
            P.dma("sp", s_a[3], cf, cond2_d[:, :], writes=[S_cf])
            P.op("act", lambda e: e.activation(out=cb.rearrange("p a b -> p (a b)"), in_=cf, func=AF.Silu),
                 reads=[S_cf], writes=[S_cb, S_c])
            hT_f = hT[:].rearrange("p a b -> p (a b)").bitcast(F32)
            mrow = hT_f[0:2, 0:12288]
            xblk = [og_f[:, 0:4096], og_f[:, 4096:8192]]
            xst = og_f[:, 8192:12288].rearrange("p (a b) -> p a b", a=32)
            sq = og_b[:, 24576:28672].rearrange("p (a b) -> p a b", a=32)
            S_xb = [Slot(), Slot()]
            S_xg = [Slot() for _ in range(8)]
            S_sq = Slot()

            def input_block(ib):
                ti, b = divmod(ib, 8)
                xsrc = xs_d if ti == 0 else xp_d
                scr, S_scr = (scrA, S_scrA) if ti == 0 else (scrC, S_scrC)
                scr_v = scr.rearrange("j p t -> p j t")
                k = ib % 2
                P.dma("sp", s_a[k], xblk[k], xsrc[128 * b:128 * b + 128, :], writes=[S_xb[k]])
                for g in range(8):
                    bank = 4 + g % 2

                    def tr(e, k=k, g=g, bank=bank):
                        for j in range(4):
                            c = 4 * g + j
                            ins = e.transpose(out=ps[bank][:, 128 * j:128 * j + 128],
                                              in_=xblk[k][:, 128 * c:128 * c + 128], identity=identf[:])
                        return ins
                    P.op("pe", tr, reads=[S_xb[k]], writes=[S_ps[bank]])
                    cp("act" if g % 2 == 0 else "dve", xst[:, 4 * g:4 * g + 4, :],
                       ps[bank][:].rearrange("p (a b) -> p a b", a=4), [S_ps[bank]], [S_xg[g]])
                P.op("act", lambda e: e.activation(out=sq, in_=xst, func=AF.Square), reads=S_xg, writes=[S_sq])
                col = 128 * (b % 4)

                def ssq(e, col=col):
                    for c in range(32):
                        ins = e.matmul(ps[6][:, col:col + 128], lhsT=onesb[:], rhs=sq[:, c, :],
                                       start=(c == 0), stop=(c == 31))
                    return ins
                P.op("pe", ssq, reads=[S_sq], writes=[S_ps[6]])
                rstd_from(ps[6][:, col:col + 128], rstdT[ti][:, 128 * b:128 * b + 128], [S_ps[6]], ti)
                P.dma("sp", s_a[2], scr_v[:, :, 128 * b:128 * b + 128], xst, reads=S_xg, writes=S_scr)

            def mod_finish(l):
                def tr(e):
                    for j in range(96):
                        ins = e.transpose(out=ps[7][:, 2 * j:2 * j + 2], in_=mrow[:, 128 * j:128 * j + 128],
                                          identity=identf[0:2, 0:2])
                    return ins
                P.op("pe", tr, reads=[S_mrow], writes=[S_ps[7]])
                pv = ps[7][:, 0:192].rearrange("p (j v) -> p j v", v=2)
                for v in range(2):
                    P.op("dve", lambda e, v=v: e.tensor_tensor(
                        out=modt[:, l, :, v], in0=pv[:, :, v], in1=bTt[:, l, :], op=ALU.add),
                        reads=[S_ps[7]], writes=[S_c])
                gm_finish(l)

            q = 0
            nib = 0
            for l in range(1):
                for n in range(24):
                    pb = n % 2
                    for kq in range(4):
                        s = G["slab_rr"] % NSLAB
                        G["slab_rr"] += 1
                        sv = slab[s][:].rearrange("p (c n) -> p c n", n=512)
                        wv = w_ada0_d[1024 * kq:1024 * kq + 1024, 512 * n:512 * n + 512].rearrange(
                            "(c p) n -> p c n", p=128)
                        P.dma("pool", s_slab[s], sv, wv, writes=[S_slab[s]])

                        def mm(e, sv=sv, kq=kq, pb=pb):
                            for c in range(8):
                                ins = e.matmul(ps[pb][0:2, :], lhsT=cb[:, 8 * kq + c, :], rhs=sv[:, c, :],
                                               start=(kq == 0 and c == 0), stop=(kq == 3 and c == 7))
                            return ins
                        P.op("pe", mm, reads=[S_slab[s], S_cb], writes=[S_ps[pb]])
                    cp("act", mrow[:, 512 * n:512 * n + 512], ps[pb][0:2, :], [S_ps[pb]], [S_mrow])
                    q += 1
                    while nib < 16 and nib < (q * 2 + 2) // 3:
                        input_block(nib)
                        nib += 1
                mod_finish(l)
            P.barrier()

        def rstd_from(ps_ap, out_ap, rd, ti):
            P.op("act", lambda e: e.activation(out=out_ap, in_=ps_ap, func=AF.Sqrt, bias=epsT[:, 0:1],
                                               scale=1.0 / 4096.0), reads=rd, writes=[S_rstdT[ti]])
            P.op("dve", lambda e: e.reciprocal(out=out_ap, in_=out_ap), reads=[S_rstdT[ti]], writes=[S_rstdT[ti]])

        def gm_finish(l):
            for v in range(2):
                P.op("dve", lambda e, v=v: e.tensor_scalar(
                    out=gm[:, l, v, :], in0=modt[:, l, 32:64, v], scalar1=1.0, scalar2=None, op0=ALU.add),
                    reads=[S_c], writes=[S_c])
                P.op("dve", lambda e, v=v: e.tensor_tensor(
                    out=gm[:, l, v, :], in0=gm[:, l, v, :], in1=gTt[:, l, :], op=ALU.mult),
                    reads=[S_c], writes=[S_c])

        def gemm_stream(jobs):
            n = len(jobs)
            st_ = {"nload": 0}

            def load_upto(k):
                while st_["nload"] < n and st_["nload"] <= k:
                    jb = jobs[st_["nload"]]
                    s = G["slab_rr"] % NSLAB
                    G["slab_rr"] += 1
                    jb["s"] = s
                    nk = jb["nk"]
                    sv = slab[s][:, 0:nk * 128].rearrange("p (a b) -> p a b", b=1024)
                    wv = jb["w"].rearrange("p (a b) -> p a b", b=1024)
                    P.dma("pool", s_slab[s], sv, wv, writes=[S_slab[s]])
                    st_["nload"] += 1

            def emit_mm(i):
                job = jobs[i]
                s, nk, act = job["s"], job["nk"], job["act"]
                pair = 2 * (G["gemm_rr"] % 2)
                G["gemm_rr"] += 1
                job["pair"] = pair
                sv = slab[s][:, 0:nk * 128].rearrange("p (c n) -> p c n", n=128)

                def mm(e, sv=sv, nk=nk, act=act, pair=pair):
                    for c in range(nk):
                        for hf in range(2):
                            ins = e.matmul(ps[pair + hf][:, :], lhsT=sv[:, c, :],
                                           rhs=act[:, c, 512 * hf:512 * hf + 512],
                                           start=(c == 0), stop=(c == nk - 1))
                    return ins
                if job.get("gemv"):
                    def mm(e, sv=sv, pair=pair):
                        for c in range(32):
                            ins = e.matmul(ps[pair][:, 0:2], lhsT=sv[:, c, :], rhs=cb[:, c, :],
                                           start=(c == 0), stop=(c == 31))
                        return ins
                P.op("pe", mm, reads=[S_slab[s]] + job["rd"], writes=[S_ps[pair], S_ps[pair + 1]])

            load_upto(1)
            emit_mm(0)
            for i in range(n):
                load_upto(i + 2)
                if i + 1 < n:
                    emit_mm(i + 1)
                jobs[i]["consume"](jobs[i]["pair"])

        def gen_norm(scr, S_scr, l, v, ti, alloc, sems, nbuf):
            xin = [alloc((128, 1024), F32) for _ in range(nbuf)]
            tmp = [alloc((128, 1024), F32) for _ in range(nbuf)]
            S_xin, S_tmp = [Slot() for _ in range(nbuf)], [Slot() for _ in range(nbuf)]
            for j in range(32):
                k = j % nbuf
                P.dma("sp", sems[k], xin[k], scr[j], reads=[S_scr[j]], writes=[S_xin[k]])
                P.op("dve", lambda e, k=k: e.tensor_tensor(out=tmp[k], in0=xin[k], in1=rstdT[ti][:], op=ALU.mult),
                     reads=[S_xin[k], S_rstdT[ti]], writes=[S_tmp[k]])
                P.op("act", lambda e, k=k, j=j: e.activation(out=hT[:, j, :], in_=tmp[k], func=AF.Identity,
                                                              scale=gm[:, l, v, j:j + 1],
                                                              bias=modt[:, l, j, v:v + 1]),
                     reads=[S_tmp[k]], writes=[S_hT])
                yield

        def phase_norm(scr, S_scr, l, v, ti):
            AR.reset()
            for _ in gen_norm(scr, S_scr, l, v, ti, AR.alloc, s_a[0:4], 4):
                pass
            P.barrier()

        def phase_outproj(W, scr_in, S_in, scr_out, S_out, l, v, ti, side=None):
            AR.reset()
            xin = [AR.alloc((128, 1024), F32) for _ in range(2)]
            xo = [AR.alloc((128, 1024), F32) for _ in range(2)]
            sq = [AR.alloc((128, 1024), BF16) for _ in range(2)]
            S_xin, S_xo, S_sq = [Slot(), Slot()], [Slot(), Slot()], [Slot(), Slot()]
            side = side() if side is not None else iter(())
            pending = []
            jobs = []
            for j in range(32):
                def consume(pair, j=j):
                    k = j % 2
                    P.dma("sp", s_a[k], xin[k], scr_in[j], reads=[S_in[j]], writes=[S_xin[k]])
                    for hf in range(2):
                        P.op("dve", lambda e, hf=hf, k=k: e.scalar_tensor_tensor(
                            out=xo[k][:, 512 * hf:512 * hf + 512], in0=ps[pair + hf][:, :],
                            scalar=modt[:, l, 64 + j, v:v + 1], in1=xin[k][:, 512 * hf:512 * hf + 512],
                            op0=ALU.mult, op1=ALU.add),
                            reads=[S_ps[pair + hf], S_xin[k]], writes=[S_xo[k]])
                    P.dma("sp", s_a[2 + k], scr_out[j], xo[k], reads=[S_xo[k]], writes=[S_out[j]])
                    P.op("act", lambda e, k=k: e.activation(out=sq[k], in_=xo[k], func=AF.Square),
                         reads=[S_xo[k]], writes=[S_sq[k]])

                    def ssq(e, k=k):
                        for hf in range(2):
                            ins = e.matmul(ps[6 + hf][:, :], lhsT=onesb[:], rhs=sq[k][:, 512 * hf:512 * hf + 512],
                                           start=(j == 0), stop=(j == 31))
                        return ins
                    while pending:
                        pending.pop(0)()
                    pending.append(lambda k=k, ssq=ssq: P.op("pe", ssq, reads=[S_sq[k]], writes=[S_ps[6], S_ps[7]]))
                    next(side, None)
                jobs.append(dict(w=W[j], nk=32, act=og[:], rd=[S_og], consume=consume))
            gemm_stream(jobs)
            while pending:
                pending.pop(0)()
            for _ in side:
                pass
            for hf in range(2):
                rstd_from(ps[6 + hf][:, :], rstdT[ti][:, 512 * hf:512 * hf + 512], [S_ps[6 + hf]], ti)
            P.barrier()

        hT_f_all = hT[:].rearrange("p a b -> p (a b)").bitcast(F32)
        HAR = Arena(hT_f_all, 16384)

        def gen_final(scr, S_scr, yout, ti, sems):
            HAR.reset()
            xin = [HAR.alloc((128, 1024), F32) for _ in range(4)]
            tmp = [HAR.alloc((128, 1024), F32) for _ in range(4)]
            yst = [HAR.alloc((128, 8, 512), F32) for _ in range(2)]
            S_xin, S_tmp, S_y = [Slot() for _ in range(4)], [Slot() for _ in range(4)], [Slot(), Slot()]
            yv = yout.rearrange("(b p) d -> p b d", p=128)
            def stage_a(j):
                k = j % 4
                P.dma("sp", sems[k], xin[k], scr[j], reads=[S_scr[j]], writes=[S_xin[k]])
                P.op("dve", lambda e: e.tensor_tensor(out=xin[k], in0=xin[k], in1=rstdT[ti][:], op=ALU.mult),
                     reads=[S_xin[k], S_rstdT[ti]], writes=[S_xin[k]])
                P.op("act", lambda e: e.activation(out=tmp[k], in_=xin[k], func=AF.Identity, scale=fg[:, j:j + 1]),
                     reads=[S_xin[k]], writes=[S_tmp[k]])

            def stage_b(j):
                k = j % 4
                q4, jj4 = divmod(j, 4)
                ky = q4 % 2
                for g in range(2):
                    bank = 4 + g

                    def tr(e, g=g, bank=bank):
                        for jj in range(4):
                            b = 4 * g + jj
                            ins = e.transpose(out=ps[bank][:, 128 * jj:128 * jj + 128],
                                              in_=tmp[k][:, 128 * b:128 * b + 128], identity=identf[:])
                        return ins
                    P.op("pe", tr, reads=[S_tmp[k]], writes=[S_ps[bank]])
                    cp("act" if g == 0 else "dve", yst[ky][:, 4 * g:4 * g + 4, 128 * jj4:128 * jj4 + 128],
                       ps[bank][:].rearrange("p (a b) -> p a b", a=4), [S_ps[bank]], [S_y[ky]])
                if jj4 == 3:
                    P.dma("sp", sems[4 + ky], yv[:, :, 512 * q4:512 * q4 + 512], yst[ky], reads=[S_y[ky]])

            for i in range(34):
                if i < 32:
                    stage_a(i)
                if i >= 2:
                    stage_b(i - 2)
                yield

        def phase_final(scr, S_scr, yout, ti):
            for _ in gen_final(scr, S_scr, yout, ti, s_b):
                pass
            P.barrier()

        def phase_pool(is_sample):
            AR.reset()
            nseq, L = (1, 1024) if is_sample else (4, 256)
            LP = L + 16
            pl = AR.alloc((128, 8, 1024), BF16)
            up = AR.alloc((128, nseq, LP), F32)
            ba = AR.alloc((128, nseq, LP), F32)
            bb = AR.alloc((128, nseq, LP), F32)
            rc = AR.alloc((128, 1024), F32)
            sgt = [AR.alloc((128, 1024), BF16) for _ in range(2)]
            S_pl, S_up, S_ba, S_bb, S_rc = Slot(), Slot(), Slot(), Slot(), Slot()
            S_sgt = [Slot(), Slot()]
            rc_d = rcS_d if is_sample else rcP_d
            for t in (up, ba, bb):
                P.op("dve", lambda e, t=t: e.memset(t, 0.0), writes=[S_up, S_ba, S_bb])
            rc3 = rc.rearrange("p (s l) -> p s l", s=nseq)
            TT = lambda o, a, b, op: (lambda e: e.tensor_tensor(out=o, in0=a, in1=b, op=op))
            jobs = []
            for gi in range(4):
                for c in range(8):
                    j = 8 * gi + c

                    def consume_u(pair, gi=gi, c=c):
                        if c == 0:
                            P.dma("sp", s_a[4], rc, rc_d[gi, :].partition_broadcast(128), writes=[S_rc])
                        for hf in range(2):
                            if is_sample:
                                o = up[:, 0, 8 + 512 * hf:8 + 512 * hf + 512]
                                i_ = ps[pair + hf][:, :]
                            else:
                                o = up[:, 2 * hf:2 * hf + 2, 8:8 + L]
                                i_ = ps[pair + hf][:].rearrange("p (s l) -> p s l", s=2)
                            cp("act", o, i_, [S_ps[pair + hf]], [S_up])
                        P.op("dve", TT(ba[:, :, 1:LP], up[:, :, 1:LP], up[:, :, 0:LP - 1], ALU.add),
                             reads=[S_up], writes=[S_ba])
                        fin, S_fin, oth, S_oth = ba, S_ba, bb, S_bb
                        if gi >= 1:
                            P.op("dve", TT(bb[:, :, 2:LP - 1], ba[:, :, 1:LP - 2], ba[:, :, 3:LP], ALU.add),
                                 reads=[S_ba], writes=[S_bb])
                            fin, S_fin, oth, S_oth = bb, S_bb, ba, S_ba
                        if gi >= 2:
                            P.op("dve", TT(ba[:, :, 4:LP - 3], bb[:, :, 2:LP - 5], bb[:, :, 6:LP - 1], ALU.add),
                                 reads=[S_bb], writes=[S_ba])
                            fin, S_fin, oth, S_oth = ba, S_ba, bb, S_bb
                        if gi >= 3:
                            P.op("dve", TT(bb[:, :, 8:LP - 8], ba[:, :, 4:LP - 12], ba[:, :, 12:LP - 4], ALU.add),
                                 reads=[S_ba], writes=[S_bb])
                            fin, S_fin, oth, S_oth = bb, S_bb, ba, S_ba
                        P.op("dve", TT(oth[:, :, 8:8 + L], fin[:, :, 8:8 + L], rc3, ALU.mult),
                             reads=[S_fin, S_rc], writes=[S_oth])
                        P.op("dve", TT(pl[:, c, :].rearrange("p (s l) -> p s l", s=nseq), oth[:, :, 8:8 + L],
                                       up[:, :, 8:8 + L], ALU.subtract),
                             reads=[S_oth, S_up], writes=[S_pl])
                    jobs.append(dict(w=w_in_pool_d[j], nk=32, act=hT[:], rd=[S_hT], consume=consume_u))
                for c in range(8):
                    j = 8 * gi + c

                    def consume_g(pair, j=j):
                        k = j % 2
                        for hf in range(2):
                            P.op("act", lambda e, hf=hf, k=k: e.activation(
                                out=sgt[k][:, 512 * hf:512 * hf + 512], in_=ps[pair + hf][:, :], func=AF.Silu),
                                reads=[S_ps[pair + hf]], writes=[S_sgt[k]])

                    def consume_y(pair, j=j):
                        k = j % 2
                        for hf in range(2):
                            P.op("dve", lambda e, hf=hf, k=k: e.scalar_tensor_tensor(
                                out=og[:, j, 512 * hf:512 * hf + 512], in0=ps[pair + hf][:, :],
                                scalar=psc[:, j:j + 1], in1=sgt[k][:, 512 * hf:512 * hf + 512],
                                op0=ALU.mult, op1=ALU.mult),
                                reads=[S_ps[pair + hf], S_sgt[k]], writes=[S_og])
                    jobs.append(dict(w=w_in_pool_d[32 + j], nk=32, act=hT[:], rd=[S_hT], consume=consume_g))
                    jobs.append(dict(w=w_grp_d[j], nk=8, act=pl, rd=[S_pl], consume=consume_y))
            gemm_stream(jobs)
            P.barrier()

        def phase_attn(is_sample):
            AR.reset()
            qT = AR.alloc((128, 1024), BF16)
            kT = AR.alloc((128, 1024), BF16)
            vTf = AR.alloc((128, 1024), F32)
            vE = AR.alloc((128, 8, 128), BF16)
            sg = AR.alloc((128, 1024), BF16)
            PT = [AR.alloc((128, 512), BF16) for _ in range(4)]
            n1 = AR.alloc((128, 512), F32)
            n2 = AR.alloc((128, 512), F32)
            S_qT, S_kT, S_vTf, S_vE, S_sg, S_n, S_n2 = Slot(), Slot(), Slot(), Slot(), Slot(), Slot(), Slot()
            S_PT = [Slot() for _ in range(4)]
            if is_sample:
                vO = AR.alloc((128, 7, 128), BF16)
                cKf = AR.alloc((128, 4, 128), F32)
                cKT = AR.alloc((128, 512), BF16)
                cV = AR.alloc((128, 4, 128), BF16)
                bt = AR.alloc((128, 14, 64), BF16)
                amk = AR.alloc((128, 256), BF16)
                S_vO, S_cKf, S_cKT, S_cV, S_bt = Slot(), Slot(), Slot(), Slot(), Slot()
                cosT = og_f[:, 12288:13312]
                sinT = og_f[:, 13312:14336]
                qf = og_f[:, 14336:14848]
                rt = og_f[:, 14848:15360]
                permf = og_f[:, 15360:15488]
                amf = og_f[:, 15488:15744]
                S_qf, S_rt = Slot(), Slot()
                P.dma("sp", s_c, cosT, ropec_d[:, :], writes=[S_c])
                P.dma("sp", s_c, sinT, ropes_d[:, :], writes=[S_c])
                P.dma("sp", s_c, permf, permT_d[:, :], writes=[S_c])
                P.dma("sp", s_c, amf, amask_d[:, :], writes=[S_c])
                P.op("dve", lambda e: e.tensor_copy(out=amk, in_=amf), reads=[S_c], writes=[S_c])
                P.barrier()
            else:
                kf = AR.alloc((128, 1024), F32)
                kst = AR.alloc((128, 8, 128), F32)
                vst = AR.alloc((128, 8, 128), F32)
                S_kf, S_kst, S_vst = Slot(), Slot(), Slot()

            ge = AR.alloc((128, 512), F32)
            S_ge = Slot()

            def silu_evac(pair):
                for hf in range(2):
                    P.op("act", lambda e, hf=hf: e.activation(out=ge, in_=ps[pair + hf][:, :], func=AF.Exp,
                                                               scale=-1.0),
                         reads=[S_ps[pair + hf]], writes=[S_ge])
                    P.op("dve", lambda e: e.tensor_scalar(out=ge, in0=ge, scalar1=1.0, scalar2=None, op0=ALU.add),
                         reads=[S_ge], writes=[S_ge])
                    P.op("dve", lambda e: e.reciprocal(out=ge, in_=ge), reads=[S_ge], writes=[S_ge])
                    P.op("dve", lambda e, hf=hf: e.tensor_tensor(out=sg[:, 512 * hf:512 * hf + 512],
                                                                  in0=ps[pair + hf][:, :], in1=ge, op=ALU.mult),
                         reads=[S_ps[pair + hf], S_ge], writes=[S_sg])

            def evac2(pair, dst, S_dst, scale=None, eng="act", excl=False):
                for hf in range(2):
                    cp(eng, dst[:, 512 * hf:512 * hf + 512], ps[pair + hf][:, :], [S_ps[pair + hf]],
                       [S_dst] + ([S_ps[pair + hf]] if excl else []), scale)

            def rope_evac(pair, dst, S_dst, scale):
                for hf in range(2):
                    cp("act", qf, ps[pair + hf][:, :], [S_ps[pair + hf]], [S_qf], scale)
                    P.op("pe", lambda e: e.matmul(ps[4][:, :], lhsT=permf, rhs=qf, start=True, stop=True),
                         reads=[S_qf], writes=[S_ps[4]])
                    P.op("dve", lambda e, hf=hf: e.tensor_tensor(out=rt, in0=qf, in1=cosT[:, 512 * hf:512 * hf + 512],
                                                                  op=ALU.mult), reads=[S_qf], writes=[S_rt])
                    P.op("dve", lambda e, hf=hf: e.tensor_tensor(out=qf, in0=ps[4][:, :],
                                                                  in1=sinT[:, 512 * hf:512 * hf + 512], op=ALU.mult),
                         reads=[S_ps[4]], writes=[S_qf])
                    P.op("dve", lambda e, hf=hf: e.tensor_tensor(out=dst[:, 512 * hf:512 * hf + 512], in0=rt, in1=qf,
                                                                  op=ALU.add), reads=[S_rt, S_qf], writes=[S_dst])

            def transposes(srcs, bank, rd):
                def tr(e):
                    for i, s_ in enumerate(srcs):
                        ins = e.transpose(out=ps[bank][:, 128 * i:128 * i + 128], in_=s_, identity=identf[:])
                    return ins
                P.op("pe", tr, reads=rd, writes=[S_ps[bank]])

            def v_transposes(out_d=None, col0=0):
                for g in range(2):
                    bank = 5 - g
                    transposes([vTf[:, 128 * (4 * g + i):128 * (4 * g + i) + 128] for i in range(4)], bank, [S_vTf])
                    pv = ps[bank][:].rearrange("p (a b) -> p a b", a=4)
                    if out_d is None:
                        cp("dve", vE[:, 4 * g:4 * g + 4, :], pv, [S_ps[bank]], [S_vE])
                    else:
                        cp("dve", vE[:, 4 * g:4 * g + 4, :], pv, [S_ps[bank]], [S_vE, S_ps[bank]])
                        cp("act", vst[:, 4 * g:4 * g + 4, :], pv, [S_ps[bank]], [S_vst, S_ps[bank]])
                if out_d is not None:
                    P.dma("sp", s_a[5], out_d.rearrange("(b p) d -> p b d", p=128)[:, :, col0:col0 + 128], vst,
                          reads=[S_vst])

            def k_out(out_d, col0):
                for g in range(2):
                    bank = 5 - g
                    transposes([kf[:, 128 * (4 * g + i):128 * (4 * g + i) + 128] for i in range(4)], bank, [S_kf])
                    cp("act", kst[:, 4 * g:4 * g + 4, :], ps[bank][:].rearrange("p (a b) -> p a b", a=4),
                       [S_ps[bank]], [S_kst])
                P.dma("sp", s_a[6], out_d.rearrange("(b p) d -> p b d", p=128)[:, :, col0:col0 + 128], kst,
                      reads=[S_kst])

            def vO_transposes():
                for g in range(2):
                    bank = 5 - g
                    n = 4 if g == 0 else 3
                    transposes([vTf[:, 64 + 128 * (4 * g + i):64 + 128 * (4 * g + i) + 128] for i in range(n)],
                               bank, [S_vTf])
                    cp("dve", vO[:, 4 * g:4 * g + n, :],
                       ps[bank][:, 0:128 * n].rearrange("p (a b) -> p a b", a=n), [S_ps[bank]], [S_vO])

            def load_cache(kd, vd, col0):
                P.dma("sp", s_a[0], cKf, kd.rearrange("(b p) d -> p b d", p=128)[:, :, col0:col0 + 128],
                      writes=[S_cKf])
                P.dma("pool", s_p[0], cV, vd.rearrange("(b p) d -> p b d", p=128)[:, :, col0:col0 + 128],
                      writes=[S_cV])
                transposes([cKf[:, b, :] for b in range(4)], 5, [S_cKf])
                cp("act", cKT, ps[5][:, :], [S_ps[5]], [S_cKT])

            def normalize(hf, chunk, sink_h):
                if sink_h is not None:
                    P.op("dve", lambda e: e.tensor_scalar(out=n1, in0=ps[7][:, :], scalar1=es[:, sink_h:sink_h + 1],
                                                          scalar2=None, op0=ALU.add), reads=[S_ps[7]], writes=[S_n])
                    P.op("act", lambda e: e.activation(out=n2, in_=ps[6][:, :], func=AF.Copy), reads=[S_ps[6]],
                         writes=[S_n2])
                    P.op("dve", lambda e: e.reciprocal(out=n1, in_=n1), reads=[S_n], writes=[S_n])
                else:
                    P.op("dve", lambda e: e.reciprocal(out=n1, in_=ps[7][:, :]), reads=[S_ps[7]], writes=[S_n])
                    P.op("act", lambda e: e.activation(out=n2, in_=ps[6][:, :], func=AF.Copy), reads=[S_ps[6]],
                         writes=[S_n2])
                P.op("dve", lambda e: e.tensor_tensor(out=n2, in0=n2, in1=n1, op=ALU.mult),
                     reads=[S_n], writes=[S_n2])
                P.op("dve", lambda e: e.tensor_tensor(out=og[:, chunk, 512 * hf:512 * hf + 512], in0=n2,
                                                      in1=sg[:, 512 * hf:512 * hf + 512], op=ALU.mult),
                     reads=[S_n2, S_sg], writes=[S_og])

            def pv_part(blocks_v, pt, S_pt, ocol, w, extra_rd, first=True, last=True):
                nb = len(blocks_v)

                def f(e):
                    for i, (vap, c0) in enumerate(blocks_v):
                        e.matmul(ps[6][:, ocol:ocol + w], lhsT=vap, rhs=pt[:, c0:c0 + w],
                                 start=(first and i == 0), stop=(last and i == nb - 1))
                    for i, (vap, c0) in enumerate(blocks_v):
                        ins = e.matmul(ps[7][:, ocol:ocol + w], lhsT=onesb[:], rhs=pt[:, c0:c0 + w],
                                       start=(first and i == 0), stop=(last and i == nb - 1))
                    return ins
                P.op("pe", f, reads=[S_pt] + extra_rd, writes=[S_ps[6], S_ps[7]])

            def run_stages(stages):
                n = len(stages)
                stages[0]["sf"]()
                for i in range(n):
                    stages[i]["ex"]()
                    if i + 1 < n:
                        stages[i + 1]["sf"]()
                    stages[i]["pv"]()
                    if stages[i].get("post"):
                        stages[i]["post"]()

            def attn_A_lat(h):
                stages = []
                for n in range(8):
                    blocks = []
                    if n > 0:
                        blocks.append((kT[:, 128 * (n - 1):128 * n], vE[:, n - 1, :], amk[:, 0:128]))
                    blocks.append((kT[:, 128 * n:128 * n + 128], vE[:, n, :], None))
                    if n < 7:
                        blocks.append((kT[:, 128 * (n + 1):128 * (n + 2)], vE[:, n + 1, :], amk[:, 128:256]))
                    for cbk_ in range(4):
                        blocks.append((cKT[:, 128 * cbk_:128 * cbk_ + 128], cV[:, cbk_, :], None))
                    for sidx in range(2):
                        bl = blocks[4 * sidx:4 * sidx + 4]
                        bank = 4 + sidx
                        pi = 2 * (n % 2) + sidx
                        wS = 128 * len(bl)

                        def sfn(bl=bl, n=n, bank=bank):
                            def sf(e):
                                for idx, (kap, vap, m) in enumerate(bl):
                                    o = ps[bank][:, 128 * idx:128 * idx + 128]
                                    ins = e.matmul(o, lhsT=kap, rhs=qT[:, 128 * n:128 * n + 128], start=True,
                                                   stop=(m is None))
                                    if m is not None:
                                        ins = e.matmul(o, lhsT=identb[:], rhs=m, start=False, stop=True)
                                return ins
                            P.op("pe", sf, reads=[S_kT, S_qT, S_cKT], writes=[S_ps[bank]])

                        def exn(bank=bank, pi=pi, wS=wS):
                            P.op("act", lambda e: e.activation(out=PT[pi][:, 0:wS], in_=ps[bank][:, 0:wS], func=AF.Exp),
                                 reads=[S_ps[bank]], writes=[S_PT[pi]])

                        def pvn(bl=bl, pi=pi, n=n, sidx=sidx):
                            pv_part([(vap, 128 * idx) for idx, (kap, vap, m) in enumerate(bl)], PT[pi], S_PT[pi],
                                    128 * (n % 4), 128, [S_vE, S_cV], first=(sidx == 0), last=(sidx == 1))
                        post = None
                        if sidx == 1 and n % 4 == 3:
                            post = (lambda n=n: normalize(n // 4, h, h))
                        stages.append(dict(sf=sfn, ex=exn, pv=pvn, post=post))
                run_stages(stages)

            def attn_B_lat(h):
                stages = []
                for r in range(16):
                    rs = min(max(r - 4, 0), 8)
                    sbk = 4 + r % 2
                    pk = r % 2

                    def sfn(r=r, rs=rs, sbk=sbk):
                        def sf(e):
                            for b in range(4):
                                o = ps[sbk][:, 64 * b:64 * b + 64]
                                t0 = 64 * (rs + 2 * b)
                                e.matmul(o, lhsT=kT[:, t0:t0 + 128], rhs=qT[:, 64 * r:64 * r + 64], start=True,
                                         stop=False)
                                e.matmul(o, lhsT=identb[:], rhs=bt[:, rs + 2 * b - r + 7, :], start=False, stop=True)
                            for c_ in range(4):
                                ins = e.matmul(ps[sbk][:, 256 + 64 * c_:256 + 64 * c_ + 64],
                                               lhsT=cKT[:, 128 * c_:128 * c_ + 128], rhs=qT[:, 64 * r:64 * r + 64],
                                               start=True, stop=True)
                            return ins
                        P.op("pe", sf, reads=[S_kT, S_qT, S_cKT, S_bt], writes=[S_ps[sbk]])

                    def exn(pk=pk, sbk=sbk):
                        P.op("act", lambda e: e.activation(out=PT[pk][:, 0:512], in_=ps[sbk][:, :], func=AF.Exp),
                             reads=[S_ps[sbk]], writes=[S_PT[pk]])

                    def pvn(r=r, rs=rs, pk=pk):
                        bl = []
                        for b in range(4):
                            vap = vE[:, (rs + 2 * b) // 2, :] if rs % 2 == 0 else vO[:, (rs - 1) // 2 + b, :]
                            bl.append((vap, 64 * b))
                        for c_ in range(4):
                            bl.append((cV[:, c_, :], 256 + 64 * c_))
                        pv_part(bl, PT[pk], S_PT[pk], 64 * (r % 8), 64, [S_vE, S_vO, S_cV])
                    post = (lambda r=r: normalize(r // 8, 16 + h, None)) if r % 8 == 7 else None
                    stages.append(dict(sf=sfn, ex=exn, pv=pvn, post=post))
                run_stages(stages)

            def attn_ctx(chunk, sink_h):
                stages = []
                for s_ in range(4):
                    sbk = 4 + s_ % 2
                    pk = s_ % 2

                    def sfn(s_=s_, sbk=sbk):
                        def sf(e):
                            for b in range(2):
                                ins = e.matmul(ps[sbk][:, 256 * b:256 * b + 256],
                                               lhsT=kT[:, 256 * s_ + 128 * b:256 * s_ + 128 * b + 128],
                                               rhs=qT[:, 256 * s_:256 * s_ + 256], start=True, stop=True)
                            return ins
                        P.op("pe", sf, reads=[S_kT, S_qT], writes=[S_ps[sbk]])

                    def exn(pk=pk, sbk=sbk):
                        P.op("act", lambda e: e.activation(out=PT[pk][:, 0:512], in_=ps[sbk][:, :], func=AF.Exp),
                             reads=[S_ps[sbk]], writes=[S_PT[pk]])

                    def pvn(s_=s_, pk=pk):
                        pv_part([(vE[:, 2 * s_ + b, :], 256 * b) for b in range(2)], PT[pk], S_PT[pk],
                                256 * (s_ % 2), 256, [S_vE])
                    post = (lambda s_=s_: normalize(s_ // 2, chunk, sink_h)) if s_ % 2 == 1 else None
                    stages.append(dict(sf=sfn, ex=exn, pv=pvn, post=post))
                run_stages(stages)

            W = w_in_attn_d
            jobs = []

            def J(col, consume):
                jobs.append(dict(w=W[col // 128], nk=32, act=hT[:], rd=[S_hT], consume=consume))

            for g in range(4):
                if is_sample:
                    def ck(pair, g=g):
                        load_cache(cak_d, cav_d, 128 * g)
                        rope_evac(pair, kT, S_kT, None)

                    def cv_(pair):
                        evac2(pair, vTf, S_vTf)
                        v_transposes()
                else:
                    def ck(pair, g=g):
                        evac2(pair, kT, S_kT, excl=True)
                        evac2(pair, kf, S_kf, eng="dve", excl=True)
                        k_out(nak_d, 128 * g)

                    def cv_(pair, g=g):
                        evac2(pair, vTf, S_vTf)
                        v_transposes(nav_d, 128 * g)
                J(2048 + 128 * g, ck)
                J(2560 + 128 * g, cv_)
                for hq in range(4):
                    h = 4 * g + hq
                    if is_sample:
                        def cq(pair):
                            rope_evac(pair, qT, S_qT, SCALE)
                    else:
                        def cq(pair):
                            evac2(pair, qT, S_qT, SCALE)

                    def cg(pair, h=h):
                        silu_evac(pair)
                        if is_sample:
                            attn_A_lat(h)
                        else:
                            attn_ctx(h, h)
                    J(128 * h, cq)
                    J(9216 + 128 * h, cg)
            first_b = [True]
            for h in range(16):
                def ck(pair, h=h):
                    if is_sample:
                        if first_b[0]:
                            first_b[0] = False
                            P.barrier()
                        load_cache(cbk_d, cbv_d, 128 * h)
                        P.dma("pool", s_p[1], bt, TB_d[h].rearrange("p (a b) -> p a b", a=14), writes=[S_bt])
                        evac2(pair, kT, S_kT)
                    else:
                        evac2(pair, kT, S_kT, excl=True)
                        evac2(pair, kf, S_kf, eng="dve", excl=True)
                        k_out(nbk_d, 128 * h)

                def cv_(pair, h=h):
                    evac2(pair, vTf, S_vTf)
                    if is_sample:
                        v_transposes()
                        vO_transposes()
                    else:
                        v_transposes(nbv_d, 128 * h)

                def cq(pair):
                    evac2(pair, qT, S_qT, SCALE)

                def cg(pair, h=h):
                    silu_evac(pair)
                    if is_sample:
                        attn_B_lat(h)
                    else:
                        attn_ctx(16 + h, None)
                J(5120 + 128 * h, ck)
                J(7168 + 128 * h, cv_)
                J(3072 + 128 * h, cq)
                J(9216 + 2048 + 128 * h, cg)
            if is_sample:
                mixed = []
                for i, jb in enumerate(jobs):
                    mixed.append(jb)
                    if i < 96:
                        def cada(pair, j=i):
                            P.op("dve", lambda e: e.tensor_scalar(out=modt[:, 1, j, :], in0=ps[pair][:, 0:2],
                                                                  scalar1=bTt[:, 1, j:j + 1], scalar2=None,
                                                                  op0=ALU.add),
                                 reads=[S_ps[pair]], writes=[S_c])
                        mixed.append(dict(w=w_ada1_d[i], nk=32, act=None, rd=[S_c],
                                          consume=cada, gemv=True))
                jobs = mixed
            gemm_stream(jobs)
            if is_sample:
                gm_finish(1)
            P.barrier()

        def S_out1_P_norm0():
            phase_outproj(w_out_pool_d, scrB, S_scrB, scrA, S_scrA, 1, 1, 0,
                          side=lambda: gen_norm(scrC, S_scrC, 0, 0, 1, AR.alloc, s_b[0:2], 2))

        def P_out0_S_final():
            phase_outproj(w_out_attn_d, scrC, S_scrC, scrD, S_scrD, 0, 0, 1,
                          side=lambda: gen_final(scrA, S_scrA, ys_d, 0, s_b))

        steps = [
            ("ada", phase_ada_input),
            ("Snorm0", lambda: phase_norm(scrA, S_scrA, 0, 1, 0)),
            ("Sattn", lambda: phase_attn(True)),
            ("Sout0", lambda: phase_outproj(w_out_attn_d, scrA, S_scrA, scrB, S_scrB, 0, 1, 0)),
            ("Snorm1", lambda: phase_norm(scrB, S_scrB, 1, 1, 0)),
            ("Spool", lambda: phase_pool(True)),
            ("Sout1", S_out1_P_norm0),
            ("Pattn", lambda: phase_attn(False)),
            ("Pout0", P_out0_S_final),
            ("Pnorm1", lambda: phase_norm(scrD, S_scrD, 1, 0, 1)),
            ("Ppool", lambda: phase_pool(False)),
            ("Pout1", lambda: phase_outproj(w_out_pool_d, scrD, S_scrD, scrC, S_scrC, 1, 0, 1)),
            ("Pfinal", lambda: phase_final(scrC, S_scrC, yp_d, 1)),
        ]
        for name, fn in steps:
            fn()
            if stop == name:
                break
        P.barrier()
        P.cnt_report = dict(P.cnt)
        print("sem counts", P.cnt)
        P.run()
    return nc


def _consts():
    f32 = np.float32
    ident = np.eye(128, dtype=f32)
    permT = np.zeros((128, 128), f32)
    sign = np.zeros(128, f32)
    for m in range(128):
        a, rem = divmod(m, 64)
        if rem < 32:
            src, sg_ = a * 64 + 32 + rem, -1.0
        else:
            src, sg_ = a * 64 + rem - 32, 1.0
        permT[src, m] = 1.0
        sign[m] = sg_
    t = np.arange(1024)
    inv_freq = (f32(10000.0) ** (-np.arange(32, dtype=f32) / f32(32))).astype(f32)
    ang_r = (t // 64).astype(f32)[:, None] * inv_freq
    ang_c = (t % 64).astype(f32)[:, None] * inv_freq
    ang = np.concatenate([ang_r, ang_r, ang_c, ang_c], axis=-1).astype(f32)
    ropec = np.ascontiguousarray(np.cos(ang).T.astype(f32))
    ropes = np.ascontiguousarray((np.sin(ang).T * sign[:, None]).astype(f32))
    jj = np.arange(128)[:, None]
    qq = np.arange(128)[None, :]
    amask = np.concatenate([np.where(qq <= jj, 0.0, NEG), np.where(jj <= qq, 0.0, NEG)], axis=1).astype(f32)

    def rc(L, reps):
        out = np.zeros((4, L), f32)
        tt = np.arange(L)
        for gi, half in enumerate((1, 2, 4, 8)):
            lo = np.clip(tt - half, 0, L)
            hi = np.clip(tt + half, 0, L)
            out[gi] = (f32(1.0) / (hi - lo).astype(f32)).astype(f32)
        return np.ascontiguousarray(np.tile(out, (1, reps)))
    return dict(ident=ident, permT=permT, ropec=ropec, ropes=ropes, amask=amask, rcS=rc(1024, 1), rcP=rc(256, 4))


def _bias_table(rpb):
    c = np.arange(64)
    kc = np.arange(64)
    col_start = np.clip(c - 8, 0, 48)
    ok = (kc[:, None] >= col_start[None, :]) & (kc[:, None] < col_start[None, :] + 16)
    dc = np.clip(kc[:, None] - c[None, :] + 15, 0, 30)
    TB = np.empty((16, 2, 64, 14, 64), np.float32)
    for a in range(2):
        for dr0 in range(14):
            g = rpb[:, dr0 + a][:, dc]
            TB[:, a, :, dr0, :] = np.where(ok[None], g, np.float32(NEG))
    return np.ascontiguousarray(TB.reshape(16, 128, 14 * 64))


_NC_CACHE = {}
_NCORES = [8]
_PREP_ONLY = [False]


def kernel(x_prompt, x_sample, c, cache_a_k, cache_a_v, cache_b_k, cache_b_v, c_ctx,
           w_ada, b_ada, norm_g, w_in_attn, a_sink, b_rpb, w_out_attn,
           w_in_pool, w_grp_pool, pool_scale, w_out_pool, final_g):
    f = lambda a: np.ascontiguousarray(np.asarray(a, dtype=np.float32))
    x_prompt, x_sample, c, c_ctx = f(x_prompt), f(x_sample), f(c), f(c_ctx)
    cache_a_k, cache_a_v, cache_b_k, cache_b_v = f(cache_a_k), f(cache_a_v), f(cache_b_k), f(cache_b_v)
    w_ada, b_ada, norm_g = f(w_ada), f(b_ada), f(norm_g)
    consts = _consts()
    def tile_w(w2d):
        K, N = w2d.shape
        return np.ascontiguousarray(w2d.reshape(K // 128, 128, N // 128, 128).transpose(2, 1, 0, 3)).reshape(
            N // 128, 128, (K // 128) * 128)
    wg = f(w_grp_pool).reshape(4, 1024, 1024)
    shared = dict(
        w_ada0=np.ascontiguousarray(w_ada[0]), w_ada1=tile_w(w_ada[1]),
        bT=f(b_ada.reshape(2, 96, 128).transpose(2, 0, 1).reshape(128, 192)),
        gT=f(norm_g.reshape(2, 32, 128).transpose(2, 0, 1).reshape(128, 64)),
        fgT=f(np.asarray(final_g, np.float32).reshape(32, 128).T),
        pscT=f(np.asarray(pool_scale, np.float32).reshape(32, 128).T),
        w_in_attn=tile_w(f(w_in_attn).reshape(4096, 13312)),
        w_out_attn=tile_w(f(w_out_attn).reshape(4096, 4096)),
        w_in_pool=tile_w(f(w_in_pool).reshape(4096, 8192)),
        w_grp=np.concatenate([tile_w(wg[g]) for g in range(4)], axis=0),
        w_out_pool=tile_w(f(w_out_pool).reshape(4096, 4096)),
        sinkb=f(np.broadcast_to(np.asarray(a_sink, np.float32).reshape(1, 16), (128, 16))),
        TB=_bias_table(np.asarray(b_rpb, np.float32)[0]),
        **consts,
    )
    in_maps = []
    for i in range(_NCORES[0]):
        cond = np.stack([c_ctx, c[i]], axis=0)
        cond2 = f(cond.reshape(2, 32, 128).transpose(2, 1, 0).reshape(128, 64))
        m = dict(shared)
        m.update(
            xs=f(x_sample[i]), xp=f(x_prompt[4 * i:4 * i + 4].reshape(1024, 4096)), cond2=cond2,
            cak=f(cache_a_k[i, 0].reshape(512, 512)), cav=f(cache_a_v[i, 0].reshape(512, 512)),
            cbk=f(cache_b_k[i, 0].reshape(512, 2048)), cbv=f(cache_b_v[i, 0].reshape(512, 2048)),
        )
        in_maps.append(m)
    if _PREP_ONLY[0]:
        return in_maps
    if "nc" not in _NC_CACHE:
        _NC_CACHE["nc"] = build_nc()
    res = run_bass_kernel_spmd(_NC_CACHE["nc"], in_maps, core_ids=list(range(8)))
    R = res.results
    y_sample = np.stack([R[i]["ys"] for i in range(8)], axis=0)
    y_prompt = np.concatenate([R[i]["yp"].reshape(4, 256, 4096) for i in range(8)], axis=0)
    cat = lambda k, nh: np.concatenate([R[i][k].reshape(4, 1, 256, nh, 128) for i in range(8)], axis=0)
    return (y_prompt, y_sample, cat("nak", 4), cat("nav", 4), cat("nbk", 16), cat("nbv", 16))
```
